# Optimizing a Trainium2 kernel written in Bass

```python
import math
import jax, jax.numpy as jnp
from jax import lax
import numpy as np

D_MODEL = 1024
BATCH = 16
SEQ = 2048
DEPTH = 2

GRID_W = 64
CTX_LEN = 256
N_MIXERS = 4
GROUP_W = D_MODEL // N_MIXERS
D_MIX = N_MIXERS * GROUP_W
LN_EPS = 1e-5
CONF_K = 31
CONF_GROUPS = 4
NA_HEADS = 4
HEAD_DIM = GROUP_W // NA_HEADS
NA_KH = 8
NA_KW = 16
ROPE_BASE = 10000.0
HY_SHORT = 3
HY_BANDS = 16
HY_EMB = 1 + 2 * HY_BANDS
HY_HIDDEN = 64
HY_SIN_FREQ = 1.0
HY_MIN_DECAY = 3.0
HY_MAX_DECAY = 15.0
SSD_HEADS = 4
SSD_HEAD_DIM = GROUP_W // SSD_HEADS
SSD_GROUPS = 2
SSD_STATE = 64
SSD_CONV = 3
SSD_CHUNK = 128
XBC_W = GROUP_W + 2 * SSD_GROUPS * SSD_STATE
IN_COLS = 2 * GROUP_W + 3 * GROUP_W + 3 * GROUP_W + GROUP_W + XBC_W + 2 * SSD_HEADS
PEER_HEADS = 8
PEER_KEYS = 128
PEER_TOPK = 16
PEER_QDIM = 256
N_EXPERTS = PEER_KEYS * PEER_KEYS
PEER_BLOCK = 128

kernel_name = 'hybrid_diffusion_block'


def _standardize(x):
    xf = x.astype(jnp.float32)
    mu = jnp.mean(xf, -1, keepdims=True)
    var = jnp.mean(jnp.square(xf - mu), -1, keepdims=True)
    return (xf - mu) * lax.rsqrt(var + LN_EPS)


def layer_norm(x, g, b):
    return (_standardize(x) * g + b).astype(x.dtype)


def dwconv(x, w):
    K, C = w.shape
    pad = (K - 1) // 2
    return lax.conv_general_dilated(x, w[:, None, :].astype(x.dtype), window_strides=(1,),
                                    padding=[(pad, pad)], dimension_numbers=('NWC', 'WIO', 'NWC'),
                                    feature_group_count=C)


def _split_cols(p):
    sizes = (2 * GROUP_W, 3 * GROUP_W, 3 * GROUP_W, GROUP_W, XBC_W, 2 * SSD_HEADS)
    points = [int(v) for v in np.cumsum(sizes)[:-1]]
    return jnp.split(p, points, axis=-1)


def conformer_conv(p, dw_w, dw_b, n_g, n_b):
    a, gate = jnp.split(p, 2, -1)
    u = a * jax.nn.sigmoid(gate)
    u = dwconv(u, dw_w) + dw_b
    Bsz, L, C = u.shape
    un = _standardize(u.reshape(Bsz, L, CONF_GROUPS, C // CONF_GROUPS)).reshape(Bsz, L, C)
    un = (un * n_g + n_b).astype(u.dtype)
    return jax.nn.silu(un)


def axial_rope(rows, head_dim):
    n_f = head_dim // 4
    inv = ROPE_BASE ** (-jnp.arange(n_f, dtype=jnp.float32) / n_f)
    t = jnp.arange(rows * GRID_W)
    r = (t // GRID_W).astype(jnp.float32)
    col = (t % GRID_W).astype(jnp.float32)
    ang = jnp.concatenate([r[:, None] * inv, col[:, None] * inv], -1)
    return jnp.cos(ang), jnp.sin(ang)


def apply_rope(x, cos, sin):
    x1, x2 = jnp.split(x.astype(jnp.float32), 2, -1)
    c = cos[None, :, None, :]
    s = sin[None, :, None, :]
    return jnp.concatenate([x1 * c - x2 * s, x1 * s + x2 * c], -1).astype(x.dtype)


def neighborhood_attention(q_rot, k_rot, v, q_plain, k_ctx, v_ctx, rpb):
    Bsz, S, H, d = q_rot.shape
    rows = S // GRID_W
    kh = min(NA_KH, rows)
    qg = q_rot.reshape(Bsz, rows, GRID_W, H, d)
    kg = k_rot.reshape(Bsz, rows, GRID_W, H, d)
    vg = v.reshape(Bsz, rows, GRID_W, H, d)
    qpg = q_plain.reshape(Bsz, rows, GRID_W, H, d)
    cq = jnp.arange(GRID_W)
    col_idx = jnp.clip(cq - NA_KW // 2, 0, GRID_W - NA_KW)[:, None] + jnp.arange(NA_KW)[None, :]
    col_bias_idx = col_idx - cq[:, None] + (NA_KW - 1)
    scale = d ** -0.5

    def row_block(r):
        rs = jnp.clip(r - kh // 2, 0, rows - kh)
        kr = lax.dynamic_slice_in_dim(kg, rs, kh, axis=1)[:, :, col_idx]
        vr = lax.dynamic_slice_in_dim(vg, rs, kh, axis=1)[:, :, col_idx]
        qr = lax.dynamic_index_in_dim(qg, r, axis=1, keepdims=False)
        qpr = lax.dynamic_index_in_dim(qpg, r, axis=1, keepdims=False)
        row_bias_idx = rs + jnp.arange(kh) - r + (NA_KH - 1)
        bias = rpb[:, row_bias_idx][:, :, col_bias_idx].transpose(0, 2, 1, 3)
        s_loc = jnp.einsum('bchd,bicjhd->bhcij', qr, kr).astype(jnp.float32) * scale + bias
        s_ctx = jnp.einsum('bchd,bnhd->bhcn', qpr, k_ctx).astype(jnp.float32) * scale
        logits = jnp.concatenate([s_loc.reshape(Bsz, H, GRID_W, kh * NA_KW), s_ctx], -1)
        p = jax.nn.softmax(logits, -1).astype(v.dtype)
        p_loc = p[..., :kh * NA_KW].reshape(Bsz, H, GRID_W, kh, NA_KW)
        p_ctx = p[..., kh * NA_KW:]
        return (jnp.einsum('bhcij,bicjhd->bchd', p_loc, vr)
                + jnp.einsum('bhcn,bnhd->bchd', p_ctx, v_ctx))

    out = lax.map(row_block, jnp.arange(rows))
    return out.transpose(1, 0, 2, 3, 4).reshape(Bsz, S, H * d)


def context_attention(q, k, v):
    s = jnp.einsum('bqhd,bkhd->bhqk', q, k).astype(jnp.float32) * (q.shape[-1] ** -0.5)
    p = jax.nn.softmax(s, -1).astype(v.dtype)
    return jnp.einsum('bhqk,bkhd->bqhd', p, v)


def hyena_filters(L, w1, b1, w2, b2, w3, decay):
    tn = jnp.arange(L, dtype=jnp.float32)[:, None] / L
    bands = jnp.arange(1, HY_BANDS + 1, dtype=jnp.float32)[None, :]
    ang = 2.0 * math.pi * bands * tn
    z = jnp.concatenate([tn, jnp.sin(ang), jnp.cos(ang)], -1)
    hmid = jnp.sin(HY_SIN_FREQ * (z @ w1.astype(jnp.float32) + b1.astype(jnp.float32)))
    hmid = jnp.sin(HY_SIN_FREQ * (hmid @ w2.astype(jnp.float32) + b2.astype(jnp.float32)))
    k = (hmid @ w3.astype(jnp.float32)) * jnp.exp(-tn * decay.astype(jnp.float32))
    k = k / (jnp.sum(jnp.abs(k), axis=0, keepdims=True) + 1e-6)
    return k[:, :GROUP_W], k[:, GROUP_W:]


def bidir_fftconv(u, k_fwd, k_bwd):
    L = u.shape[1]
    k2 = jnp.concatenate([k_fwd, jnp.zeros_like(k_fwd[:1]), k_bwd[1:][::-1]], 0)
    kf = jnp.fft.rfft(k2, n=2 * L, axis=0)
    uf = jnp.fft.rfft(u.astype(jnp.float32), n=2 * L, axis=1)
    return jnp.fft.irfft(uf * kf[None], n=2 * L, axis=1)[:, :L].astype(u.dtype)


def hyena(p, short_w, short_b, w1, b1, w2, b2, w3, decay, skip):
    L = p.shape[1]
    p = dwconv(p, short_w) + short_b
    x0, x1, v = jnp.split(p, 3, -1)
    k_fwd, k_bwd = hyena_filters(L, w1, b1, w2, b2, w3, decay)
    u = v * x1
    y = bidir_fftconv(u, k_fwd, k_bwd) + u * skip
    return y * x0


def segsum(a):
    T = a.shape[-1]
    a_rep = jnp.broadcast_to(a[..., None], a.shape + (T,))
    a_rep = jnp.where(jnp.tril(jnp.ones((T, T), bool), -1), a_rep, 0.0)
    s = jnp.cumsum(a_rep, axis=-2)
    return jnp.where(jnp.tril(jnp.ones((T, T), bool)), s, -jnp.inf)


def ssd_scan(x, dt, A, Bh, Ch, init_state, want_y):
    Bsz, L, H, P = x.shape
    N = Bh.shape[-1]
    nc = L // SSD_CHUNK
    xd = (x.astype(jnp.float32) * dt[..., None]).reshape(Bsz, nc, SSD_CHUNK, H, P)
    a = (dt * A).reshape(Bsz, nc, SSD_CHUNK, H).transpose(0, 3, 1, 2)
    Bc = Bh.astype(jnp.float32).reshape(Bsz, nc, SSD_CHUNK, H, N)
    Cc = Ch.astype(jnp.float32).reshape(Bsz, nc, SSD_CHUNK, H, N)
    a_cum = jnp.cumsum(a, -1)
    decay_states = jnp.exp(a_cum[..., -1:] - a_cum)
    states = jnp.einsum('bclhn,bhcl,bclhp->bchpn', Bc, decay_states, xd)
    states = jnp.concatenate([init_state[:, None], states], 1)
    decay_chunk = jnp.exp(segsum(jnp.pad(a_cum[..., -1], ((0, 0), (0, 0), (1, 0)))))
    new_states = jnp.einsum('bhzc,bchpn->bzhpn', decay_chunk, states)
    prev_states, final = new_states[:, :-1], new_states[:, -1]
    if not want_y:
        return None, final
    Lmat = jnp.exp(segsum(a))
    y_diag = jnp.einsum('bclhn,bcshn,bhcls,bcshp->bclhp', Cc, Bc, Lmat, xd)
    y_off = jnp.einsum('bclhn,bchpn,bhcl->bclhp', Cc, prev_states, jnp.exp(a_cum))
    return (y_diag + y_off).reshape(Bsz, L, H, P), final


def _flip(t):
    return jnp.flip(t, axis=1)


def ssd_bidir(z, xbc, dt_raw, init_f, init_b, want_y, conv_w, conv_b, a_log, dt_bias, d_skip, norm_g):
    Bsz, L, _ = xbc.shape
    xbc = jax.nn.silu(dwconv(xbc, conv_w) + conv_b)
    xs = xbc[..., :GROUP_W].reshape(Bsz, L, SSD_HEADS, SSD_HEAD_DIM)
    bc = xbc[..., GROUP_W:].reshape(Bsz, L, 2, SSD_GROUPS, SSD_STATE)
    rep = SSD_HEADS // SSD_GROUPS
    Bh = jnp.repeat(bc[:, :, 0], rep, axis=2)
    Ch = jnp.repeat(bc[:, :, 1], rep, axis=2)
    dt = jax.nn.softplus(dt_raw.astype(jnp.float32).reshape(Bsz, L, 2, SSD_HEADS) + dt_bias.astype(jnp.float32))
    A = -jnp.exp(a_log.astype(jnp.float32))
    y_f, s_f = ssd_scan(xs, dt[:, :, 0], A[0], Bh, Ch, init_f, want_y)
    y_b, s_b = ssd_scan(_flip(xs), _flip(dt[:, :, 1]), A[1], _flip(Bh), _flip(Ch), init_b, want_y)
    if not want_y:
        return None, s_f, s_b
    y = y_f + _flip(y_b) + xs.astype(jnp.float32) * d_skip.astype(jnp.float32)[:, None]
    yg = (y.reshape(Bsz, L, GROUP_W) * jax.nn.silu(z.astype(jnp.float32)))
    yg = yg.reshape(Bsz, L, SSD_GROUPS, GROUP_W // SSD_GROUPS)
    yg = yg * lax.rsqrt(jnp.mean(jnp.square(yg), -1, keepdims=True) + LN_EPS)
    return (yg.reshape(Bsz, L, GROUP_W) * norm_g).astype(z.dtype), s_f, s_b


def mixer_sublayer(h, hc, ctx_out, w_in, w_out, conf_dw_w, conf_dw_b, conf_norm_g, conf_norm_b, na_rpb,
                   hy_short_w, hy_short_b, hy_w1, hy_b1, hy_w2, hy_b2, hy_w3, hy_decay, hy_bias,
                   ssd_conv_w, ssd_conv_b, ssd_a_log, ssd_dt_bias, ssd_d, ssd_norm_g):
    Bsz, S, _ = h.shape
    Lc = hc.shape[1]
    pa, pb, py, pz, pxbc, pdt = _split_cols(h @ w_in)
    ca, cb, cy, cz, cxbc, cdt = _split_cols(hc @ w_in)
    qc, kc, vc = [t.reshape(Bsz, Lc, NA_HEADS, HEAD_DIM) for t in jnp.split(cb, 3, -1)]
    q, k, v = [t.reshape(Bsz, S, NA_HEADS, HEAD_DIM) for t in jnp.split(pb, 3, -1)]
    cos, sin = axial_rope(S // GRID_W, HEAD_DIM)
    y_b = neighborhood_attention(apply_rope(q, cos, sin), apply_rope(k, cos, sin), v, q, kc, vc, na_rpb)
    zero = jnp.zeros((Bsz, SSD_HEADS, SSD_HEAD_DIM, SSD_STATE), jnp.float32)
    y_dc, s_f, s_b = ssd_bidir(cz, cxbc, cdt, zero, zero, ctx_out, ssd_conv_w, ssd_conv_b,
                               ssd_a_log, ssd_dt_bias, ssd_d, ssd_norm_g)
    y_d, _, _ = ssd_bidir(pz, pxbc, pdt, s_f, s_b, True, ssd_conv_w, ssd_conv_b,
                          ssd_a_log, ssd_dt_bias, ssd_d, ssd_norm_g)
    y_a = conformer_conv(pa, conf_dw_w, conf_dw_b, conf_norm_g, conf_norm_b)
    y_c = hyena(py, hy_short_w, hy_short_b, hy_w1, hy_b1, hy_w2, hy_b2, hy_w3, hy_decay, hy_bias)
    y = jnp.concatenate([y_a, y_b, y_c, y_d], -1) @ w_out
    if not ctx_out:
        return y, None
    y_ac = conformer_conv(ca, conf_dw_w, conf_dw_b, conf_norm_g, conf_norm_b)
    y_bc = context_attention(qc, kc, vc).reshape(Bsz, Lc, GROUP_W)
    y_cc = hyena(cy, hy_short_w, hy_short_b, hy_w1, hy_b1, hy_w2, hy_b2, hy_w3, hy_decay, hy_bias)
    yc = jnp.concatenate([y_ac, y_bc, y_cc, y_dc], -1) @ w_out
    return y, yc


def peer(h, wq, sub_keys, u_tab, v_tab):
    Bsz, L, D = h.shape
    ht = h.reshape((Bsz * L) // PEER_BLOCK, PEER_BLOCK, D)
    K = PEER_TOPK

    def block(hb):
        q = (hb @ wq).reshape(PEER_BLOCK, PEER_HEADS, 2, PEER_QDIM // 2)
        s = jnp.einsum('thpk,hpnk->thpn', q, sub_keys).astype(jnp.float32)
        v1, i1 = lax.top_k(s[:, :, 0], K)
        v2, i2 = lax.top_k(s[:, :, 1], K)
        cand = (v1[..., :, None] + v2[..., None, :]).reshape(PEER_BLOCK, PEER_HEADS, K * K)
        cv, ci = lax.top_k(cand, K)
        e = (jnp.take_along_axis(i1, ci // K, axis=-1) * PEER_KEYS
             + jnp.take_along_axis(i2, ci % K, axis=-1)).reshape(PEER_BLOCK, PEER_HEADS * K)
        g = jax.nn.softmax(cv, -1).reshape(PEER_BLOCK, PEER_HEADS * K)
        act = jax.nn.gelu(jnp.einsum('td,ted->te', hb, u_tab[e]).astype(jnp.float32))
        w = (g * act).astype(hb.dtype)
        return jnp.einsum('te,ted->td', w, v_tab[e])

    return lax.map(block, ht).reshape(Bsz, L, D)


def setup_inputs(seed: int = 0) -> dict:
    key = jax.random.key(seed)
    ks = iter(jax.random.split(key, 48))
    f32 = jnp.float32
    Dp = DEPTH
    G = GROUP_W
    beta = (8.0 * DEPTH) ** -0.25

    def nrm(shape, scale):
        return jax.random.normal(next(ks), shape, f32) * scale

    x = nrm((BATCH, SEQ, D_MODEL), 1.0)
    c = nrm((BATCH, D_MODEL), 1.0)
    ctx = nrm((BATCH, CTX_LEN, D_MODEL), 1.0)
    c_ctx = nrm((D_MODEL,), 1.0)
    w_ada = nrm((Dp, D_MODEL, 6 * D_MODEL), D_MODEL ** -0.5)
    b_ada = nrm((Dp, 6 * D_MODEL), 0.02)
    w_in = nrm((Dp, D_MODEL, IN_COLS), D_MODEL ** -0.5)
    w_out = nrm((Dp, D_MIX, D_MODEL), D_MIX ** -0.5 * beta)
    ln1_g = 1.0 + nrm((Dp, D_MODEL), 0.02)
    ln1_b = nrm((Dp, D_MODEL), 0.02)
    ln2_g = 1.0 + nrm((Dp, D_MODEL), 0.02)
    ln2_b = nrm((Dp, D_MODEL), 0.02)
    conf_dw_w = nrm((Dp, CONF_K, G), CONF_K ** -0.5)
    conf_dw_b = nrm((Dp, G), 0.02)
    conf_norm_g = 1.0 + nrm((Dp, G), 0.02)
    conf_norm_b = nrm((Dp, G), 0.02)
    na_rpb = nrm((Dp, NA_HEADS, 2 * NA_KH - 1, 2 * NA_KW - 1), 0.1)
    hy_short_w = nrm((Dp, HY_SHORT, 3 * G), HY_SHORT ** -0.5)
    hy_short_b = nrm((Dp, 3 * G), 0.02)
    hy_w1 = nrm((Dp, HY_EMB, HY_HIDDEN), HY_EMB ** -0.5)
    hy_b1 = nrm((Dp, HY_HIDDEN), 0.1)
    hy_w2 = nrm((Dp, HY_HIDDEN, HY_HIDDEN), HY_HIDDEN ** -0.5)
    hy_b2 = nrm((Dp, HY_HIDDEN), 0.1)
    hy_w3 = nrm((Dp, HY_HIDDEN, 2 * G), HY_HIDDEN ** -0.5)
    hy_decay = jnp.broadcast_to(jnp.linspace(HY_MIN_DECAY, HY_MAX_DECAY, 2 * G, dtype=f32), (Dp, 2 * G)) * (1.0 + nrm((Dp, 2 * G), 0.05))
    hy_bias = nrm((Dp, G), 0.5)
    ssd_conv_w = nrm((Dp, SSD_CONV, XBC_W), SSD_CONV ** -0.5)
    ssd_conv_b = nrm((Dp, XBC_W), 0.02)
    ssd_a_log = jnp.log(jax.random.uniform(next(ks), (Dp, 2, SSD_HEADS), f32, minval=1.0, maxval=16.0))
    dt0 = jnp.exp(jax.random.uniform(next(ks), (Dp, 2, SSD_HEADS), f32, minval=math.log(1e-3), maxval=math.log(1e-1)))
    ssd_dt_bias = dt0 + jnp.log(-jnp.expm1(-dt0))
    ssd_d = 1.0 + nrm((Dp, SSD_HEADS), 0.02)
    ssd_norm_g = 1.0 + nrm((Dp, G), 0.02)
    peer_wq = nrm((Dp, D_MODEL, PEER_HEADS * PEER_QDIM), D_MODEL ** -0.5)
    peer_keys = nrm((Dp, PEER_HEADS, 2, PEER_KEYS, PEER_QDIM // 2), (PEER_QDIM // 2) ** -0.5)
    peer_u = nrm((Dp, N_EXPERTS, D_MODEL), D_MODEL ** -0.5)
    peer_v = nrm((Dp, N_EXPERTS, D_MODEL), (PEER_HEADS * PEER_TOPK) ** -0.5 * beta)
    return {'x': x, 'c': c, 'ctx': ctx, 'c_ctx': c_ctx, 'w_ada': w_ada, 'b_ada': b_ada,
            'w_in': w_in, 'w_out': w_out, 'ln1_g': ln1_g, 'ln1_b': ln1_b, 'ln2_g': ln2_g, 'ln2_b': ln2_b,
            'conf_dw_w': conf_dw_w, 'conf_dw_b': conf_dw_b, 'conf_norm_g': conf_norm_g, 'conf_norm_b': conf_norm_b,
            'na_rpb': na_rpb, 'hy_short_w': hy_short_w, 'hy_short_b': hy_short_b, 'hy_w1': hy_w1, 'hy_b1': hy_b1,
            'hy_w2': hy_w2, 'hy_b2': hy_b2, 'hy_w3': hy_w3, 'hy_decay': hy_decay, 'hy_bias': hy_bias,
            'ssd_conv_w': ssd_conv_w, 'ssd_conv_b': ssd_conv_b, 'ssd_a_log': ssd_a_log, 'ssd_dt_bias': ssd_dt_bias,
            'ssd_d': ssd_d, 'ssd_norm_g': ssd_norm_g, 'peer_wq': peer_wq, 'peer_keys': peer_keys,
            'peer_u': peer_u, 'peer_v': peer_v}


def reference(x, c, ctx, c_ctx, w_ada, b_ada, w_in, w_out, ln1_g, ln1_b, ln2_g, ln2_b,
              conf_dw_w, conf_dw_b, conf_norm_g, conf_norm_b, na_rpb, hy_short_w, hy_short_b,
              hy_w1, hy_b1, hy_w2, hy_b2, hy_w3, hy_decay, hy_bias, ssd_conv_w, ssd_conv_b,
              ssd_a_log, ssd_dt_bias, ssd_d, ssd_norm_g, peer_wq, peer_keys, peer_u, peer_v):
    alpha = (2.0 * DEPTH) ** 0.25
    s_c = jax.nn.silu(c)
    s_cc = jax.nn.silu(c_ctx)
    xc = ctx
    for l in range(DEPTH):
        ctx_out = l < DEPTH - 1
        mod = (s_c @ w_ada[l] + b_ada[l])[:, None, :]
        mod_c = (s_cc @ w_ada[l] + b_ada[l])[None, None, :]
        sh1, sc1, g1, sh2, sc2, g2 = jnp.split(mod, 6, -1)
        sh1c, sc1c, g1c, sh2c, sc2c, g2c = jnp.split(mod_c, 6, -1)
        y, yc = mixer_sublayer(x * (1.0 + sc1) + sh1, xc * (1.0 + sc1c) + sh1c, ctx_out,
                               w_in[l], w_out[l], conf_dw_w[l], conf_dw_b[l], conf_norm_g[l], conf_norm_b[l],
                               na_rpb[l], hy_short_w[l], hy_short_b[l], hy_w1[l], hy_b1[l], hy_w2[l], hy_b2[l],
                               hy_w3[l], hy_decay[l], hy_bias[l], ssd_conv_w[l], ssd_conv_b[l],
                               ssd_a_log[l], ssd_dt_bias[l], ssd_d[l], ssd_norm_g[l])
        x = layer_norm(alpha * x + g1 * y, ln1_g[l], ln1_b[l])
        x = layer_norm(alpha * x + g2 * peer(x * (1.0 + sc2) + sh2, peer_wq[l], peer_keys[l], peer_u[l], peer_v[l]),
                       ln2_g[l], ln2_b[l])
        if ctx_out:
            xc = layer_norm(alpha * xc + g1c * yc, ln1_g[l], ln1_b[l])
            xc = layer_norm(alpha * xc + g2c * peer(xc * (1.0 + sc2c) + sh2c, peer_wq[l], peer_keys[l], peer_u[l], peer_v[l]),
                            ln2_g[l], ln2_b[l])
    return x
```

```python
import math
from contextlib import ExitStack
import numpy as np
import ml_dtypes
import concourse.bass as bass
import concourse.mybir as mybir
from concourse.bass_utils import run_bass_kernel_spmd

F32 = mybir.dt.float32; BF16 = mybir.dt.bfloat16; I32 = mybir.dt.int32; U32 = mybir.dt.uint32
AF = mybir.ActivationFunctionType; ALU = mybir.AluOpType; AX = mybir.AxisListType

D = 1024; NB = 2; S = 2048; LC = 256; DEPTH = 2; G = 256
ALPHA = (2.0 * DEPTH) ** 0.25
EPS = 1e-5
NDQ = 20
DENSE = True
RELAX_SAME_ENGINE = True
PI = math.pi
ARENA_COLS = 44 * 1024

WSHAPES = dict(
    w_ada=[2, 1024, 6144], b_ada=[2, 6144], w_in=[2, 1024, 2824], w_out=[2, 1024, 1024],
    ln1_g=[2, 1024], ln1_b=[2, 1024], ln2_g=[2, 1024], ln2_b=[2, 1024],
    conf_dw_w=[2, 31, 256], conf_dw_b=[2, 256], conf_norm_g=[2, 256], conf_norm_b=[2, 256],
    na_rpb=[2, 4, 15, 31], hy_short_w=[2, 3, 768], hy_short_b=[2, 768], hy_w1=[2, 33, 64], hy_b1=[2, 64],
    hy_w2=[2, 64, 64], hy_b2=[2, 64], hy_w3=[2, 64, 512], hy_decay=[2, 512], hy_bias=[2, 256],
    ssd_conv_w=[2, 3, 512], ssd_conv_b=[2, 512], ssd_a_log=[2, 2, 4], ssd_dt_bias=[2, 2, 4], ssd_d=[2, 4],
    ssd_norm_g=[2, 256], peer_wq=[2, 1024, 2048], peer_keys=[2, 8, 2, 128, 128],
    peer_u0=[16384, 1024], peer_u1=[16384, 1024], peer_v0=[16384, 1024], peer_v1=[16384, 1024])


class Prog:
    def __init__(self, nc):
        self.nc = nc
        self.E = dict(pe=nc.tensor, dve=nc.vector, act=nc.scalar, pool=nc.gpsimd, sp=nc.sync)
        self.stream = {e: [] for e in self.E}
        self.cnt = {}
        self.waited = {e: {} for e in self.E}
        self.bufs = {}
        self.dq = {e: 0 for e in self.E}
        self.nops = 0

    def _need(self, eng, deps):
        for k, v in deps.items():
            if eng == 'pe' and k == 'c_pe':
                continue
            if self.waited[eng].get(k, 0) < v:
                self.stream[eng].append(('wait', k, v))
                self.waited[eng][k] = v

    def op(self, eng, fn, r=(), w=(), dma=False):
        w = list(w) + [b for b in r if b.startswith('ps') and b not in w]
        deps = {}
        own = 'c_' + eng
        owncnt = self.cnt.get(own, 0)
        relax = RELAX_SAME_ENGINE and not dma
        def add(tok, raw):
            if tok is None:
                return
            if relax and tok[0] == own:
                if not raw or tok[1] < owncnt:
                    return
            if deps.get(tok[0], 0) < tok[1]:
                deps[tok[0]] = tok[1]
        psw = set(b for b in r if b.startswith('ps'))
        for b in r:
            st = self.bufs.get(b)
            if st:
                add(st['w'], True)
        for b in w:
            st = self.bufs.get(b)
            if st:
                add(st['w'], b in psw)
                for k, v in st['r'].items():
                    add((k, v), False)
        if dma:
            i = self.dq[eng] % NDQ
            self.dq[eng] += 1
            key = 'd_%s_%d' % (eng, i)
            cur = self.cnt.get(key, 0)
            if cur:
                add((key, cur), True)
            inc = 16
        else:
            key = 'c_' + eng
            inc = 1
        self._need(eng, deps)
        self.cnt[key] = self.cnt.get(key, 0) + inc
        tok = (key, self.cnt[key])
        self.stream[eng].append(('op', fn, key, inc))
        self.nops += 1
        for b in r:
            st = self.bufs.setdefault(b, {'w': None, 'r': {}})
            if st['r'].get(tok[0], 0) < tok[1]:
                st['r'][tok[0]] = tok[1]
        for b in w:
            self.bufs[b] = {'w': tok, 'r': {}}
        return tok

    def barrier(self):
        for e in self.E:
            self._need(e, dict(self.cnt))
        self.bufs = {}

    def emit(self, sems, block):
        def mk(e):
            def body(eng):
                for it in self.stream[e]:
                    if it[0] == 'wait':
                        eng.wait_ge(sems[it[1]], it[2])
                    else:
                        ins = it[1](eng)
                        ins.then_inc(sems[it[2]], it[3])
            return body
        block.tensor(mk('pe')); block.vector(mk('dve')); block.scalar(mk('act'))
        block.gpsimd(mk('pool')); block.sync(mk('sp'))

    def sem_keys(self):
        ks = ['c_' + e for e in self.E]
        for e in ('sp', 'act', 'pool'):
            ks += ['d_%s_%d' % (e, i) for i in range(NDQ)]
        return ks


class _Stop(Exception):
    pass


class Arena:
    def __init__(self, t, n):
        self.t = t; self.n = n; self.top = 0; self.k = 0; self.P = None

    def alloc(self, cols, dt=F32, parts=128):
        n32 = cols if dt != BF16 else (cols + 1) // 2
        assert self.top + n32 <= self.n, ('arena overflow', self.top, n32, self.n)
        a = self.t[0:parts, self.top:self.top + n32]
        if dt != F32:
            a = a.bitcast(dt)
            if dt == BF16 and cols % 2:
                a = a[:, 0:cols]
        self.top += n32
        self.hw = max(getattr(self, 'hw', 0), self.top)
        self.k += 1
        return a, 'A%d' % self.k

    def mark(self):
        return self.top

    def release(self, m):
        if self.P is not None:
            self.P.barrier()
        if m == 0 or getattr(self, 'verbose', False):
            pass
        self.top = m


def consts():
    C = {}
    n_f = 16
    inv = 10000.0 ** (-np.arange(n_f, dtype=np.float32) / n_f)
    t = np.arange(S)
    r = (t // 64).astype(np.float32); col = (t % 64).astype(np.float32)
    ang = np.concatenate([r[:, None] * inv, col[:, None] * inv], -1)
    cos = np.cos(ang).astype(np.float32).T; sin = np.sin(ang).astype(np.float32).T
    C['rope'] = np.stack([np.concatenate([cos, cos], 0), np.concatenate([-sin, sin], 0)]).astype(np.float32)
    m = np.zeros((9, 128, 896), np.float32)
    kl = np.arange(128); krl = kl // 64; kc = kl % 64
    for cls, i in enumerate([0, 1, 2, 3, 6, 12, 13, 14, 15]):
        for dj in range(-3, 4):
            j = i + dj
            if j < 0 or j > 15:
                continue
            ql = np.arange(128); rl = ql // 64; c = ql % 64
            rr = 2 * j + rl; kr = 2 * i + krl
            rs = np.clip(rr - 4, 0, 24); cs = np.clip(c - 8, 0, 48)
            ok = ((kr[:, None] >= rs[None, :]) & (kr[:, None] < rs[None, :] + 8)
                  & (kc[:, None] >= cs[None, :]) & (kc[:, None] < cs[None, :] + 16))
            m[cls, :, (dj + 3) * 128:(dj + 4) * 128] = ok
    C['namask'] = np.ascontiguousarray(m.transpose(1, 0, 2)).astype(ml_dtypes.bfloat16)
    sp = np.arange(128)[:, None]; lq = np.arange(128)[None, :]
    C['iota256'] = np.arange(256, dtype=np.float32)[None, :]
    C['ssdc'] = np.stack([(sp <= lq), (sp < lq), np.where(lq < sp, -30000.0, 0.0), np.where(lq > sp, 30000.0, 0.0)]).astype(np.float32)
    for L in (S, LC):
        tn = np.arange(L, dtype=np.float32)[:, None] / np.float32(L)
        bands = np.arange(1, 17, dtype=np.float32)[None, :]
        a2 = (2.0 * math.pi * bands * tn).astype(np.float32)
        z = np.concatenate([tn, np.sin(a2), np.cos(a2)], -1).astype(np.float32)
        C['zT%d' % L] = np.ascontiguousarray(z.T)
        N = 2 * L; NFp = L + 128
        s = np.arange(L, dtype=np.float64)[:, None]; f = np.arange(NFp, dtype=np.float64)[None, :]
        th = 2 * np.pi * ((s * f) % N) / N
        valid = (f <= L)
        fw = np.concatenate([np.cos(th) * valid, np.sin(th) * valid], 1)
        nch = (2 * NFp + 511) // 512
        fwp = np.zeros((L, nch * 512), np.float64); fwp[:, :2 * NFp] = fw
        fwc = fwp.reshape(L // 128, 128, nch, 512).transpose(2, 1, 0, 3)
        C['fw%d' % L] = np.ascontiguousarray(fwc).astype(np.float32).astype(ml_dtypes.bfloat16)
        wf = np.where((f == 0) | (f == L), 1.0, 2.0) * valid / N
        iv = np.stack([(np.cos(th) * wf).T, (-np.sin(th) * wf).T])
        C['inv%d' % L] = iv.astype(np.float32).astype(ml_dtypes.bfloat16)
    return C


CSHAPES = dict(rope=([2, 64, S], F32), iota256=([1, 256], F32), namask=([128, 9, 896], BF16), ssdc=([4, 128, 128], F32),
               zT2048=([33, S], F32), zT256=([33, LC], F32),
               fw2048=([9, 128, 16, 512], BF16), inv2048=([2, S + 128, S], BF16),
               fw256=([2, 128, 2, 512], BF16), inv256=([2, LC + 128, LC], BF16))


def build(plan=None, dbg=None):
    plan = plan or {}
    layers = plan.get('layers', list(range(DEPTH)))
    batches = plan.get('batches', list(range(NB)))
    mixers = plan.get('mixers', ['conf', 'attn', 'hy', 'ssd'])
    do_tail = plan.get('tail', True)
    nc = bass.Bass("TRN2", target_bir_lowering=False)
    I = {}
    def din(name, shape, dt=F32):
        I[name] = nc.dram_tensor(name, list(shape), dt, kind="ExternalInput").ap()
    def dscr(name, shape, dt=F32):
        return nc.dram_tensor(name, list(shape), dt, kind="Internal").ap()
    din('x', [NB, S, D]); din('ctx', [NB, LC, D]); din('cvec', [3, D])
    for k, shp in WSHAPES.items():
        din(k, shp)
    for k, (shp, dt) in CSHAPES.items():
        din(k, shp, dt)
    out = nc.dram_tensor('out', [NB, S, D], F32, kind="ExternalOutput").ap()
    mod_d = dscr('mod_d', [DEPTH, 3, 6 * D])
    xs_d = dscr('xs_d', [NB, S, D]); xcs_d = dscr('xcs_d', [NB, LC, D])
    ycat_d = {S: dscr('ycat_lat', [8, 128, S], BF16), LC: dscr('ycat_ctx', [8, 128, LC], BF16)}
    ksp_d = {S: dscr('ksp_lat', [2, 256, S + 128]), LC: dscr('ksp_ctx', [2, 256, LC + 128])}
    zq_d = dscr('zq_d', [60, 64, 127])
    UT_d = dscr('UT_d', [128, 128, 1024], BF16); V_d = dscr('V_d', [128, 128, 1024], BF16)
    x1_d = dscr('x1_d', [S, D]); h2T_d = dscr('h2T_d', [8, 128, S], BF16); sm_d = dscr('sm_d', [S // 128, 128, 3 * 128])
    dbg_out = None
    if dbg:
        dbg_out = nc.dram_tensor('dbg', list(dbg[1]), dbg[2] if len(dbg) > 2 and dbg[2] is not None else F32, kind="ExternalOutput").ap()

    P = Prog(nc)
    with ExitStack() as es:
        def sb(name, shape, dt=F32):
            return es.enter_context(nc.sbuf_tensor('s_' + name, list(shape), dt))
        A = Arena(sb('arena', [128, ARENA_COLS], F32), ARENA_COLS)
        A.P = P
        ident = sb('ident', [128, 128]); ones = sb('ones', [128, 128]); blk = sb('blk', [128, 128])
        modT = sb('modT', [128, DEPTH * 48 * 3]); modT1 = sb('modT1', [128, DEPTH * 48 * 3])
        ssdc = sb('ssdc', [128, 4 * 128])
        cw = sb('cw', [128, 2 * 31]); cv3 = sb('cv3', [128, 2 * 3])
        hsp = sb('hsp', [128, 6 * 4]); hskip = sb('hskip', [128, 2])
        sxp = sb('sxp', [128, 2 * 4]); sbp = sb('sbp', [64, 4 * 4])
        A_bc = sb('A_bc', [128, 8]); dtb_bc = sb('dtb_bc', [128, 8]); dsk_bc = sb('dsk_bc', [128, 4]); ng_bc = sb('ng_bc', [128, 256])
        biasK = sb('biasK', [128, 4 * 896])
        kcT = sb('kcT', [64, 4 * 256], BF16); vc1 = sb('vc1', [128, 2 * 4 * 65], BF16)
        Sst = sb('Sst', [64, 8 * 64]); SstB = sb('SstB', [64, 8 * 64], BF16)
        ps = [es.enter_context(nc.psum_tensor('ps%d' % i, [128, 512], F32)) for i in range(8)]
        psn = ['ps%d' % i for i in range(8)]
        sems = {k: es.enter_context(nc.semaphore(k)) for k in P.sem_keys()}
        qrot = [0]

        ck = [0]
        def CK(tag):
            ck[0] += 1
            if plan.get('stop') == ck[0] or plan.get('stoptag') == tag:
                print('STOP at', tag, flush=True)
                raise _Stop()
        def MM(o, lhsT, rhs, start, stop, r, w):
            P.op('pe', lambda e: e.matmul(o, lhsT=lhsT, rhs=rhs, start=start, stop=stop), r=r, w=w)
        def TR(o, in_, n, r, w):
            P.op('pe', lambda e: e.transpose(out=o, in_=in_, identity=ident[0:n, 0:n]), r=list(r) + ['ident'], w=w)
        def DMA(o, in_, r, w, q=None, slow=False):
            if q is None:
                q = ('sp', 'act')[qrot[0] % 2]; qrot[0] += 1
            P.op(q, lambda e: e.dma_start(out=o, in_=in_, allow_slow_non_contiguous=slow), r=r, w=w, dma=True)
        def ACT(o, in_, func, r, w, bias=None, scale=None, accum=None):
            kw = {}
            if bias is not None: kw['bias'] = bias
            if scale is not None: kw['scale'] = scale
            if accum is not None: kw['accum_out'] = accum
            P.op('act', lambda e: e.activation(out=o, in_=in_, func=func, **kw), r=r, w=w)
        def TT(eng, o, in0, in1, op, r, w):
            P.op(eng, lambda e: e.tensor_tensor(out=o, in0=in0, in1=in1, op=op), r=r, w=w)
        def TS(eng, o, in0, s1, s2, op0, op1, r, w, accum=None):
            kw = {}
            if op1 is not None: kw['op1'] = op1
            if accum is not None: kw['accum_out'] = accum
            P.op(eng, lambda e: e.tensor_scalar(out=o, in0=in0, scalar1=s1, scalar2=s2, op0=op0, **kw), r=r, w=w)
        def STT(o, in0, sc, in1, op0, op1, r, w, accum=None):
            kw = {}
            if accum is not None: kw['accum_out'] = accum
            P.op('dve', lambda e: e.scalar_tensor_tensor(out=o, in0=in0, scalar=sc, in1=in1, op0=op0, op1=op1, **kw), r=r, w=w)
        def CP(eng, o, in_, r, w):
            if eng == 'act':
                P.op('act', lambda e: e.copy(out=o, in_=in_), r=r, w=w)
            else:
                P.op(eng, lambda e: e.tensor_copy(out=o, in_=in_), r=r, w=w)
        def MS(eng, o, val, w):
            P.op(eng, lambda e: e.memset(o, val), w=w)
        def RCP(o, in_, r, w):
            P.op('dve', lambda e: e.reciprocal(out=o, in_=in_), r=r, w=w)

        MS('pool', ident[:], 0.0, ['ident'])
        P.op('pool', lambda e: e.affine_select(out=ident[:], in_=ident[:], pattern=[[-1, 128]], compare_op=ALU.not_equal,
                                                fill=1.0, base=0, channel_multiplier=1), r=['ident'], w=['ident'])
        MS('pool', ones[:], 1.0, ['ones'])
        MS('pool', modT[:], 0.0, ['modT'])
        MS('pool', blk[:], 0.0, ['blk'])
        MS('pool', blk[0:64, 0:64], 1.0 / 64, ['blk'])
        MS('pool', blk[64:128, 64:128], 1.0 / 64, ['blk'])
        ssdc3 = ssdc[:].rearrange("p (a b) -> p a b", a=4)
        for a_ in range(4):
            DMA(ssdc3[:, a_, :], I['ssdc'][a_], [], ['ssdc'])
        P.barrier()

        mT4 = modT[:].rearrange("p (l j v) -> p l j v", l=DEPTH, j=48)
        mT14 = modT1[:].rearrange("p (l j v) -> p l j v", l=DEPTH, j=48)

        def phase_mod(l):
            m0 = A.mark()
            cT, cTn = A.alloc(24); sT, sTn = A.alloc(24)
            mrow, mrn = A.alloc(6 * D, parts=3)
            wbuf = [A.alloc(8 * 512) for _ in range(2)]
            brow, brn = A.alloc(6 * D, parts=1)
            cT3 = cT.rearrange("p (k v) -> p k v", v=3)
            for v in range(3):
                DMA(cT3[:, :, v], I['cvec'][v].rearrange("(k p) -> p k", p=128), [], [cTn], slow=True)
            DMA(brow, I['b_ada'][l:l + 1, :], [], [brn])
            ACT(sT, cT, AF.Silu, [cTn], [sTn])
            sT3 = sT.rearrange("p (k v) -> p k v", v=3)
            for n in range(12):
                wb, wbn = wbuf[n % 2]
                wb3 = wb.rearrange("p (k c) -> p k c", c=512)
                DMA(wb3, I['w_ada'][l, :, n * 512:(n + 1) * 512].rearrange("(k p) c -> p k c", p=128), [], [wbn])
                pt = ps[n % 2]; pn = psn[n % 2]
                for k in range(8):
                    MM(pt[0:3, :], sT3[:, k, :], wb3[:, k, :], k == 0, False, [sTn, wbn], [pn])
                MM(pt[0:3, :], ones[0:1, 0:3], brow[0:1, n * 512:(n + 1) * 512], False, True, ['ones', brn], [pn])
                CP('dve', mrow[0:3, n * 512:(n + 1) * 512], pt[0:3, :], [pn], [mrn])
            DMA(mod_d[l], mrow, [mrn], ['mod_d'], q='sp')
            for j in range(48):
                pt = ps[2 + j % 2]; pn = psn[2 + j % 2]
                TR(pt[:, 0:3], mrow[0:3, j * 128:(j + 1) * 128], 3, [mrn], [pn])
                CP('dve', mT4[:, l, j, :], pt[:, 0:3], [pn], ['modT'])
            P.barrier()
            A.release(m0)

        for l in layers:
            phase_mod(l)
        TS('dve', modT1[:], modT[:], 1.0, None, ALU.add, None, ['modT'], ['modT1'])
        P.barrier()

        def paramT(dst3, dname, rows, n, nch, Pn, c0=0):
            m0 = A.mark()
            if not isinstance(rows, list):
                C = rows.shape[-1]
                stg, sn = A.alloc(C, parts=n)
                DMA(stg[0:n, :], rows, [], [sn])
            else:
                C = rows[0].shape[-1]
                stg, sn = A.alloc(C, parts=max(n, 1))
                j = 0
                for rw in rows:
                    nr = rw.shape[0]
                    DMA(stg[j:j + nr, :], rw, [], [sn])
                    j += nr
            for ch in range(nch):
                pt = ps[ch % 2]; pn = psn[ch % 2]
                TR(pt[0:Pn, 0:n], stg[0:n, c0 + ch * Pn:c0 + (ch + 1) * Pn], n, [sn], [pn])
                CP('dve', dst3[0:Pn, ch, :], pt[0:Pn, 0:n], [pn], [dname])
            P.barrier()
            A.release(m0)

        def load_w(dst3, dname, src2d, n, wst):
            for k in range(8):
                st, sn = wst[k % 2]
                DMA(st[:, 0:n], src2d[k * 128:(k + 1) * 128, :], [], [sn])
                CP('act' if k % 2 else 'pool', dst3[:, k, 0:n], st[:, 0:n], [sn], [dname])

        def phase_params(l):
            row = lambda nm: I[nm][l:l + 1, :]
            paramT(cw[:].rearrange("p (c k) -> p c k", c=2), 'cw', I['conf_dw_w'][l], 31, 2, 128)
            paramT(cv3[:].rearrange("p (c k) -> p c k", c=2), 'cv3', [row('conf_dw_b'), row('conf_norm_g'), row('conf_norm_b')], 3, 2, 128)
            paramT(hsp[:].rearrange("p (c k) -> p c k", c=6), 'hsp', [I['hy_short_w'][l], row('hy_short_b')], 4, 6, 128)
            paramT(hskip[:].rearrange("p (c k) -> p c k", c=2), 'hskip', [row('hy_bias')], 1, 2, 128)
            srows = [I['ssd_conv_w'][l], row('ssd_conv_b')]
            paramT(sxp[:].rearrange("p (c k) -> p c k", c=2), 'sxp', srows, 4, 2, 128)
            paramT(sbp[:].rearrange("p (c k) -> p c k", c=4), 'sbp', srows, 4, 4, 64, c0=256)
            alog = I['ssd_a_log'].rearrange("l a b -> l (a b)")[l:l + 1, :]
            dtb = I['ssd_dt_bias'].rearrange("l a b -> l (a b)")[l:l + 1, :]
            DMA(A_bc[:], alog.partition_broadcast(128), [], ['A_bc'])
            DMA(dtb_bc[:], dtb.partition_broadcast(128), [], ['dtb_bc'])
            DMA(dsk_bc[:], row('ssd_d').partition_broadcast(128), [], ['dsk_bc'])
            DMA(ng_bc[:], row('ssd_norm_g').partition_broadcast(128), [], ['ng_bc'])
            ACT(A_bc[:], A_bc[:], AF.Exp, ['A_bc'], ['A_bc'])
            TS('dve', A_bc[:], A_bc[:], -1.0, None, ALU.mult, None, ['A_bc'], ['A_bc'])
            m0 = A.mark()
            Pt, Pn_ = A.alloc(127, parts=60)
            MS('pool', Pt, 0.0, [Pn_])
            DMA(Pt[:, 48:79], I['na_rpb'][l].rearrange("h a b -> (h a) b"), [Pn_], [Pn_])
            DMA(zq_d, Pt.unsqueeze(1).to_broadcast([60, 64, 127]), [Pn_], ['zq_d'], q='sp')
            bq, bqn = A.alloc(4 * 896)
            bq3 = bq.rearrange("p (h c) -> p h c", h=4)
            bK3 = biasK[:].rearrange("p (h c) -> p h c", h=4)
            for h in range(4):
                for di in range(-3, 4):
                    for rl in range(2):
                        for krl in range(2):
                            X = 2 * di + krl - rl + 7
                            src = bass.AP(zq_d.tensor, (h * 15 + X) * 64 * 127 + 63, [[126, 64], [1, 64]])
                            DMA(bq3[rl * 64:(rl + 1) * 64, h, (di + 3) * 128 + krl * 64:(di + 3) * 128 + (krl + 1) * 64], src, ['zq_d'], [bqn])
            k_ = 0
            for h in range(4):
                for dj in range(-3, 4):
                    di = -dj
                    pt = ps[k_ % 2]; pn = psn[k_ % 2]; k_ += 1
                    TR(pt[:, 0:128], bq3[:, h, (di + 3) * 128:(di + 4) * 128], 128, [bqn], [pn])
                    CP('dve', bK3[:, h, (dj + 3) * 128:(dj + 4) * 128], pt[:, 0:128], [pn], ['biasK'])
            P.barrier()
            A.release(m0)

        def sinwrap(dst, src, bias, n, tmp, r, w):
            tA, tAn = tmp[0]; tB, tBn = tmp[1]; tC, tCn = tmp[2]
            TS('dve', tA[0:64, 0:n], src, bias, None, ALU.add, None, r, [tAn])
            TS('dve', tB[0:64, 0:n], tA[0:64, 0:n], PI, -2 * PI, ALU.is_gt, ALU.mult, [tAn], [tBn])
            TT('dve', tC[0:64, 0:n], tA[0:64, 0:n], tB[0:64, 0:n], ALU.add, [tAn, tBn], [tCn])
            TS('dve', tB[0:64, 0:n], tA[0:64, 0:n], -PI, 2 * PI, ALU.is_lt, ALU.mult, [tAn], [tBn])
            TT('dve', tA[0:64, 0:n], tC[0:64, 0:n], tB[0:64, 0:n], ALU.add, [tCn, tBn], [tAn])
            ACT(dst, tA[0:64, 0:n], AF.Sin, [tAn], w)

        def phase_filters(l, L):
            NT = L // 128; CH = min(512, L); NQ = L // CH; NFp = L + 128
            m0 = A.mark()
            zT, zn = A.alloc(L, parts=33)
            DMA(zT, I['zT%d' % L], [], [zn])
            w1, w1n = A.alloc(64, parts=33); w2, w2n = A.alloc(64, parts=64); w3, w3n = A.alloc(512, parts=64)
            dec, decn = A.alloc(512, parts=1)
            b12, b12n = A.alloc(2, parts=64)
            DMA(w1, I['hy_w1'][l], [], [w1n]); DMA(w2, I['hy_w2'][l], [], [w2n]); DMA(w3, I['hy_w3'][l], [], [w3n])
            DMA(dec, I['hy_decay'][l:l + 1, :], [], [decn])
            bst, bstn = A.alloc(64, parts=2)
            DMA(bst[0:1, :], I['hy_b1'][l:l + 1, :], [], [bstn]); DMA(bst[1:2, :], I['hy_b2'][l:l + 1, :], [], [bstn])
            TR(ps[0][0:64, 0:2], bst[0:2, 0:64], 2, [bstn], [psn[0]])
            CP('dve', b12, ps[0][0:64, 0:2], [psn[0]], [b12n])
            h1, h1n = A.alloc(L, parts=64); h2, h2n = A.alloc(L, parts=64)
            tmp = [A.alloc(CH, parts=64) for _ in range(3)]
            for q in range(NQ):
                cs = slice(q * CH, (q + 1) * CH)
                MM(ps[1][0:64, 0:CH], w1[0:33, :], zT[0:33, cs], True, True, [w1n, zn], [psn[1]])
                sinwrap(h1[:, cs], ps[1][0:64, 0:CH], b12[:, 0:1], CH, tmp, [psn[1], b12n], [h1n])
            for q in range(NQ):
                cs = slice(q * CH, (q + 1) * CH)
                MM(ps[2][0:64, 0:CH], w2[0:64, :], h1[:, cs], True, True, [w2n, h1n], [psn[2]])
                sinwrap(h2[:, cs], ps[2][0:64, 0:CH], b12[:, 1:2], CH, tmp, [psn[2], b12n], [h2n])
            kk, kkn = A.alloc(NT * 512)
            kk3 = kk.rearrange("p (t c) -> p t c", c=512)
            et = [A.alloc(512) for _ in range(2)]; abt = [A.alloc(512) for _ in range(2)]
            for t in range(NT):
                pa = ps[2 + t % 2]; pb = ps[4 + t % 2]
                MM(pa[:, 0:512], h2[0:64, t * 128:(t + 1) * 128], w3[0:64, :], True, True, [h2n, w3n], [psn[2 + t % 2]])
                MM(pb[:, 0:512], zT[0:1, t * 128:(t + 1) * 128], dec[0:1, :], True, True, [zn, decn], [psn[4 + t % 2]])
                e_, en = et[t % 2]; a_, an = abt[t % 2]
                ACT(e_, pb[:, 0:512], AF.Exp, [psn[4 + t % 2]], [en], scale=-1.0)
                TT('dve', kk3[:, t, :], pa[:, 0:512], e_, ALU.mult, [psn[2 + t % 2], en], [kkn])
                ACT(a_, kk3[:, t, :], AF.Abs, [kkn], [an])
                MM(ps[6][:, 0:512], ones[:], a_, t == 0, t == NT - 1, ['ones', an], [psn[6]])
            rect, rn = A.alloc(512)
            TS('dve', rect, ps[6][:, 0:512], 1e-6, None, ALU.add, None, [psn[6]], [rn])
            RCP(rect, rect, [rn], [rn])
            TT('dve', kk3, kk3, rect.unsqueeze(1).to_broadcast([128, NT, 512]), ALU.mult, [kkn, rn], [kkn])
            MS('dve', kk3[0:1, 0, 256:512], 0.0, [kkn])
            ksd, ksdn = A.alloc(2 * NT * 256, BF16)
            ksd4 = ksd.rearrange("p (a t c) -> p a t c", a=2, c=256)
            TT('dve', ksd4[:, 0], kk3[:, :, 0:256], kk3[:, :, 256:512], ALU.add, [kkn], [ksdn])
            TT('dve', ksd4[:, 1], kk3[:, :, 256:512], kk3[:, :, 0:256], ALU.subtract, [kkn], [ksdn])
            if dbg and dbg[0] == 'filt%d' % L:
                DMA(dbg_out, kk, [kkn], ['dbg'], q='sp')
            tbs = [A.alloc(NT * 512, BF16) for _ in range(2)]
            ko = [A.alloc(512) for _ in range(2)]
            nch = (2 * NFp + 511) // 512
            for g_ in range(nch):
                tb, tbn = tbs[g_ % 2]
                tb3 = tb.rearrange("p (t c) -> p t c", c=512)
                DMA(tb, I['fw%d' % L][g_].rearrange("p t c -> p (t c)"), [], [tbn])
                glo = g_ * 512; ghi = min(glo + 512, 2 * NFp)
                segs = []
                if glo < NFp:
                    segs.append((0, glo, min(ghi, NFp)))
                if ghi > NFp:
                    segs.append((1, max(glo, NFp), ghi))
                for cc in range(2):
                    pt = ps[(g_ * 2 + cc) % 4]; pn = psn[(g_ * 2 + cc) % 4]
                    k_, kn_ = ko[cc]
                    for (ri, a_, b_) in segs:
                        lo = a_ - glo; hi = b_ - glo
                        for t in range(NT):
                            MM(pt[:, lo:hi], ksd4[:, ri, t, cc * 128:(cc + 1) * 128], tb3[:, t, lo:hi], t == 0, t == NT - 1, [ksdn, tbn], [pn])
                        CP('act', k_[:, lo:hi], pt[:, lo:hi], [pn], [kn_])
                        DMA(ksp_d[L][ri, cc * 128:(cc + 1) * 128, a_ - ri * NFp:b_ - ri * NFp], k_[:, lo:hi], [kn_], ['ksp%d' % L])
            P.barrier()
            A.release(m0)

        def run_seq(l, b, is_ctx):
            last = (l == DEPTH - 1)
            L = LC if is_ctx else S
            NT = L // 128; CH = min(512, L); NQ = L // CH
            v = 2 if is_ctx else b
            if is_ctx:
                src = I['ctx'][b] if l == 0 else xcs_d[b]
                dst = xcs_d[b]
            else:
                src = I['x'][b] if l == 0 else xs_d[b]
                dst = out[b] if last else xs_d[b]
            ctx_full = is_ctx and not last
            yc_d = ycat_d[L]
            mseq = A.mark()
            if dbg and dbg[0] == ('ycat', l, b, is_ctx):
                mz_ = A.mark()
                zt, ztn_ = A.alloc(L, BF16)
                MS('pool', zt, 0.0, [ztn_])
                for ch_ in range(8):
                    DMA(yc_d[ch_], zt, [ztn_], ['ycat_d'])
                A.release(mz_)
            hT, hTn = A.alloc(8 * L, BF16)
            hT3 = hT.rearrange("p (k t) -> p k t", k=8)
            wst = [A.alloc(1024) for _ in range(2)]

            def s1():
                m0 = A.mark()
                xb = [A.alloc(1024) for _ in range(2)]
                for t in range(NT):
                    xt, xn = xb[t % 2]
                    DMA(xt, src[t * 128:(t + 1) * 128, :], [], [xn])
                    for half in range(2):
                        bi = (2 * t + half) % 4
                        for kk_ in range(4):
                            k = half * 4 + kk_
                            TR(ps[bi][:, kk_ * 128:(kk_ + 1) * 128], xt[:, k * 128:(k + 1) * 128], 128, [xn], [psn[bi]])
                        for kk_ in range(4):
                            k = half * 4 + kk_
                            ACT(hT3[:, k, t * 128:(t + 1) * 128], ps[bi][:, kk_ * 128:(kk_ + 1) * 128], AF.Identity, [psn[bi], 'modT', 'modT1'], [hTn],
                                scale=mT14[:, l, 8 + k, v:v + 1], bias=mT4[:, l, k, v:v + 1])
                P.barrier()
                A.release(m0)
            s1()

            def conformer():
                m0 = A.mark()
                wc, wcn = A.alloc(8 * 512, BF16)
                wc3 = wc.rearrange("p (k c) -> p k c", k=8)
                load_w(wc3, wcn, I['w_in'][l][:, 0:512], 512, wst)
                cw3 = cw[:].rearrange("p (c k) -> p c k", c=2); cv33 = cv3[:].rearrange("p (c k) -> p c k", c=2)
                sg = [A.alloc(CH) for _ in range(2)]
                tm = [A.alloc(CH) for _ in range(4)]
                for cc in range(2):
                    m1 = A.mark()
                    up, upn = A.alloc(L + 30); acc, an = A.alloc(L); yo, yon = A.alloc(L, BF16)
                    MS('pool', up[:, 0:15], 0.0, [upn]); MS('pool', up[:, L + 15:L + 30], 0.0, [upn])
                    for q in range(NQ):
                        cs = slice(q * CH, (q + 1) * CH)
                        pa = ps[q % 2]; pg = ps[2 + q % 2]
                        for k in range(8):
                            MM(pa[:, 0:CH], wc3[:, k, cc * 128:(cc + 1) * 128], hT3[:, k, cs], k == 0, k == 7, [wcn, hTn], [psn[q % 2]])
                        for k in range(8):
                            MM(pg[:, 0:CH], wc3[:, k, 256 + cc * 128:256 + (cc + 1) * 128], hT3[:, k, cs], k == 0, k == 7, [wcn, hTn], [psn[2 + q % 2]])
                        s_, sn_ = sg[q % 2]
                        ACT(s_, pg[:, 0:CH], AF.Sigmoid, [psn[2 + q % 2]], [sn_])
                        TT('dve', up[:, 15 + q * CH:15 + (q + 1) * CH], pa[:, 0:CH], s_, ALU.mult, [psn[q % 2], sn_], [upn])
                    TS('dve', acc, up[:, 0:L], cw3[:, cc, 0:1], cv33[:, cc, 0:1], ALU.mult, ALU.add, [upn, 'cw', 'cv3'], [an])
                    for k in range(1, 31):
                        STT(acc, up[:, k:k + L], cw3[:, cc, k:k + 1], acc, ALU.mult, ALU.add, [upn, 'cw', an], [an])
                    for q in range(NQ):
                        cs = slice(q * CH, (q + 1) * CH)
                        pm = ps[4 + q % 2]; pv = ps[6 + q % 2]
                        cen, cn_ = tm[0]; sq_, sqn = tm[1]; sd_, sdn = tm[2]; un_, unn = tm[3]
                        MM(pm[:, 0:CH], blk[:], acc[:, cs], True, True, ['blk', an], [psn[4 + q % 2]])
                        TT('dve', cen, acc[:, cs], pm[:, 0:CH], ALU.subtract, [an, psn[4 + q % 2]], [cn_])
                        ACT(sq_, cen, AF.Square, [cn_], [sqn])
                        MM(pv[:, 0:CH], blk[:], sq_, True, True, ['blk', sqn], [psn[6 + q % 2]])
                        ACT(sd_, pv[:, 0:CH], AF.Sqrt, [psn[6 + q % 2]], [sdn], bias=EPS)
                        RCP(sd_, sd_, [sdn], [sdn])
                        TT('dve', un_, cen, sd_, ALU.mult, [cn_, sdn], [unn])
                        ACT(yo[:, cs], un_, AF.Silu, [unn, 'cv3'], [yon], scale=cv33[:, cc, 1:2], bias=cv33[:, cc, 2:3])
                    DMA(yc_d[cc], yo, [yon], ['ycat_d'])
                    A.release(m1)
                P.barrier()
                A.release(m0)

            def attention():
                m0 = A.mark()
                wa, wan = A.alloc(8 * 768, BF16)
                wa3 = wa.rearrange("p (k c) -> p k c", k=8)
                load_w(wa3, wan, I['w_in'][l][:, 512:1280], 768, wst)
                vc14 = vc1[:].rearrange("p (t h e) -> p t h e", t=2, h=4)
                kcT3 = kcT[:].rearrange("p (h t) -> p h t", h=4)
                if is_ctx:
                    V14 = vc14; Vn = 'vc1'
                else:
                    V1, Vn = A.alloc(NT * 4 * 65, BF16)
                    V14 = V1.rearrange("p (t h e) -> p t h e", t=NT, h=4)
                MS('pool', V14, 1.0, [Vn])
                for t in range(NT):
                    pv = ps[t % 2]
                    for k in range(8):
                        MM(pv[:, 0:256], hT3[:, k, t * 128:(t + 1) * 128], wa3[:, k, 512:768], k == 0, k == 7, [hTn, wan], [psn[t % 2]])
                    CP('act', V14[:, t, :, 0:64], pv[:, 0:256].rearrange("p (h e) -> p h e", h=4), [psn[t % 2]], [Vn])
                CK('attn_V')
                need_y = (not is_ctx) or ctx_full
                if not is_ctx:
                    rps = [A.alloc(2 * CH, parts=64) for _ in range(2)]
                    mk, mkn = A.alloc(9 * 896, BF16)
                    mk3 = mk.rearrange("p (i c) -> p i c", i=9)
                    DMA(mk3, I['namask'], [], [mkn])
                    PT, PTn = A.alloc(NT * 896, BF16)
                    def keytiles(j):
                        lo = min(max(2 * j - 4, 0), 24); hi = min(max(2 * j + 1 - 4, 0), 24) + 7
                        return list(range(lo // 2, hi // 2 + 1))
                    mcls = lambda i: i if i < 4 else (4 if i <= 11 else i - 7)
                    PT3 = PT.rearrange("p (i c) -> p i c", i=NT)
                    qrt, qrn = A.alloc(L, BF16, parts=64); krt, krn = A.alloc(L, BF16, parts=64)
                    t1s = [A.alloc(CH, parts=64) for _ in range(2)]; t2s = [A.alloc(CH, parts=64) for _ in range(2)]
                    et = [A.alloc(384) for _ in range(2)]; et2 = [A.alloc(384) for _ in range(2)]
                if need_y:
                    yb, ybn = A.alloc(NT * 256)
                    yb3 = yb.rearrange("p (t c) -> p t c", c=256)
                    qpl, qpn = A.alloc(L, BF16, parts=64)
                    PcT, PcTn = A.alloc(2 * L, BF16)
                    PcT3 = PcT.rearrange("p (c t) -> p c t", c=2)
                    rz = [A.alloc(1) for _ in range(2)]
                bK3 = biasK[:].rearrange("p (h c) -> p h c", h=4)
                for h in range(4):
                    for q in range(NQ):
                        cs = slice(q * CH, (q + 1) * CH)
                        if need_y:
                            for k in range(8):
                                MM(ps[2][0:64, 0:CH], wa3[:, k, h * 64:(h + 1) * 64], hT3[:, k, cs], k == 0, k == 7, [wan, hTn], [psn[2]])
                            CP('act', qpl[:, cs], ps[2][0:64, 0:CH], [psn[2]], [qpn])
                        if not is_ctx:
                            rp, rpn = rps[q % 2]
                            rp3 = rp.rearrange("p (a t) -> p a t", a=2)
                            DMA(rp3[:, 0, :], I['rope'][0][:, cs], [], [rpn]); DMA(rp3[:, 1, :], I['rope'][1][:, cs], [], [rpn])
                            if plan.get('fine'): CK('f_q')
                            for k in range(8):
                                MM(ps[3][0:32, 0:CH], wa3[:, k, h * 64 + 32:h * 64 + 64], hT3[:, k, cs], k == 0, k == 7, [wan, hTn], [psn[3]])
                            if plan.get('fine'): CK('f_sw1')
                            for k in range(8):
                                MM(ps[3][32:64, 0:CH], wa3[:, k, h * 64:h * 64 + 32], hT3[:, k, cs], k == 0, k == 7, [wan, hTn], [psn[3]])
                            if plan.get('fine'): CK('f_sw2')
                            t1, t1n = t1s[0]; t2, t2n = t2s[0]
                            TT('dve', t1, ps[2][0:64, 0:CH], rp3[:, 0, :], ALU.mult, [psn[2], rpn], [t1n])
                            if plan.get('fine'): CK('f_t1')
                            TT('dve', t2, ps[3][0:64, 0:CH], rp3[:, 1, :], ALU.mult, [psn[3], rpn], [t2n])
                            if plan.get('fine'): CK('f_t2')
                            TT('pool', qrt[:, cs], t1, t2, ALU.add, [t1n, t2n], [qrn])
                            if plan.get('fine'): CK('f_add')
                        for k in range(8):
                            MM(ps[4][0:64, 0:CH], wa3[:, k, 256 + h * 64:256 + (h + 1) * 64], hT3[:, k, cs], k == 0, k == 7, [wan, hTn], [psn[4]])
                        if is_ctx:
                            CP('act', kcT3[:, h, cs], ps[4][0:64, 0:CH], [psn[4]], ['kcT'])
                        else:
                            for k in range(8):
                                MM(ps[5][0:32, 0:CH], wa3[:, k, 256 + h * 64 + 32:256 + h * 64 + 64], hT3[:, k, cs], k == 0, k == 7, [wan, hTn], [psn[5]])
                            for k in range(8):
                                MM(ps[5][32:64, 0:CH], wa3[:, k, 256 + h * 64:256 + h * 64 + 32], hT3[:, k, cs], k == 0, k == 7, [wan, hTn], [psn[5]])
                            t1, t1n = t1s[1]; t2, t2n = t2s[1]
                            TT('dve', t1, ps[4][0:64, 0:CH], rp3[:, 0, :], ALU.mult, [psn[4], rpn], [t1n])
                            TT('dve', t2, ps[5][0:64, 0:CH], rp3[:, 1, :], ALU.mult, [psn[5], rpn], [t2n])
                            TT('pool', krt[:, cs], t1, t2, ALU.add, [t1n, t2n], [krn])
                    CK('attn_qk%d' % h)
                    if not need_y:
                        continue
                    if not is_ctx:
                        it = 0
                        for i in range(NT):
                            js = [j for j in range(NT) if i in keytiles(j)]
                            runs = [js[a_:a_ + 3] for a_ in range(0, len(js), 3)]
                            for run in runs:
                                ja, jb = run[0], run[-1]
                                n = (jb - ja + 1) * 128; c0 = (ja - i + 3) * 128
                                bi = 6 + it % 2
                                e1, e1n = et[it % 2]; e2, e2n = et2[it % 2]; it += 1
                                MM(ps[bi][:, 0:n], krt[:, i * 128:(i + 1) * 128], qrt[:, ja * 128:(jb + 1) * 128], True, True, [krn, qrn], [psn[bi]])
                                STT(e1[:, 0:n], ps[bi][:, 0:n], 0.125, bK3[:, h, c0:c0 + n], ALU.mult, ALU.add, [psn[bi], 'biasK'], [e1n])
                                ACT(e2[:, 0:n], e1[:, 0:n], AF.Exp, [e1n], [e2n])
                                TT('pool', PT3[:, i, c0:c0 + n], e2[:, 0:n], mk3[:, mcls(i), c0:c0 + n], ALU.mult, [e2n, mkn], [PTn])
                    for q in range(NQ):
                        cs = slice(q * CH, (q + 1) * CH)
                        for ct in range(2):
                            bi = (q * 2 + ct) % 2
                            MM(ps[bi][:, 0:CH], kcT3[:, h, ct * 128:(ct + 1) * 128], qpl[:, cs], True, True, ['kcT', qpn], [psn[bi]])
                            ACT(PcT3[:, ct, cs], ps[bi][:, 0:CH], AF.Exp, [psn[bi]], [PcTn], scale=0.125)
                    CK('attn_sc%d' % h)
                    for j in range(NT):
                        bi = 2 + j % 2
                        mms = []
                        if not is_ctx:
                            for i in keytiles(j):
                                mms.append((PT3[:, i, (j - i + 3) * 128:(j - i + 4) * 128], V14[:, i, h, :], [PTn, Vn]))
                        for ct in range(2):
                            mms.append((PcT3[:, ct, j * 128:(j + 1) * 128], vc14[:, ct, h, :], [PcTn, 'vc1']))
                        for ii, (lt, rh, rr) in enumerate(mms):
                            MM(ps[bi][:, 0:65], lt, rh, ii == 0, ii == len(mms) - 1, rr, [psn[bi]])
                        r_, rn_ = rz[j % 2]
                        RCP(r_, ps[bi][:, 64:65], [psn[bi]], [rn_])
                        ACT(yb3[:, j, h * 64:(h + 1) * 64], ps[bi][:, 0:64], AF.Identity, [psn[bi], rn_], [ybn], scale=r_)
                CK('attn_pv')
                if need_y:
                    yo, yon = A.alloc(2 * L, BF16)
                    yo3 = yo.rearrange("p (c t) -> p c t", c=2)
                    for j in range(NT):
                        for cc in range(2):
                            bi = 4 + (2 * j + cc) % 4
                            TR(ps[bi][:, 0:128], yb3[:, j, cc * 128:(cc + 1) * 128], 128, [ybn], [psn[bi]])
                            CP('dve' if cc else 'act', yo3[:, cc, j * 128:(j + 1) * 128], ps[bi][:, 0:128], [psn[bi]], [yon])
                    CK('attn_tr')
                    for cc in range(2):
                        DMA(yc_d[2 + cc], yo3[:, cc, :], [yon], ['ycat_d'])
                    CK('attn_out')
                P.barrier()
                A.release(m0)

            def hyena():
                NFp = L + 128; NFT = NFp // 128
                m0 = A.mark()
                wh, whn = A.alloc(8 * 768, BF16)
                wh3 = wh.rearrange("p (k c) -> p k c", k=8)
                load_w(wh3, whn, I['w_in'][l][:, 1280:2048], 768, wst)
                hsp3 = hsp[:].rearrange("p (c k) -> p c k", c=6)
                invv = I['inv%d' % L].rearrange("a f t -> (a f) t")
                TH = min(L, 1024); nbk = TH // CH
                for cc in range(2):
                    m1 = A.mark()
                    xc0 = A.alloc(L); xc2 = A.alloc(L)
                    ms_ = A.mark()
                    xc1 = A.alloc(L)
                    xc = [xc0, xc1, xc2]
                    pad, padn = A.alloc(L + 2)
                    MS('pool', pad[:, 0:1], 0.0, [padn]); MS('pool', pad[:, L + 1:L + 2], 0.0, [padn])
                    for g in range(3):
                        ci = g * 2 + cc
                        for q in range(NQ):
                            pp = ps[q % 2]
                            for k in range(8):
                                MM(pp[:, 0:CH], wh3[:, k, g * 256 + cc * 128:g * 256 + (cc + 1) * 128], hT3[:, k, q * CH:(q + 1) * CH], k == 0, k == 7, [whn, hTn], [psn[q % 2]])
                            CP('act', pad[:, 1 + q * CH:1 + (q + 1) * CH], pp[:, 0:CH], [psn[q % 2]], [padn])
                        x_, xn_ = xc[g]
                        TS('dve', x_, pad[:, 0:L], hsp3[:, ci, 0:1], hsp3[:, ci, 3:4], ALU.mult, ALU.add, [padn, 'hsp'], [xn_])
                        STT(x_, pad[:, 1:L + 1], hsp3[:, ci, 1:2], x_, ALU.mult, ALU.add, [padn, 'hsp', xn_], [xn_])
                        STT(x_, pad[:, 2:L + 2], hsp3[:, ci, 2:3], x_, ALU.mult, ALU.add, [padn, 'hsp', xn_], [xn_])
                    u, un = xc[2]
                    TT('dve', u, xc[2][0], xc[1][0], ALU.mult, [xc[2][1], xc[1][1]], [un])
                    A.release(ms_)
                    Utm, Utn = A.alloc(NT * 128, BF16)
                    Ut3 = Utm.rearrange("p (t c) -> p t c", c=128)
                    for t in range(NT):
                        bi = 2 + t % 2
                        TR(ps[bi][:, 0:128], u[:, t * 128:(t + 1) * 128], 128, [un], [psn[bi]])
                        CP('act' if t % 2 else 'dve', Ut3[:, t, :], ps[bi][:, 0:128], [psn[bi]], [Utn])
                    Uf, Ufn = A.alloc(2 * NFp)
                    m2 = A.mark()
                    tbs = [A.alloc(NT * 512, BF16) for _ in range(2)]
                    it = 0
                    for c0 in range(0, 2 * NFp, 512):
                        n = min(512, 2 * NFp - c0)
                        tb, tbn = tbs[it % 2]
                        tb3 = tb.rearrange("p (t c) -> p t c", c=512)
                        DMA(tb, I['fw%d' % L][c0 // 512].rearrange("p t c -> p (t c)"), [], [tbn])
                        bi = 4 + it % 2; it += 1
                        for t in range(NT):
                            MM(ps[bi][:, 0:n], Ut3[:, t, :], tb3[:, t, 0:n], t == 0, t == NT - 1, [Utn, tbn], [psn[bi]])
                        CP('act', Uf[:, c0:c0 + n], ps[bi][:, 0:n], [psn[bi]], [Ufn])
                    A.release(m2)
                    Kf, Kfn = A.alloc(2 * NFp)
                    DMA(Kf[:, 0:NFp], ksp_d[L][0, cc * 128:(cc + 1) * 128, :], ['ksp%d' % L], [Kfn])
                    DMA(Kf[:, NFp:2 * NFp], ksp_d[L][1, cc * 128:(cc + 1) * 128, :], ['ksp%d' % L], [Kfn])
                    Yf, Yfn = A.alloc(2 * NFp)
                    ta, tan = A.alloc(NFp); tb_, tbn_ = A.alloc(NFp)
                    Uc = Uf[:, 0:NFp]; Us = Uf[:, NFp:2 * NFp]; Kr = Kf[:, 0:NFp]; Ki = Kf[:, NFp:2 * NFp]
                    TT('dve', ta, Uc, Kr, ALU.mult, [Ufn, Kfn], [tan]); TT('pool', tb_, Us, Ki, ALU.mult, [Ufn, Kfn], [tbn_])
                    TT('dve', Yf[:, 0:NFp], ta, tb_, ALU.add, [tan, tbn_], [Yfn])
                    TT('dve', ta, Uc, Ki, ALU.mult, [Ufn, Kfn], [tan]); TT('pool', tb_, Us, Kr, ALU.mult, [Ufn, Kfn], [tbn_])
                    TT('dve', Yf[:, NFp:2 * NFp], ta, tb_, ALU.subtract, [tan, tbn_], [Yfn])
                    YT, YTn = A.alloc(2 * NFT * 128, BF16)
                    YT3 = YT.rearrange("p (f c) -> p f c", c=128)
                    for ft in range(2 * NFT):
                        bi = 2 + ft % 2
                        TR(ps[bi][:, 0:128], Yf[:, ft * 128:(ft + 1) * 128], 128, [Yfn], [psn[bi]])
                        CP('act' if ft % 2 else 'dve', YT3[:, ft, :], ps[bi][:, 0:128], [psn[bi]], [YTn])
                    ibs = [A.alloc(TH, BF16) for _ in range(4)]
                    yo, yon = A.alloc(L, BF16)
                    tq = [A.alloc(CH) for _ in range(2)]
                    for th in range(L // TH):
                        for ft in range(2 * NFT):
                            ib, ibn = ibs[ft % 4]
                            DMA(ib, invv[ft * 128:(ft + 1) * 128, th * TH:(th + 1) * TH], [], [ibn])
                            for bq in range(nbk):
                                MM(ps[4 + bq][:, 0:CH], YT3[:, ft, :], ib[:, bq * CH:(bq + 1) * CH], ft == 0, ft == 2 * NFT - 1, [YTn, ibn], [psn[4 + bq]])
                        for bq in range(nbk):
                            cs = slice(th * TH + bq * CH, th * TH + (bq + 1) * CH)
                            t_, tn_ = tq[bq % 2]
                            STT(t_, u[:, cs], hskip[:, cc:cc + 1], ps[4 + bq][:, 0:CH], ALU.mult, ALU.add, [un, 'hskip', psn[4 + bq]], [tn_])
                            TT('dve', yo[:, cs], t_, xc[0][0][:, cs], ALU.mult, [tn_, xc[0][1]], [yon])
                    DMA(yc_d[4 + cc], yo, [yon], ['ycat_d'])
                    P.barrier()
                    A.release(m1)
                A.release(m0)

            def ssd():
                want_y = (not is_ctx) or ctx_full
                m0 = A.mark()
                wz, wzn = A.alloc(8 * 264, BF16); wx, wxn = A.alloc(8 * 512, BF16)
                wz3 = wz.rearrange("p (k c) -> p k c", k=8); wx3 = wx.rearrange("p (k c) -> p k c", k=8)
                load_w(wz3, wzn, I['w_in'][l][:, 2048:2304], 256, wst)
                for k in range(8):
                    st, sn = wst[k % 2]
                    DMA(st[:, 0:8], I['w_in'][l][k * 128:(k + 1) * 128, 2816:2824], [], [sn])
                    CP('pool', wz3[:, k, 256:264], st[:, 0:8], [sn], [wzn])
                load_w(wx3, wxn, I['w_in'][l][:, 2304:2816], 512, wst)
                ztm, ztn = A.alloc(NT * 256); dta, dtn = A.alloc(NT * 8); aal, aan = A.alloc(NT * 8)
                z3 = ztm.rearrange("p (t c) -> p t c", c=256); dt3 = dta.rearrange("p (t c) -> p t c", c=8); a3 = aal.rearrange("p (t c) -> p t c", c=8)
                for t in range(NT):
                    pz = ps[t % 2]
                    for k in range(8):
                        MM(pz[:, 0:264], hT3[:, k, t * 128:(t + 1) * 128], wz3[:, k, :], k == 0, k == 7, [hTn, wzn], [psn[t % 2]])
                    CP('act', z3[:, t, :], pz[:, 0:256], [psn[t % 2]], [ztn])
                    TT('dve', dt3[:, t, :], pz[:, 256:264], dtb_bc[:], ALU.add, [psn[t % 2], 'dtb_bc'], [dtn])
                ACT(dta, dta, AF.Exp, [dtn], [dtn])
                ACT(dta, dta, AF.Ln, [dtn], [dtn], bias=1.0)
                TT('dve', a3, dt3, A_bc[:].unsqueeze(1).to_broadcast([128, NT, 8]), ALU.mult, [dtn, 'A_bc'], [aan])
                xtm, xtn = A.alloc(NT * 256); x3 = xtm.rearrange("p (t c) -> p t c", c=256)
                Btm, Btn = A.alloc(NT * 128, BF16); B3 = Btm.rearrange("p (t c) -> p t c", c=128)
                BTb, BTn = A.alloc(2 * L, BF16, parts=64); CTb, CTn = A.alloc(2 * L, BF16, parts=64)
                BT3 = BTb.rearrange("p (g t) -> p g t", g=2); CT3 = CTb.rearrange("p (g t) -> p g t", g=2)
                sxp3 = sxp[:].rearrange("p (c k) -> p c k", c=2); sbp3 = sbp[:].rearrange("p (c k) -> p c k", c=4)
                m1 = A.mark()
                pad, padn = A.alloc(L + 2); cv, cvn = A.alloc(L); sx, sxn = A.alloc(L)
                MS('pool', pad[:, 0:1], 0.0, [padn]); MS('pool', pad[:, L + 1:L + 2], 0.0, [padn])
                chunks = [('x', 0, 128, 0), ('x', 1, 128, 128), ('B', 0, 64, 256), ('B', 1, 64, 320), ('C', 0, 64, 384), ('C', 1, 64, 448)]
                for (kind, idx, Pn, col0) in chunks:
                    for q in range(NQ):
                        pp = ps[2 + q % 2]
                        for k in range(8):
                            MM(pp[0:Pn, 0:CH], wx3[:, k, col0:col0 + Pn], hT3[:, k, q * CH:(q + 1) * CH], k == 0, k == 7, [wxn, hTn], [psn[2 + q % 2]])
                        CP('act', pad[0:Pn, 1 + q * CH:1 + (q + 1) * CH], pp[0:Pn, 0:CH], [psn[2 + q % 2]], [padn])
                    if kind == 'x':
                        wv = sxp3[:, idx, :]; wname = 'sxp'
                    else:
                        wv = sbp3[:, (0 if kind == 'B' else 2) + idx, :]; wname = 'sbp'
                    TS('dve', cv[0:Pn, :], pad[0:Pn, 0:L], wv[0:Pn, 0:1], wv[0:Pn, 3:4], ALU.mult, ALU.add, [padn, wname], [cvn])
                    STT(cv[0:Pn, :], pad[0:Pn, 1:L + 1], wv[0:Pn, 1:2], cv[0:Pn, :], ALU.mult, ALU.add, [padn, wname, cvn], [cvn])
                    STT(cv[0:Pn, :], pad[0:Pn, 2:L + 2], wv[0:Pn, 2:3], cv[0:Pn, :], ALU.mult, ALU.add, [padn, wname, cvn], [cvn])
                    if kind == 'C':
                        ACT(CT3[:, idx, :], cv[0:64, :], AF.Silu, [cvn], [CTn])
                        continue
                    ACT(sx[0:Pn, :], cv[0:Pn, :], AF.Silu, [cvn], [sxn])
                    if kind == 'B':
                        CP('pool', BT3[:, idx, :], sx[0:64, :], [sxn], [BTn])
                    for t in range(NT):
                        bi = 4 + t % 2
                        TR(ps[bi][:, 0:Pn], sx[0:Pn, t * 128:(t + 1) * 128], Pn, [sxn], [psn[bi]])
                        if kind == 'x':
                            CP('act' if t % 2 else 'dve', x3[:, t, idx * 128:(idx + 1) * 128], ps[bi][:, 0:128], [psn[bi]], [xtn])
                        else:
                            CP('act' if t % 2 else 'dve', B3[:, t, idx * 64:(idx + 1) * 64], ps[bi][:, 0:64], [psn[bi]], [Btn])
                P.barrier()
                A.release(m1)
                xd, xdn = A.alloc(NT * 512, BF16)
                xd5 = xd.rearrange("p (t d h e) -> p t d h e", d=2, h=4, e=64)
                x4 = xtm.rearrange("p (t h e) -> p t h e", h=4, e=64)
                for d in range(2):
                    TT('dve', xd5[:, :, d], x4, dt3[:, :, d * 4:(d + 1) * 4].unsqueeze(3).to_broadcast([128, NT, 4, 64]), ALU.mult, [xtn, dtn], [xdn])
                if want_y:
                    yal, yaln = A.alloc(NT * 256)
                    y4 = yal.rearrange("p (t h e) -> p t h e", h=4, e=64)
                    for t in range(NT):
                        TT('pool', y4[:, t], x4[:, t], dsk_bc[:].unsqueeze(2).to_broadcast([128, 4, 64]), ALU.mult, [xtn, 'dsk_bc'], [yaln])
                S3 = Sst[:].rearrange("p (i e) -> p i e", i=8); SB3 = SstB[:].rearrange("p (i e) -> p i e", i=8)
                if is_ctx:
                    MS('dve', Sst[:], 0.0, ['Sst%d' % i for i in range(8)])
                    MS('dve', SstB[:], 0.0, ['SstB%d' % i for i in range(8)])
                cum = [A.alloc(24) for _ in range(2)]
                GTs = [A.alloc(128) for _ in range(2)]
                abcs = [A.alloc(128) for _ in range(2)]; Lts = [A.alloc(128) for _ in range(2)]
                MTs = [A.alloc(128, BF16) for _ in range(2)]
                yds = [A.alloc(64) for _ in range(2)]; yd2s = [A.alloc(64) for _ in range(2)]
                xdds = [A.alloc(64, BF16) for _ in range(2)]
                ssdc3_ = ssdc[:].rearrange("p (a b) -> p a b", a=4)
                it = 0
                for d in range(2):
                    order = list(range(NT)) if d == 0 else list(range(NT - 1, -1, -1))
                    U = ssdc3_[:, d, :]; MSK = ssdc3_[:, 2 + d, :]
                    for c in order:
                        tok = slice(c * 128, (c + 1) * 128)
                        cm, cmn = cum[it % 2]
                        MM(ps[0][:, 0:4], U, a3[:, c, d * 4:(d + 1) * 4], True, True, ['ssdc', aan], [psn[0]])
                        MM(ps[0][:, 8:12], ones[:], a3[:, c, d * 4:(d + 1) * 4], True, True, ['ones', aan], [psn[0]])
                        CP('dve', cm[:, 0:4], ps[0][:, 0:4], [psn[0]], [cmn])
                        CP('dve', cm[:, 4:8], ps[0][:, 8:12], [psn[0]], [cmn])
                        TS('dve', cm[:, 8:12], cm[:, 0:4], -1.0, None, ALU.mult, None, [cmn], [cmn])
                        TT('dve', cm[:, 16:20], cm[:, 4:8], cm[:, 0:4], ALU.subtract, [cmn], [cmn])
                        ACT(cm[:, 12:16], cm[:, 0:4], AF.Exp, [cmn], [cmn])
                        ACT(cm[:, 16:20], cm[:, 16:20], AF.Exp, [cmn], [cmn])
                        ACT(cm[:, 20:24], cm[:, 4:8], AF.Exp, [cmn], [cmn])
                        for g in range(2):
                            GT, GTn = GTs[g]
                            MM(ps[1][:, 0:128], BT3[:, g, tok], CT3[:, g, tok], True, True, [BTn, CTn], [psn[1]])
                            CP('act', GT, ps[1][:, 0:128], [psn[1]], [GTn])
                            for hh in range(2):
                                h = g * 2 + hh; si = d * 4 + h
                                abc, abcn = abcs[hh]; Lt, Ltn = Lts[hh]; MT, MTn = MTs[hh]
                                yd, ydn = yds[hh]; yd2, yd2n = yd2s[hh]; xdd, xddn = xdds[hh]
                                bR = 2 + hh; bY = 4 + hh; bS = 6 + hh
                                TS('pool', abc, ones[:], a3[:, c, si:si + 1], None, ALU.mult, None, ['ones', aan], [abcn])
                                MM(ps[bR][:, 0:128], abc, U, True, False, [abcn, 'ssdc'], [psn[bR]])
                                MM(ps[bR][:, 0:128], ident[:], MSK, False, True, ['ident', 'ssdc'], [psn[bR]])
                                if d == 0:
                                    ACT(Lt, ps[bR][:, 0:128], AF.Exp, [psn[bR], cmn], [Ltn], bias=cm[:, 8 + h:9 + h], scale=1.0)
                                else:
                                    ACT(Lt, ps[bR][:, 0:128], AF.Exp, [psn[bR], cmn], [Ltn], bias=cm[:, h:h + 1], scale=-1.0)
                                if want_y:
                                    TT('dve', MT, GT, Lt, ALU.mult, [GTn, Ltn], [MTn])
                                    MM(ps[bY][:, 0:64], MT, xd5[:, c, d, h, :], True, True, [MTn, xdn], [psn[bY]])
                                    MM(ps[bY][:, 64:128], CT3[:, g, tok], SB3[:, si, :], True, True, [CTn, 'SstB%d' % si], [psn[bY]])
                                    CP('act', yd, ps[bY][:, 0:64], [psn[bY]], [ydn])
                                    ecol = cm[:, 12 + h:13 + h] if d == 0 else cm[:, 16 + h:17 + h]
                                    STT(yd2, ps[bY][:, 64:128], ecol, yd, ALU.mult, ALU.add, [psn[bY], cmn, ydn], [yd2n])
                                    TT('pool', y4[:, c, h, :], y4[:, c, h, :], yd2, ALU.add, [yaln, yd2n], [yaln])
                                dcol = cm[:, 16 + h:17 + h] if d == 0 else cm[:, 12 + h:13 + h]
                                TS('pool', xdd, xd5[:, c, d, h, :], dcol, None, ALU.mult, None, [xdn, cmn], [xddn])
                                MM(ps[bS][0:64, 0:64], B3[:, c, g * 64:(g + 1) * 64], xdd, True, True, [Btn, xddn], [psn[bS]])
                                STT(S3[:, si, :], S3[:, si, :], cm[0:64, 20 + h:21 + h], ps[bS][0:64, 0:64], ALU.mult, ALU.add,
                                    ['Sst%d' % si, cmn, psn[bS]], ['Sst%d' % si])
                                CP('act', SB3[:, si, :], S3[:, si, :], ['Sst%d' % si], ['SstB%d' % si])
                        it += 1
                if dbg and dbg[0] == 'sst' and is_ctx:
                    DMA(dbg_out, Sst[:], ['Sst%d' % i for i in range(8)], ['dbg'], q='sp')
                if want_y:
                    szt, szn = A.alloc(NT * 256)
                    ACT(szt, ztm, AF.Silu, [ztn], [szn])
                    TT('dve', yal, yal, szt, ALU.mult, [yaln, szn], [yaln])
                    ACT(szt, yal, AF.Square, [yaln], [szn])
                    ssq, ssqn = A.alloc(NT * 2)
                    P.op('dve', lambda e: e.reduce_sum(out=ssq, in_=szt.rearrange("p (a c) -> p a c", c=128), axis=AX.X), r=[szn], w=[ssqn])
                    ACT(ssq, ssq, AF.Sqrt, [ssqn], [ssqn], bias=EPS, scale=1.0 / 128)
                    RCP(ssq, ssq, [ssqn], [ssqn])
                    TT('dve', yal.rearrange("p (a c) -> p a c", c=128), yal.rearrange("p (a c) -> p a c", c=128),
                       ssq.unsqueeze(2).to_broadcast([128, NT * 2, 128]), ALU.mult, [yaln, ssqn], [yaln])
                    y3 = yal.rearrange("p (t c) -> p t c", c=256)
                    TT('dve', y3, y3, ng_bc[:].unsqueeze(1).to_broadcast([128, NT, 256]), ALU.mult, [yaln, 'ng_bc'], [yaln])
                    yo, yon = A.alloc(2 * L, BF16)
                    yo3 = yo.rearrange("p (c t) -> p c t", c=2)
                    for t in range(NT):
                        for cc in range(2):
                            bi = 4 + (2 * t + cc) % 4
                            TR(ps[bi][:, 0:128], y3[:, t, cc * 128:(cc + 1) * 128], 128, [yaln], [psn[bi]])
                            CP('dve' if cc else 'act', yo3[:, cc, t * 128:(t + 1) * 128], ps[bi][:, 0:128], [psn[bi]], [yon])
                    for cc in range(2):
                        DMA(yc_d[6 + cc], yo3[:, cc, :], [yon], ['ycat_d'])
                P.barrier()
                A.release(m0)

            if 'conf' in mixers and (not is_ctx or ctx_full):
                conformer()
            if 'attn' in mixers:
                attention()
            if 'hy' in mixers and (not is_ctx or ctx_full):
                hyena()
            if 'ssd' in mixers:
                ssd()
            P.barrier()
            A.release(mseq)
            if dbg and dbg[0] == ('ycat', l, b, is_ctx):
                DMA(dbg_out, yc_d, ['ycat_d'], ['dbg'], q='sp')
                P.barrier()
            if do_tail and (not is_ctx or ctx_full):
                tail(l, b, is_ctx, src, dst, L, v)

        def tail(l, b, is_ctx, src, dst, L, v):
            NT = L // 128
            yc_d = ycat_d[L]
            m0 = A.mark()
            wo, won = A.alloc(8 * 1024, BF16); wq, wqn = A.alloc(8 * 2048, BF16)
            wo3 = wo.rearrange("p (k c) -> p k c", k=8); wq3 = wq.rearrange("p (k c) -> p k c", k=8)
            mw = A.mark()
            wst = [A.alloc(1024) for _ in range(2)]
            load_w(wo3, won, I['w_out'][l], 1024, wst)
            load_w(wq3[:, :, 0:1024], wqn, I['peer_wq'][l][:, 0:1024], 1024, wst)
            load_w(wq3[:, :, 1024:2048], wqn, I['peer_wq'][l][:, 1024:2048], 1024, wst)
            A.release(mw)
            rows = {}
            for nm, srcrow in (('ln1g', I['ln1_g'][l:l + 1, :]), ('ln1b', I['ln1_b'][l:l + 1, :]), ('ln2g', I['ln2_g'][l:l + 1, :]), ('ln2b', I['ln2_b'][l:l + 1, :]),
                               ('g1', mod_d[l, v:v + 1, 2 * D:3 * D]), ('sh2', mod_d[l, v:v + 1, 3 * D:4 * D]),
                               ('sc2', mod_d[l, v:v + 1, 4 * D:5 * D]), ('g2', mod_d[l, v:v + 1, 5 * D:6 * D])):
                t_, n_ = A.alloc(1024)
                DMA(t_, srcrow.partition_broadcast(128), ['mod_d'], [n_])
                rows[nm] = (t_, n_)
            TS('dve', rows['sc2'][0], rows['sc2'][0], 1.0, None, ALU.add, None, [rows['sc2'][1]], [rows['sc2'][1]])
            CK('t_rows')
            keysT, kTn = A.alloc(16 * 128)
            kT3 = keysT.rearrange("p (c n) -> p c n", c=16)
            mk_ = A.mark()
            kst = [A.alloc(128) for _ in range(2)]
            for c in range(16):
                st, sn = kst[c % 2]
                DMA(st, I['peer_keys'][l, c // 2, c % 2], [], [sn])
                TR(ps[c % 2][:, 0:128], st, 128, [sn], [psn[c % 2]])
                CP('dve', kT3[:, c, :], ps[c % 2][:, 0:128], [psn[c % 2]], [kTn])
            A.release(mk_)
            ycb = [A.alloc(8 * 128, BF16) for _ in range(2)]
            xb = [A.alloc(1024) for _ in range(2)]
            tt_, ttn = A.alloc(1024); r1, r1n = A.alloc(1024); x1, x1n = A.alloc(1024); h2, h2n = A.alloc(1024)
            junk, jn = tt_, ttn
            st8, st8n = A.alloc(8)
            h2T, h2Tn = A.alloc(8 * 128, BF16); h2T3 = h2T.rearrange("p (k t) -> p k t", k=8)
            qT, qTn = A.alloc(16 * 128); qT3 = qT.rearrange("p (c t) -> p c t", c=16)
            sc, scn = A.alloc(16 * 128); sc_b, sc_bn = A.alloc(16 * 128)
            wk, wkn = A.alloc(256)
            m16, m16n = A.alloc(256); i16, i16n = A.alloc(256, U32); i16f, i16fn = A.alloc(256)
            m163 = m16.rearrange("p (c k) -> p c k", c=16); i163 = i16.rearrange("p (c k) -> p c k", c=16)
            cand, candn = A.alloc(2048); eid, eidn = A.alloc(2560)
            cv, cvn = A.alloc(128); ex, exn = A.alloc(128); gz, gzn = A.alloc(16)
            ci_, cin = A.alloc(128, U32); cif, cifn = A.alloc(128)
            iot, iotn = A.alloc(256)
            DMA(iot, I['iota256'].partition_broadcast(128), [], [iotn])
            esel, eseln = A.alloc(128); eseli, eselin = A.alloc(128, I32)
            if not DENSE:
                av, avn = A.alloc(128); wv, wvn = A.alloc(128)
                gb = [A.alloc(1024) for _ in range(3)]
                acc, accn = A.alloc(1024)
            else:
                sm, smn = A.alloc(3 * 128); sm3 = sm.rearrange("p (a t) -> p a t", a=3)
                i1i, i1in = A.alloc(128, I32); i1f, i1fn = A.alloc(128); i2f, i2fn = A.alloc(128); etmp, etn = A.alloc(128)
                af_, afn = A.alloc(128); bf_, bfn = A.alloc(128)

            def layernorm(xin, xinn, gname, bname, xout, xoutn):
                TS('dve', st8[:, 1:2], st8[:, 0:1], -1.0 / D, None, ALU.mult, None, [st8n], [st8n])
                ACT(junk, xin, AF.Square, [xinn, st8n], [jn, st8n], bias=st8[:, 1:2], accum=st8[:, 2:3])
                ACT(st8[:, 3:4], st8[:, 2:3], AF.Sqrt, [st8n], [st8n], bias=EPS, scale=1.0 / D)
                RCP(st8[:, 4:5], st8[:, 3:4], [st8n], [st8n])
                TS('dve', xout, xin, st8[:, 1:2], st8[:, 4:5], ALU.add, ALU.mult, [xinn, st8n], [xoutn])
                TT('dve', xout, xout, rows[gname][0], ALU.mult, [xoutn, rows[gname][1]], [xoutn])
                TT('dve', xout, xout, rows[bname][0], ALU.add, [xoutn, rows[bname][1]], [xoutn])

            scs = [(sc, scn), (sc_b, sc_bn)]
            def front(t):
                sc3 = scs[t % 2][0].rearrange("p (c n) -> p c n", c=16); scn = scs[t % 2][1]
                tok = slice(t * 128, (t + 1) * 128)
                yc, ycn = ycb[t % 2]; yc3 = yc.rearrange("p (k t) -> p k t", k=8)
                xt, xn = xb[t % 2]
                DMA(yc3, yc_d[:, :, tok].rearrange("k p t -> p k t"), ['ycat_d'], [ycn])
                DMA(xt, src[tok, :], [], [xn])
                for nh in range(2):
                    for k in range(8):
                        MM(ps[nh][:, :], yc3[:, k, :], wo3[:, k, nh * 512:(nh + 1) * 512], k == 0, k == 7, [ycn, won], [psn[nh]])
                    TT('dve', tt_[:, nh * 512:(nh + 1) * 512], ps[nh][:, :], rows['g1'][0][:, nh * 512:(nh + 1) * 512], ALU.mult, [psn[nh], rows['g1'][1]], [ttn])
                STT(r1, xt, ALPHA, tt_, ALU.mult, ALU.add, [xn, ttn], [r1n, st8n], accum=st8[:, 0:1])
                layernorm(r1, r1n, 'ln1g', 'ln1b', x1, x1n)
                CK('t_ln1')
                TT('dve', h2, x1, rows['sc2'][0], ALU.mult, [x1n, rows['sc2'][1]], [h2n])
                TT('dve', h2, h2, rows['sh2'][0], ALU.add, [h2n, rows['sh2'][1]], [h2n])
                for half in range(2):
                    bi = 2 + half
                    for kk_ in range(4):
                        k = half * 4 + kk_
                        TR(ps[bi][:, kk_ * 128:(kk_ + 1) * 128], h2[:, k * 128:(k + 1) * 128], 128, [h2n], [psn[bi]])
                    CP('act', h2T3[:, half * 4:(half + 1) * 4, :], ps[bi][:, :].rearrange("p (k t) -> p k t", k=4), [psn[bi]], [h2Tn])
                CK('t_h2T')
                if DENSE:
                    DMA(x1_d[tok, :], x1, [x1n], ['x1_d'], q='sp')
                    DMA(h2T_d[:, :, tok].rearrange("k p t -> p k t"), h2T3, [h2Tn], ['h2T_d'], q='sp')
                for cq in range(4):
                    bi = 4 + cq % 2
                    for ci in range(4):
                        c = cq * 4 + ci
                        for k in range(8):
                            MM(ps[bi][:, ci * 128:(ci + 1) * 128], wq3[:, k, c * 128:(c + 1) * 128], h2T3[:, k, :], k == 0, k == 7, [wqn, h2Tn], [psn[bi]])
                    CP('act', qT3[:, cq * 4:(cq + 1) * 4, :], ps[bi][:, :].rearrange("p (c t) -> p c t", c=4), [psn[bi]], [qTn])
                CK('t_qT')
                for cq in range(4):
                    bi = 6 + cq % 2
                    for ci in range(4):
                        c = cq * 4 + ci
                        MM(ps[bi][:, ci * 128:(ci + 1) * 128], qT3[:, c, :], kT3[:, c, :], True, True, [qTn, kTn], [psn[bi]])
                    CP('act', sc3[:, cq * 4:(cq + 1) * 4, :], ps[bi][:, :].rearrange("p (c n) -> p c n", c=4), [psn[bi]], [scn])
                CK('t_sc')
            def back(t):
                tok = slice(t * 128, (t + 1) * 128)
                sc3 = scs[t % 2][0].rearrange("p (c n) -> p c n", c=16); scn = scs[t % 2][1]
                m16c = ['%s_%d' % (m16n, c) for c in range(16)]; i16c = ['%s_%d' % (i16n, c) for c in range(16)]
                wks = [(wk[:, 0:128], wkn + 'a'), (wk[:, 128:256], wkn + 'b')]
                for c0 in range(0, 16, 2):
                    pr = (c0, c0 + 1)
                    for c in pr:
                        P.op('dve', lambda e, c=c: e.max(out=m163[:, c, 0:8], in_=sc3[:, c, :]), r=[scn], w=[m16c[c]])
                    for c in pr:
                        P.op('dve', lambda e, c=c: e.max_index(out=i163[:, c, 0:8], in_max=m163[:, c, 0:8], in_values=sc3[:, c, :]), r=[scn, m16c[c]], w=[i16c[c]])
                    for j_, c in enumerate(pr):
                        P.op('dve', lambda e, c=c, j_=j_: e.match_replace(out=wks[j_][0], in_to_replace=m163[:, c, 0:8], in_values=sc3[:, c, :], imm_value=-1e30), r=[scn, m16c[c]], w=[wks[j_][1]])
                    for j_, c in enumerate(pr):
                        P.op('dve', lambda e, c=c, j_=j_: e.max(out=m163[:, c, 8:16], in_=wks[j_][0]), r=[wks[j_][1]], w=[m16c[c]])
                    for j_, c in enumerate(pr):
                        P.op('dve', lambda e, c=c, j_=j_: e.max_index(out=i163[:, c, 8:16], in_max=m163[:, c, 8:16], in_values=wks[j_][0]), r=[wks[j_][1], m16c[c]], w=[i16c[c]])
                CP('dve', i16f, i16, i16c, [i16fn])
                CK('t_top16')
                m4 = m16.rearrange("p (h s k) -> p h s k", h=8, s=2); if4 = i16f.rearrange("p (h s k) -> p h s k", h=8, s=2)
                cand4 = cand.rearrange("p (h a b) -> p h a b", h=8, a=16)
                TT('dve', cand4, m4[:, :, 0, :].unsqueeze(3).to_broadcast([128, 8, 16, 16]), m4[:, :, 1, :].unsqueeze(2).to_broadcast([128, 8, 16, 16]), ALU.add, m16c, [candn])
                cand3 = cand.rearrange("p (h c) -> p h c", h=8)
                cv3_ = cv.rearrange("p (h k) -> p h k", h=8); ex3 = ex.rearrange("p (h k) -> p h k", h=8)
                CK('t_cand')
                ci3 = ci_.rearrange("p (h k) -> p h k", h=8)
                cvh = ['%s_%d' % (cvn, h) for h in range(8)]; cih = ['%s_%d' % (cin, h) for h in range(8)]
                wk2 = [(eid[:, 0:256], eidn + 'a'), (eid[:, 256:512], eidn + 'b')]
                for h0 in range(0, 8, 2):
                    pr = (h0, h0 + 1)
                    for h in pr:
                        P.op('dve', lambda e, h=h: e.max(out=cv3_[:, h, 0:8], in_=cand3[:, h, :]), r=[candn], w=[cvh[h]])
                    for h in pr:
                        P.op('dve', lambda e, h=h: e.max_index(out=ci3[:, h, 0:8], in_max=cv3_[:, h, 0:8], in_values=cand3[:, h, :]), r=[candn, cvh[h]], w=[cih[h]])
                    for j_, h in enumerate(pr):
                        P.op('dve', lambda e, h=h, j_=j_: e.match_replace(out=wk2[j_][0], in_to_replace=cv3_[:, h, 0:8], in_values=cand3[:, h, :], imm_value=-1e30), r=[candn, cvh[h]], w=[wk2[j_][1]])
                    for j_, h in enumerate(pr):
                        P.op('dve', lambda e, h=h, j_=j_: e.max(out=cv3_[:, h, 8:16], in_=wk2[j_][0]), r=[wk2[j_][1]], w=[cvh[h]])
                    for j_, h in enumerate(pr):
                        P.op('dve', lambda e, h=h, j_=j_: e.max_index(out=ci3[:, h, 8:16], in_max=cv3_[:, h, 8:16], in_values=wk2[j_][0]), r=[wk2[j_][1], cvh[h]], w=[cih[h]])
                CP('dve', cif, ci_, cih, [cifn])
                TS('dve', gz[:, 0:8], cv3_[:, :, 0], -1.0, None, ALU.mult, None, cvh, [gzn])
                CK('t_cv')
                for h in range(8):
                    ACT(ex3[:, h, :], cv3_[:, h, :], AF.Exp, [cvh[h], gzn], [exn, gzn], bias=gz[:, h:h + 1], accum=gz[:, 8 + h:9 + h])
                RCP(gz[:, 8:16], gz[:, 8:16], [gzn], [gzn])
                TT('dve', ex3, ex3, gz[:, 8:16].unsqueeze(2).to_broadcast([128, 8, 16]), ALU.mult, [exn, gzn], [exn])
                CK('t_sm')
                CK('t_esel')
                if DENSE:
                    TS('dve', etmp, cif, 1.0 / 16, -0.47, ALU.mult, ALU.add, [cifn], [etn])
                    CP('dve', i1i, etmp, [etn], [i1in])
                    CP('dve', af_, i1i, [i1in], [afn])
                    STT(bf_, af_, -16.0, cif, ALU.mult, ALU.add, [afn, cifn], [bfn])
                    oh3 = eid[:, 512:2560].rearrange("p (s a) -> p s a", a=16)
                    oh4 = eid[:, 512:2560].rearrange("p (h k a) -> p h k a", h=8, k=16)
                    io16 = iot[:, 0:16].unsqueeze(1).to_broadcast([128, 128, 16])
                    for (sel_, seln_, side, dst_, dstn_) in ((af_, afn, 0, i1f, i1fn), (bf_, bfn, 1, i2f, i2fn)):
                        TT('dve', oh3, io16, sel_.unsqueeze(2).to_broadcast([128, 128, 16]), ALU.is_equal, [iotn, seln_], [eidn])
                        TT('dve', oh4, oh4, if4[:, :, side, :].unsqueeze(2).to_broadcast([128, 8, 16, 16]), ALU.mult, [eidn, i16fn], [eidn])
                        P.op('dve', lambda e, dst_=dst_: e.reduce_sum(out=dst_, in_=oh3, axis=AX.X), r=[eidn], w=[dstn_])
                    for a_, (src_, srcn_) in enumerate(((ex, exn), (i1f, i1fn), (i2f, i2fn))):
                        TR(ps[a_][:, 0:128], src_, 128, [srcn_], [psn[a_]])
                        CP('act', sm3[:, a_, :], ps[a_][:, 0:128], [psn[a_]], [smn])
                    DMA(sm_d[t], sm, [smn], ['sm_d'], q='sp')
                    return
                for j in range(128):
                    g_, gn_ = gb[j % 3]
                    P.op('pool', lambda e, j=j, g_=g_: e.indirect_dma_start(out=g_, out_offset=None, in_=I['peer_u%d' % l],
                         in_offset=bass.IndirectOffsetOnAxis(ap=eseli[:, j:j + 1], axis=0)), r=[eselin], w=[gn_], dma=True)
                    STT(junk, g_, 1.0, h2, ALU.mult, ALU.mult, [gn_, h2n], [jn, avn], accum=av[:, j:j + 1])
                ACT(wv, av, AF.Gelu_apprx_tanh, [avn], [wvn])
                CK('t_ug')
                TT('dve', wv, wv, ex, ALU.mult, [wvn, exn], [wvn])
                for j in range(128):
                    g_, gn_ = gb[j % 3]
                    P.op('pool', lambda e, j=j, g_=g_: e.indirect_dma_start(out=g_, out_offset=None, in_=I['peer_v%d' % l],
                         in_offset=bass.IndirectOffsetOnAxis(ap=eseli[:, j:j + 1], axis=0)), r=[eselin], w=[gn_], dma=True)
                    if j == 0:
                        TS('dve', acc, g_, wv[:, 0:1], None, ALU.mult, None, [gn_, wvn], [accn])
                    else:
                        STT(acc, g_, wv[:, j:j + 1], acc, ALU.mult, ALU.add, [gn_, wvn, accn], [accn])
                if dbg and dbg[0] == ('peer', l, b, is_ctx) :
                    DMA(dbg_out[tok, :], acc, [accn], ['dbg'], q='sp')
                TT('dve', acc, acc, rows['g2'][0], ALU.mult, [accn, rows['g2'][1]], [accn])
                STT(r1, x1, ALPHA, acc, ALU.mult, ALU.add, [x1n, accn], [r1n, st8n], accum=st8[:, 0:1])
                layernorm(r1, r1n, 'ln2g', 'ln2b', h2, h2n)
                CK('t_ln2')
                DMA(dst[tok, :], h2, [h2n], ['dst'], q='sp')
                CK('t_tile%d_%s' % (t, 'c' if is_ctx else 'l'))
            front(0)
            for t in range(NT):
                if t + 1 < NT:
                    front(t + 1)
                back(t)
            P.barrier()
            A.release(m0)
            if DENSE:
                tailB(l, b, is_ctx, dst, L, v)
            CK('tail_done')

        def phase_tables(l):
            m0 = A.mark()
            ub = [A.alloc(1024) for _ in range(2)]; vf = [A.alloc(1024) for _ in range(2)]
            ut = [A.alloc(1024, BF16) for _ in range(2)]; vb = [A.alloc(1024, BF16) for _ in range(2)]
            for i1 in range(128):
                rs_ = slice(i1 * 128, (i1 + 1) * 128)
                u_, un_ = ub[i1 % 2]; o_, on_ = ut[i1 % 2]
                o3 = o_.rearrange("p (k e) -> p k e", k=8)
                DMA(u_, I['peer_u%d' % l][rs_, :], [], [un_])
                for half in range(2):
                    bi = (2 * i1 + half) % 4
                    for kk_ in range(4):
                        k = half * 4 + kk_
                        TR(ps[bi][:, kk_ * 128:(kk_ + 1) * 128], u_[:, k * 128:(k + 1) * 128], 128, [un_], [psn[bi]])
                    CP('act' if half else 'dve', o3[:, half * 4:(half + 1) * 4, :], ps[bi][:, :].rearrange("p (k e) -> p k e", k=4), [psn[bi]], [on_])
                DMA(UT_d[i1], o_, [on_], ['UT_d'])
                v_, vn_ = vf[i1 % 2]; w_, wn_ = vb[i1 % 2]
                DMA(v_, I['peer_v%d' % l][rs_, :], [], [vn_])
                CP('pool', w_, v_, [vn_], [wn_])
                DMA(V_d[i1], w_, [wn_], ['V_d'])
            P.barrier()
            A.release(m0)

        def tailB(l, b, is_ctx, dst, L, v):
            m0 = A.mark()
            rows = {}
            for nm, srcrow in (('ln2g', I['ln2_g'][l:l + 1, :]), ('ln2b', I['ln2_b'][l:l + 1, :]), ('g2', mod_d[l, v:v + 1, 5 * D:6 * D])):
                t_, n_ = A.alloc(1024)
                DMA(t_, srcrow.partition_broadcast(128), ['mod_d'], [n_])
                rows[nm] = (t_, n_)
            iot, iotn = A.alloc(128)
            DMA(iot, I['iota256'][:, 0:128].partition_broadcast(128), [], [iotn])
            W_, Wn = A.alloc(128 * 256, BF16); W3 = W_.rearrange("p (i t) -> p i t", i=128)
            Wall = ['%s_%d' % (Wn, i_) for i_ in range(128)]
            h2s, h2sn = A.alloc(8 * 256, BF16); h2s3 = h2s.rearrange("p (k t) -> p k t", k=8)
            sms, smsn = A.alloc(2 * 384); sms4 = sms.rearrange("p (j a t) -> p j a t", j=2, a=3)
            Qb = [A.alloc(32 * 128, BF16) for _ in range(2)]; Rb = [A.alloc(32 * 128, BF16) for _ in range(2)]
            R1, R1n = A.alloc(32 * 128, BF16)
            NSB = 6
            utb = [A.alloc(1024, BF16) for _ in range(NSB)]; vtb = [A.alloc(1024, BF16) for _ in range(NSB)]
            gab = [A.alloc(256, BF16) for _ in range(2)]
            x1t, x1tn = A.alloc(1024); acc, accn = A.alloc(1024); r1, r1n = A.alloc(1024); xo, xon = A.alloc(1024)
            st8, st8n = A.alloc(8)
            iot3 = iot.unsqueeze(1).to_broadcast([128, 32, 128])
            for sp_ in range(L // 256):
                t0 = sp_ * 256
                DMA(h2s3, h2T_d[:, :, t0:t0 + 256].rearrange("k p t -> p k t"), ['h2T_d'], [h2sn])
                for jt in range(2):
                    DMA(sms[:, jt * 384:(jt + 1) * 384], sm_d[2 * sp_ + jt], ['sm_d'], [smsn])
                for sub in range(8):
                    jt = sub // 4; tr = slice((sub % 4) * 32, (sub % 4 + 1) * 32)
                    q_, qn_ = Qb[sub % 2]; r_, rn_ = Rb[sub % 2]
                    q3 = q_.rearrange("p (t i) -> p t i", t=32); r3 = r_.rearrange("p (t i) -> p t i", t=32); R13 = R1.rearrange("p (t i) -> p t i", t=32)
                    TT('dve', q3, iot3, sms4[:, jt, 2, tr].unsqueeze(2).to_broadcast([128, 32, 128]), ALU.is_equal, [iotn, smsn], [qn_])
                    TT('dve', R13, iot3, sms4[:, jt, 1, tr].unsqueeze(2).to_broadcast([128, 32, 128]), ALU.is_equal, [iotn, smsn], [R1n])
                    TT('pool', r3, R13, sms4[:, jt, 0, tr].unsqueeze(2).to_broadcast([128, 32, 128]), ALU.mult, [R1n, smsn], [rn_])
                    for j in range(32):
                        tg = sub * 32 + j
                        bi = (tg // 4) % 2
                        MM(ps[bi][:, (tg % 4) * 128:(tg % 4 + 1) * 128], q3[:, j, :], r3[:, j, :], True, True, [qn_, rn_], [psn[bi]])
                        if tg % 4 == 3:
                            CP('act' if (tg // 4) % 2 else 'dve', W3[:, :, tg - 3:tg + 1].rearrange("p i t -> p t i"),
                               ps[bi][:, :].rearrange("p (t i) -> p t i", t=4), [psn[bi]], Wall)
                def vmm(i1):
                    v_, vn_ = vtb[i1 % NSB]
                    for jt in range(2):
                        for dh in range(2):
                            bi = 4 + jt * 2 + dh
                            MM(ps[bi][:, :], W3[:, i1, jt * 128:(jt + 1) * 128], v_[:, dh * 512:(dh + 1) * 512], i1 == 0, i1 == 127, [Wall[i1], vn_], [psn[bi]])
                for i1 in range(128):
                    u_, un_ = utb[i1 % NSB]; u3 = u_.rearrange("p (k e) -> p k e", k=8)
                    DMA(u_, UT_d[i1], ['UT_d'], [un_], q='sp')
                    v_, vn_ = vtb[i1 % NSB]
                    DMA(v_, V_d[i1], ['V_d'], [vn_], q='act')
                    bi = 2 + i1 % 2
                    for k in range(8):
                        MM(ps[bi][:, 0:256], u3[:, k, :], h2s3[:, k, :], k == 0, k == 7, [un_, h2sn], [psn[bi]])
                    g_, gn_ = gab[i1 % 2]
                    ACT(g_, ps[bi][:, 0:256], AF.Gelu_apprx_tanh, [psn[bi]], [gn_])
                    TT('pool' if i1 % 2 else 'dve', W3[:, i1, :], W3[:, i1, :], g_, ALU.mult, [Wall[i1], gn_], [Wall[i1]])
                    if i1 >= 1:
                        vmm(i1 - 1)
                vmm(127)
                for jt in range(2):
                    tok = slice(t0 + jt * 128, t0 + (jt + 1) * 128)
                    DMA(x1t, x1_d[tok, :], ['x1_d'], [x1tn])
                    for dh in range(2):
                        bi = 4 + jt * 2 + dh
                        TT('dve', acc[:, dh * 512:(dh + 1) * 512], ps[bi][:, :], rows['g2'][0][:, dh * 512:(dh + 1) * 512], ALU.mult, [psn[bi], rows['g2'][1]], [accn])
                    if dbg and dbg[0] == ('peer', l, b, is_ctx):
                        pass
                    STT(r1, x1t, ALPHA, acc, ALU.mult, ALU.add, [x1tn, accn], [r1n, st8n], accum=st8[:, 0:1])
                    TS('dve', st8[:, 1:2], st8[:, 0:1], -1.0 / D, None, ALU.mult, None, [st8n], [st8n])
                    ACT(acc, r1, AF.Square, [r1n, st8n], [accn, st8n], bias=st8[:, 1:2], accum=st8[:, 2:3])
                    ACT(st8[:, 3:4], st8[:, 2:3], AF.Sqrt, [st8n], [st8n], bias=EPS, scale=1.0 / D)
                    RCP(st8[:, 4:5], st8[:, 3:4], [st8n], [st8n])
                    TS('dve', xo, r1, st8[:, 1:2], st8[:, 4:5], ALU.add, ALU.mult, [r1n, st8n], [xon])
                    TT('dve', xo, xo, rows['ln2g'][0], ALU.mult, [xon, rows['ln2g'][1]], [xon])
                    TT('dve', xo, xo, rows['ln2b'][0], ALU.add, [xon, rows['ln2b'][1]], [xon])
                    DMA(dst[tok, :], xo, [xon], ['dst'], q='sp')
            P.barrier()
            A.release(m0)

        try:
            for l in layers:
                phase_params(l)
                CK('params')
                if DENSE and do_tail:
                    phase_tables(l)
                if 'hy' in mixers:
                    phase_filters(l, S)
                    CK('filtS')
                    if l < DEPTH - 1:
                        phase_filters(l, LC)
                        CK('filtC')
                for b in batches:
                    run_seq(l, b, True)
                    run_seq(l, b, False)
        except _Stop:
            pass

        if dbg and dbg[0] == 'xs':
            DMA(dbg_out, xs_d[dbg[3]], ['x'], ['dbg'], q='sp')
        if dbg and dbg[0] == 'xcs':
            DMA(dbg_out, xcs_d[dbg[3]], ['x'], ['dbg'], q='sp')
        if dbg and dbg[0] == 'modT':
            DMA(dbg_out, modT[:], ['modT'], ['dbg'], q='sp')
        if dbg and dbg[0] == 'biasK':
            DMA(dbg_out, biasK[:], ['biasK'], ['dbg'], q='sp')
        if dbg and dbg[0] in ('ksp2048', 'ksp256'):
            DMA(dbg_out, ksp_d[int(dbg[0][3:])], ['x'], ['dbg'], q='sp')
        P.barrier()
        with nc.Block() as block:
            P.emit(sems, block)
    nc._prog_nops = P.nops
    return nc


_CACHE = {}


def kernel(**inputs):
    if 'nc' not in _CACHE:
        _CACHE['nc'] = build()
        _CACHE['consts'] = consts()
    nc = _CACHE['nc']; C = _CACHE['consts']
    f32 = lambda a: np.ascontiguousarray(np.asarray(a, dtype=np.float32))
    shared = {k: f32(inputs[k]) for k in WSHAPES if not k.startswith('peer_u') and not k.startswith('peer_v')}
    for l in range(DEPTH):
        shared['peer_u%d' % l] = f32(inputs['peer_u'][l]); shared['peer_v%d' % l] = f32(inputs['peer_v'][l])
    shared.update(C)
    x = f32(inputs['x']); ctx = f32(inputs['ctx']); c = f32(inputs['c']); cc = f32(inputs['c_ctx'])
    in_maps = []
    for i in range(8):
        m = dict(shared)
        m['x'] = x[2 * i:2 * i + 2]; m['ctx'] = ctx[2 * i:2 * i + 2]
        m['cvec'] = np.ascontiguousarray(np.stack([c[2 * i], c[2 * i + 1], cc]))
        in_maps.append(m)
    res = run_bass_kernel_spmd(nc, in_maps, core_ids=list(range(8)))
    return np.concatenate([np.asarray(r['out']) for r in res.results], axis=0).astype(np.float32)
```

```python
import math
from contextlib import ExitStack
import numpy as np
import ml_dtypes
import concourse.bass as bass
import concourse.mybir as mybir
from concourse.bass_utils import run_bass_kernel_spmd

F32 = mybir.dt.float32; BF16 = mybir.dt.bfloat16; I32 = mybir.dt.int32; U32 = mybir.dt.uint32
AF = mybir.ActivationFunctionType; ALU = mybir.AluOpType; AX = mybir.AxisListType

D = 1024; NB = 2; S = 2048; LC = 256; DEPTH = 2; G = 256
ALPHA = (2.0 * DEPTH) ** 0.25
EPS = 1e-5
NDQ = 20
DENSE = True
RELAX_SAME_ENGINE = False
PI = math.pi
ARENA_COLS = 44 * 1024

WSHAPES = dict(
    w_ada=[2, 1024, 6144], b_ada=[2, 6144], w_in=[2, 1024, 2824], w_out=[2, 1024, 1024],
    ln1_g=[2, 1024], ln1_b=[2, 1024], ln2_g=[2, 1024], ln2_b=[2, 1024],
    conf_dw_w=[2, 31, 256], conf_dw_b=[2, 256], conf_norm_g=[2, 256], conf_norm_b=[2, 256],
    na_rpb=[2, 4, 15, 31], hy_short_w=[2, 3, 768], hy_short_b=[2, 768], hy_w1=[2, 33, 64], hy_b1=[2, 64],
    hy_w2=[2, 64, 64], hy_b2=[2, 64], hy_w3=[2, 64, 512], hy_decay=[2, 512], hy_bias=[2, 256],
    ssd_conv_w=[2, 3, 512], ssd_conv_b=[2, 512], ssd_a_log=[2, 2, 4], ssd_dt_bias=[2, 2, 4], ssd_d=[2, 4],
    ssd_norm_g=[2, 256], peer_wq=[2, 1024, 2048], peer_keys=[2, 8, 2, 128, 128],
    peer_u0=[16384, 1024], peer_u1=[16384, 1024], peer_v0=[16384, 1024], peer_v1=[16384, 1024])


class Prog:
    def __init__(self, nc):
        self.nc = nc
        self.E = dict(pe=nc.tensor, dve=nc.vector, act=nc.scalar, pool=nc.gpsimd, sp=nc.sync)
        self.stream = {e: [] for e in self.E}
        self.cnt = {}
        self.waited = {e: {} for e in self.E}
        self.bufs = {}
        self.dq = {e: 0 for e in self.E}
        self.nops = 0

    def _need(self, eng, deps):
        for k, v in deps.items():
            if eng == 'pe' and k == 'c_pe':
                continue
            if self.waited[eng].get(k, 0) < v:
                self.stream[eng].append(('wait', k, v))
                self.waited[eng][k] = v

    def op(self, eng, fn, r=(), w=(), dma=False):
        w = list(w) + [b for b in r if b.startswith('ps') and b not in w]
        deps = {}
        own = 'c_' + eng
        owncnt = self.cnt.get(own, 0)
        relax = RELAX_SAME_ENGINE and not dma
        def add(tok, raw):
            if tok is None:
                return
            if relax and tok[0] == own:
                if not raw or tok[1] < owncnt:
                    return
            if deps.get(tok[0], 0) < tok[1]:
                deps[tok[0]] = tok[1]
        psw = set(b for b in r if b.startswith('ps'))
        for b in r:
            st = self.bufs.get(b)
            if st:
                add(st['w'], True)
        for b in w:
            st = self.bufs.get(b)
            if st:
                add(st['w'], b in psw)
                for k, v in st['r'].items():
                    add((k, v), False)
        if dma:
            i = self.dq[eng] % NDQ
            self.dq[eng] += 1
            key = 'd_%s_%d' % (eng, i)
            cur = self.cnt.get(key, 0)
            if cur:
                add((key, cur), True)
            inc = 16
        else:
            key = 'c_' + eng
            inc = 1
        self._need(eng, deps)
        self.cnt[key] = self.cnt.get(key, 0) + inc
        tok = (key, self.cnt[key])
        self.stream[eng].append(('op', fn, key, inc))
        self.nops += 1
        for b in r:
            st = self.bufs.setdefault(b, {'w': None, 'r': {}})
            if st['r'].get(tok[0], 0) < tok[1]:
                st['r'][tok[0]] = tok[1]
        for b in w:
            self.bufs[b] = {'w': tok, 'r': {}}
        return tok

    def barrier(self):
        for e in self.E:
            self._need(e, dict(self.cnt))
        self.bufs = {}

    def emit(self, sems, block):
        def mk(e):
            def body(eng):
                for it in self.stream[e]:
                    if it[0] == 'wait':
                        eng.wait_ge(sems[it[1]], it[2])
                    else:
                        ins = it[1](eng)
                        ins.then_inc(sems[it[2]], it[3])
            return body
        block.tensor(mk('pe')); block.vector(mk('dve')); block.scalar(mk('act'))
        block.gpsimd(mk('pool')); block.sync(mk('sp'))

    def sem_keys(self):
        ks = ['c_' + e for e in self.E]
        for e in ('sp', 'act', 'pool'):
            ks += ['d_%s_%d' % (e, i) for i in range(NDQ)]
        return ks


class _Stop(Exception):
    pass


class Arena:
    def __init__(self, t, n):
        self.t = t; self.n = n; self.top = 0; self.k = 0; self.P = None

    def alloc(self, cols, dt=F32, parts=128):
        n32 = cols if dt != BF16 else (cols + 1) // 2
        assert self.top + n32 <= self.n, ('arena overflow', self.top, n32, self.n)
        a = self.t[0:parts, self.top:self.top + n32]
        if dt != F32:
            a = a.bitcast(dt)
            if dt == BF16 and cols % 2:
                a = a[:, 0:cols]
        self.top += n32
        self.hw = max(getattr(self, 'hw', 0), self.top)
        self.k += 1
        return a, 'A%d' % self.k

    def mark(self):
        return self.top

    def release(self, m):
        if self.P is not None:
            self.P.barrier()
        if m == 0 or getattr(self, 'verbose', False):
            pass
        self.top = m


def consts():
    C = {}
    n_f = 16
    inv = 10000.0 ** (-np.arange(n_f, dtype=np.float32) / n_f)
    t = np.arange(S)
    r = (t // 64).astype(np.float32); col = (t % 64).astype(np.float32)
    ang = np.concatenate([r[:, None] * inv, col[:, None] * inv], -1)
    cos = np.cos(ang).astype(np.float32).T; sin = np.sin(ang).astype(np.float32).T
    C['rope'] = np.stack([np.concatenate([cos, cos], 0), np.concatenate([-sin, sin], 0)]).astype(np.float32)
    m = np.zeros((9, 128, 896), np.float32)
    kl = np.arange(128); krl = kl // 64; kc = kl % 64
    for cls, i in enumerate([0, 1, 2, 3, 6, 12, 13, 14, 15]):
        for dj in range(-3, 4):
            j = i + dj
            if j < 0 or j > 15:
                continue
            ql = np.arange(128); rl = ql // 64; c = ql % 64
            rr = 2 * j + rl; kr = 2 * i + krl
            rs = np.clip(rr - 4, 0, 24); cs = np.clip(c - 8, 0, 48)
            ok = ((kr[:, None] >= rs[None, :]) & (kr[:, None] < rs[None, :] + 8)
                  & (kc[:, None] >= cs[None, :]) & (kc[:, None] < cs[None, :] + 16))
            m[cls, :, (dj + 3) * 128:(dj + 4) * 128] = ok
    C['namask'] = np.ascontiguousarray(m.transpose(1, 0, 2)).astype(ml_dtypes.bfloat16)
    sp = np.arange(128)[:, None]; lq = np.arange(128)[None, :]
    C['iota256'] = np.arange(256, dtype=np.float32)[None, :]
    C['ssdc'] = np.stack([(sp <= lq), (sp < lq), np.where(lq < sp, -30000.0, 0.0), np.where(lq > sp, 30000.0, 0.0)]).astype(np.float32)
    for L in (S, LC):
        tn = np.arange(L, dtype=np.float32)[:, None] / np.float32(L)
        bands = np.arange(1, 17, dtype=np.float32)[None, :]
        a2 = (2.0 * math.pi * bands * tn).astype(np.float32)
        z = np.concatenate([tn, np.sin(a2), np.cos(a2)], -1).astype(np.float32)
        C['zT%d' % L] = np.ascontiguousarray(z.T)
        N = 2 * L; NFp = L + 128
        s = np.arange(L, dtype=np.float64)[:, None]; f = np.arange(NFp, dtype=np.float64)[None, :]
        th = 2 * np.pi * ((s * f) % N) / N
        valid = (f <= L)
        fw = np.concatenate([np.cos(th) * valid, np.sin(th) * valid], 1)
        nch = (2 * NFp + 511) // 512
        fwp = np.zeros((L, nch * 512), np.float64); fwp[:, :2 * NFp] = fw
        fwc = fwp.reshape(L // 128, 128, nch, 512).transpose(2, 1, 0, 3)
        C['fw%d' % L] = np.ascontiguousarray(fwc).astype(np.float32).astype(ml_dtypes.bfloat16)
        wf = np.where((f == 0) | (f == L), 1.0, 2.0) * valid / N
        iv = np.stack([(np.cos(th) * wf).T, (-np.sin(th) * wf).T])
        C['inv%d' % L] = iv.astype(np.float32).astype(ml_dtypes.bfloat16)
    return C


CSHAPES = dict(rope=([2, 64, S], F32), iota256=([1, 256], F32), namask=([128, 9, 896], BF16), ssdc=([4, 128, 128], F32),
               zT2048=([33, S], F32), zT256=([33, LC], F32),
               fw2048=([9, 128, 16, 512], BF16), inv2048=([2, S + 128, S], BF16),
               fw256=([2, 128, 2, 512], BF16), inv256=([2, LC + 128, LC], BF16))


def build(plan=None, dbg=None):
    plan = plan or {}
    layers = plan.get('layers', list(range(DEPTH)))
    batches = plan.get('batches', list(range(NB)))
    mixers = plan.get('mixers', ['conf', 'attn', 'hy', 'ssd'])
    do_tail = plan.get('tail', True)
    nc = bass.Bass("TRN2", target_bir_lowering=False)
    I = {}
    def din(name, shape, dt=F32):
        I[name] = nc.dram_tensor(name, list(shape), dt, kind="ExternalInput").ap()
    def dscr(name, shape, dt=F32):
        return nc.dram_tensor(name, list(shape), dt, kind="Internal").ap()
    din('x', [NB, S, D]); din('ctx', [NB, LC, D]); din('cvec', [3, D])
    for k, shp in WSHAPES.items():
        din(k, shp)
    for k, (shp, dt) in CSHAPES.items():
        din(k, shp, dt)
    out = nc.dram_tensor('out', [NB, S, D], F32, kind="ExternalOutput").ap()
    mod_d = dscr('mod_d', [DEPTH, 3, 6 * D])
    xs_d = dscr('xs_d', [NB, S, D]); xcs_d = dscr('xcs_d', [NB, LC, D])
    ycat_d = {S: dscr('ycat_lat', [8, 128, S], BF16), LC: dscr('ycat_ctx', [8, 128, LC], BF16)}
    ksp_d = {S: dscr('ksp_lat', [2, 256, S + 128]), LC: dscr('ksp_ctx', [2, 256, LC + 128])}
    zq_d = dscr('zq_d', [60, 64, 127])
    UT_d = dscr('UT_d', [128, 128, 1024], BF16); V_d = dscr('V_d', [128, 128, 1024], BF16)
    x1_d = dscr('x1_d', [S, D]); h2T_d = dscr('h2T_d', [8, 128, S], BF16); sm_d = dscr('sm_d', [S // 128, 128, 3 * 128])
    dbg_out = None
    if dbg:
        dbg_out = nc.dram_tensor('dbg', list(dbg[1]), dbg[2] if len(dbg) > 2 and dbg[2] is not None else F32, kind="ExternalOutput").ap()

    P = Prog(nc)
    with ExitStack() as es:
        def sb(name, shape, dt=F32):
            return es.enter_context(nc.sbuf_tensor('s_' + name, list(shape), dt))
        A = Arena(sb('arena', [128, ARENA_COLS], F32), ARENA_COLS)
        A.P = P
        ident = sb('ident', [128, 128]); ones = sb('ones', [128, 128]); blk = sb('blk', [128, 128])
        modT = sb('modT', [128, DEPTH * 48 * 3]); modT1 = sb('modT1', [128, DEPTH * 48 * 3])
        ssdc = sb('ssdc', [128, 4 * 128])
        cw = sb('cw', [128, 2 * 31]); cv3 = sb('cv3', [128, 2 * 3])
        hsp = sb('hsp', [128, 6 * 4]); hskip = sb('hskip', [128, 2])
        sxp = sb('sxp', [128, 2 * 4]); sbp = sb('sbp', [64, 4 * 4])
        A_bc = sb('A_bc', [128, 8]); dtb_bc = sb('dtb_bc', [128, 8]); dsk_bc = sb('dsk_bc', [128, 4]); ng_bc = sb('ng_bc', [128, 256])
        biasK = sb('biasK', [128, 4 * 896])
        kcT = sb('kcT', [64, 4 * 256], BF16); vc1 = sb('vc1', [128, 2 * 4 * 65], BF16)
        Sst = sb('Sst', [64, 8 * 64]); SstB = sb('SstB', [64, 8 * 64], BF16)
        ps = [es.enter_context(nc.psum_tensor('ps%d' % i, [128, 512], F32)) for i in range(8)]
        psn = ['ps%d' % i for i in range(8)]
        sems = {k: es.enter_context(nc.semaphore(k)) for k in P.sem_keys()}
        qrot = [0]

        ck = [0]
        def CK(tag):
            ck[0] += 1
            if plan.get('stop') == ck[0] or plan.get('stoptag') == tag:
                print('STOP at', tag, flush=True)
                raise _Stop()
        def MM(o, lhsT, rhs, start, stop, r, w):
            P.op('pe', lambda e: e.matmul(o, lhsT=lhsT, rhs=rhs, start=start, stop=stop), r=r, w=w)
        def TR(o, in_, n, r, w):
            P.op('pe', lambda e: e.transpose(out=o, in_=in_, identity=ident[0:n, 0:n]), r=list(r) + ['ident'], w=w)
        def DMA(o, in_, r, w, q=None, slow=False):
            if q is None:
                q = ('sp', 'act')[qrot[0] % 2]; qrot[0] += 1
            P.op(q, lambda e: e.dma_start(out=o, in_=in_, allow_slow_non_contiguous=slow), r=r, w=w, dma=True)
        def ACT(o, in_, func, r, w, bias=None, scale=None, accum=None):
            kw = {}
            if bias is not None: kw['bias'] = bias
            if scale is not None: kw['scale'] = scale
            if accum is not None: kw['accum_out'] = accum
            P.op('act', lambda e: e.activation(out=o, in_=in_, func=func, **kw), r=r, w=w)
        def TT(eng, o, in0, in1, op, r, w):
            P.op(eng, lambda e: e.tensor_tensor(out=o, in0=in0, in1=in1, op=op), r=r, w=w)
        def TS(eng, o, in0, s1, s2, op0, op1, r, w, accum=None):
            kw = {}
            if op1 is not None: kw['op1'] = op1
            if accum is not None: kw['accum_out'] = accum
            P.op(eng, lambda e: e.tensor_scalar(out=o, in0=in0, scalar1=s1, scalar2=s2, op0=op0, **kw), r=r, w=w)
        def STT(o, in0, sc, in1, op0, op1, r, w, accum=None):
            kw = {}
            if accum is not None: kw['accum_out'] = accum
            P.op('dve', lambda e: e.scalar_tensor_tensor(out=o, in0=in0, scalar=sc, in1=in1, op0=op0, op1=op1, **kw), r=r, w=w)
        def CP(eng, o, in_, r, w):
            if eng == 'act':
                P.op('act', lambda e: e.copy(out=o, in_=in_), r=r, w=w)
            else:
                P.op(eng, lambda e: e.tensor_copy(out=o, in_=in_), r=r, w=w)
        def MS(eng, o, val, w):
            P.op(eng, lambda e: e.memset(o, val), w=w)
        def RCP(o, in_, r, w):
            P.op('dve', lambda e: e.reciprocal(out=o, in_=in_), r=r, w=w)

        MS('pool', ident[:], 0.0, ['ident'])
        P.op('pool', lambda e: e.affine_select(out=ident[:], in_=ident[:], pattern=[[-1, 128]], compare_op=ALU.not_equal,
                                                fill=1.0, base=0, channel_multiplier=1), r=['ident'], w=['ident'])
        MS('pool', ones[:], 1.0, ['ones'])
        MS('pool', modT[:], 0.0, ['modT'])
        MS('pool', blk[:], 0.0, ['blk'])
        MS('pool', blk[0:64, 0:64], 1.0 / 64, ['blk'])
        MS('pool', blk[64:128, 64:128], 1.0 / 64, ['blk'])
        ssdc3 = ssdc[:].rearrange("p (a b) -> p a b", a=4)
        for a_ in range(4):
            DMA(ssdc3[:, a_, :], I['ssdc'][a_], [], ['ssdc'])
        P.barrier()

        mT4 = modT[:].rearrange("p (l j v) -> p l j v", l=DEPTH, j=48)
        mT14 = modT1[:].rearrange("p (l j v) -> p l j v", l=DEPTH, j=48)

        def phase_mod(l):
            m0 = A.mark()
            cT, cTn = A.alloc(24); sT, sTn = A.alloc(24)
            mrow, mrn = A.alloc(6 * D, parts=3)
            wbuf = [A.alloc(8 * 512) for _ in range(2)]
            brow, brn = A.alloc(6 * D, parts=1)
            cT3 = cT.rearrange("p (k v) -> p k v", v=3)
            for v in range(3):
                DMA(cT3[:, :, v], I['cvec'][v].rearrange("(k p) -> p k", p=128), [], [cTn], slow=True)
            DMA(brow, I['b_ada'][l:l + 1, :], [], [brn])
            ACT(sT, cT, AF.Silu, [cTn], [sTn])
            sT3 = sT.rearrange("p (k v) -> p k v", v=3)
            for n in range(12):
                wb, wbn = wbuf[n % 2]
                wb3 = wb.rearrange("p (k c) -> p k c", c=512)
                DMA(wb3, I['w_ada'][l, :, n * 512:(n + 1) * 512].rearrange("(k p) c -> p k c", p=128), [], [wbn])
                pt = ps[n % 2]; pn = psn[n % 2]
                for k in range(8):
                    MM(pt[0:3, :], sT3[:, k, :], wb3[:, k, :], k == 0, False, [sTn, wbn], [pn])
                MM(pt[0:3, :], ones[0:1, 0:3], brow[0:1, n * 512:(n + 1) * 512], False, True, ['ones', brn], [pn])
                CP('dve', mrow[0:3, n * 512:(n + 1) * 512], pt[0:3, :], [pn], [mrn])
            DMA(mod_d[l], mrow, [mrn], ['mod_d'], q='sp')
            for j in range(48):
                pt = ps[2 + j % 2]; pn = psn[2 + j % 2]
                TR(pt[:, 0:3], mrow[0:3, j * 128:(j + 1) * 128], 3, [mrn], [pn])
                CP('dve', mT4[:, l, j, :], pt[:, 0:3], [pn], ['modT'])
            P.barrier()
            A.release(m0)

        for l in layers:
            phase_mod(l)
        TS('dve', modT1[:], modT[:], 1.0, None, ALU.add, None, ['modT'], ['modT1'])
        P.barrier()

        def paramT(dst3, dname, rows, n, nch, Pn, c0=0):
            m0 = A.mark()
            if not isinstance(rows, list):
                C = rows.shape[-1]
                stg, sn = A.alloc(C, parts=n)
                DMA(stg[0:n, :], rows, [], [sn])
            else:
                C = rows[0].shape[-1]
                stg, sn = A.alloc(C, parts=max(n, 1))
                j = 0
                for rw in rows:
                    nr = rw.shape[0]
                    DMA(stg[j:j + nr, :], rw, [], [sn])
                    j += nr
            for ch in range(nch):
                pt = ps[ch % 2]; pn = psn[ch % 2]
                TR(pt[0:Pn, 0:n], stg[0:n, c0 + ch * Pn:c0 + (ch + 1) * Pn], n, [sn], [pn])
                CP('dve', dst3[0:Pn, ch, :], pt[0:Pn, 0:n], [pn], [dname])
            P.barrier()
            A.release(m0)

        def load_w(dst3, dname, src2d, n, wst):
            for k in range(8):
                st, sn = wst[k % 2]
                DMA(st[:, 0:n], src2d[k * 128:(k + 1) * 128, :], [], [sn])
                CP('act' if k % 2 else 'pool', dst3[:, k, 0:n], st[:, 0:n], [sn], [dname])

        def phase_params(l):
            row = lambda nm: I[nm][l:l + 1, :]
            paramT(cw[:].rearrange("p (c k) -> p c k", c=2), 'cw', I['conf_dw_w'][l], 31, 2, 128)
            paramT(cv3[:].rearrange("p (c k) -> p c k", c=2), 'cv3', [row('conf_dw_b'), row('conf_norm_g'), row('conf_norm_b')], 3, 2, 128)
            paramT(hsp[:].rearrange("p (c k) -> p c k", c=6), 'hsp', [I['hy_short_w'][l], row('hy_short_b')], 4, 6, 128)
            paramT(hskip[:].rearrange("p (c k) -> p c k", c=2), 'hskip', [row('hy_bias')], 1, 2, 128)
            srows = [I['ssd_conv_w'][l], row('ssd_conv_b')]
            paramT(sxp[:].rearrange("p (c k) -> p c k", c=2), 'sxp', srows, 4, 2, 128)
            paramT(sbp[:].rearrange("p (c k) -> p c k", c=4), 'sbp', srows, 4, 4, 64, c0=256)
            alog = I['ssd_a_log'].rearrange("l a b -> l (a b)")[l:l + 1, :]
            dtb = I['ssd_dt_bias'].rearrange("l a b -> l (a b)")[l:l + 1, :]
            DMA(A_bc[:], alog.partition_broadcast(128), [], ['A_bc'])
            DMA(dtb_bc[:], dtb.partition_broadcast(128), [], ['dtb_bc'])
            DMA(dsk_bc[:], row('ssd_d').partition_broadcast(128), [], ['dsk_bc'])
            DMA(ng_bc[:], row('ssd_norm_g').partition_broadcast(128), [], ['ng_bc'])
            ACT(A_bc[:], A_bc[:], AF.Exp, ['A_bc'], ['A_bc'])
            TS('dve', A_bc[:], A_bc[:], -1.0, None, ALU.mult, None, ['A_bc'], ['A_bc'])
            m0 = A.mark()
            Pt, Pn_ = A.alloc(127, parts=60)
            MS('pool', Pt, 0.0, [Pn_])
            DMA(Pt[:, 48:79], I['na_rpb'][l].rearrange("h a b -> (h a) b"), [Pn_], [Pn_])
            DMA(zq_d, Pt.unsqueeze(1).to_broadcast([60, 64, 127]), [Pn_], ['zq_d'], q='sp')
            bq, bqn = A.alloc(4 * 896)
            bq3 = bq.rearrange("p (h c) -> p h c", h=4)
            bK3 = biasK[:].rearrange("p (h c) -> p h c", h=4)
            for h in range(4):
                for di in range(-3, 4):
                    for rl in range(2):
                        for krl in range(2):
                            X = 2 * di + krl - rl + 7
                            src = bass.AP(zq_d.tensor, (h * 15 + X) * 64 * 127 + 63, [[126, 64], [1, 64]])
                            DMA(bq3[rl * 64:(rl + 1) * 64, h, (di + 3) * 128 + krl * 64:(di + 3) * 128 + (krl + 1) * 64], src, ['zq_d'], [bqn])
            k_ = 0
            for h in range(4):
                for dj in range(-3, 4):
                    di = -dj
                    pt = ps[k_ % 2]; pn = psn[k_ % 2]; k_ += 1
                    TR(pt[:, 0:128], bq3[:, h, (di + 3) * 128:(di + 4) * 128], 128, [bqn], [pn])
                    CP('dve', bK3[:, h, (dj + 3) * 128:(dj + 4) * 128], pt[:, 0:128], [pn], ['biasK'])
            P.barrier()
            A.release(m0)

        def sinwrap(dst, src, bias, n, tmp, r, w):
            tA, tAn = tmp[0]; tB, tBn = tmp[1]; tC, tCn = tmp[2]
            TS('dve', tA[0:64, 0:n], src, bias, None, ALU.add, None, r, [tAn])
            TS('dve', tB[0:64, 0:n], tA[0:64, 0:n], PI, -2 * PI, ALU.is_gt, ALU.mult, [tAn], [tBn])
            TT('dve', tC[0:64, 0:n], tA[0:64, 0:n], tB[0:64, 0:n], ALU.add, [tAn, tBn], [tCn])
            TS('dve', tB[0:64, 0:n], tA[0:64, 0:n], -PI, 2 * PI, ALU.is_lt, ALU.mult, [tAn], [tBn])
            TT('dve', tA[0:64, 0:n], tC[0:64, 0:n], tB[0:64, 0:n], ALU.add, [tCn, tBn], [tAn])
            ACT(dst, tA[0:64, 0:n], AF.Sin, [tAn], w)

        def phase_filters(l, L):
            NT = L // 128; CH = min(512, L); NQ = L // CH; NFp = L + 128
            m0 = A.mark()
            zT, zn = A.alloc(L, parts=33)
            DMA(zT, I['zT%d' % L], [], [zn])
            w1, w1n = A.alloc(64, parts=33); w2, w2n = A.alloc(64, parts=64); w3, w3n = A.alloc(512, parts=64)
            dec, decn = A.alloc(512, parts=1)
            b12, b12n = A.alloc(2, parts=64)
            DMA(w1, I['hy_w1'][l], [], [w1n]); DMA(w2, I['hy_w2'][l], [], [w2n]); DMA(w3, I['hy_w3'][l], [], [w3n])
            DMA(dec, I['hy_decay'][l:l + 1, :], [], [decn])
            bst, bstn = A.alloc(64, parts=2)
            DMA(bst[0:1, :], I['hy_b1'][l:l + 1, :], [], [bstn]); DMA(bst[1:2, :], I['hy_b2'][l:l + 1, :], [], [bstn])
            TR(ps[0][0:64, 0:2], bst[0:2, 0:64], 2, [bstn], [psn[0]])
            CP('dve', b12, ps[0][0:64, 0:2], [psn[0]], [b12n])
            h1, h1n = A.alloc(L, parts=64); h2, h2n = A.alloc(L, parts=64)
            tmp = [A.alloc(CH, parts=64) for _ in range(3)]
            for q in range(NQ):
                cs = slice(q * CH, (q + 1) * CH)
                MM(ps[1][0:64, 0:CH], w1[0:33, :], zT[0:33, cs], True, True, [w1n, zn], [psn[1]])
                sinwrap(h1[:, cs], ps[1][0:64, 0:CH], b12[:, 0:1], CH, tmp, [psn[1], b12n], [h1n])
            for q in range(NQ):
                cs = slice(q * CH, (q + 1) * CH)
                MM(ps[2][0:64, 0:CH], w2[0:64, :], h1[:, cs], True, True, [w2n, h1n], [psn[2]])
                sinwrap(h2[:, cs], ps[2][0:64, 0:CH], b12[:, 1:2], CH, tmp, [psn[2], b12n], [h2n])
            kk, kkn = A.alloc(NT * 512)
            kk3 = kk.rearrange("p (t c) -> p t c", c=512)
            et = [A.alloc(512) for _ in range(2)]; abt = [A.alloc(512) for _ in range(2)]
            for t in range(NT):
                pa = ps[2 + t % 2]; pb = ps[4 + t % 2]
                MM(pa[:, 0:512], h2[0:64, t * 128:(t + 1) * 128], w3[0:64, :], True, True, [h2n, w3n], [psn[2 + t % 2]])
                MM(pb[:, 0:512], zT[0:1, t * 128:(t + 1) * 128], dec[0:1, :], True, True, [zn, decn], [psn[4 + t % 2]])
                e_, en = et[t % 2]; a_, an = abt[t % 2]
                ACT(e_, pb[:, 0:512], AF.Exp, [psn[4 + t % 2]], [en], scale=-1.0)
                TT('dve', kk3[:, t, :], pa[:, 0:512], e_, ALU.mult, [psn[2 + t % 2], en], [kkn])
                ACT(a_, kk3[:, t, :], AF.Abs, [kkn], [an])
                MM(ps[6][:, 0:512], ones[:], a_, t == 0, t == NT - 1, ['ones', an], [psn[6]])
            rect, rn = A.alloc(512)
            TS('dve', rect, ps[6][:, 0:512], 1e-6, None, ALU.add, None, [psn[6]], [rn])
            RCP(rect, rect, [rn], [rn])
            TT('dve', kk3, kk3, rect.unsqueeze(1).to_broadcast([128, NT, 512]), ALU.mult, [kkn, rn], [kkn])
            MS('dve', kk3[0:1, 0, 256:512], 0.0, [kkn])
            ksd, ksdn = A.alloc(2 * NT * 256, BF16)
            ksd4 = ksd.rearrange("p (a t c) -> p a t c", a=2, c=256)
            TT('dve', ksd4[:, 0], kk3[:, :, 0:256], kk3[:, :, 256:512], ALU.add, [kkn], [ksdn])
            TT('dve', ksd4[:, 1], kk3[:, :, 256:512], kk3[:, :, 0:256], ALU.subtract, [kkn], [ksdn])
            if dbg and dbg[0] == 'filt%d' % L:
                DMA(dbg_out, kk, [kkn], ['dbg'], q='sp')
            tbs = [A.alloc(NT * 512, BF16) for _ in range(2)]
            ko = [A.alloc(512) for _ in range(2)]
            nch = (2 * NFp + 511) // 512
            for g_ in range(nch):
                tb, tbn = tbs[g_ % 2]
                tb3 = tb.rearrange("p (t c) -> p t c", c=512)
                DMA(tb, I['fw%d' % L][g_].rearrange("p t c -> p (t c)"), [], [tbn])
                glo = g_ * 512; ghi = min(glo + 512, 2 * NFp)
                segs = []
                if glo < NFp:
                    segs.append((0, glo, min(ghi, NFp)))
                if ghi > NFp:
                    segs.append((1, max(glo, NFp), ghi))
                for cc in range(2):
                    pt = ps[(g_ * 2 + cc) % 4]; pn = psn[(g_ * 2 + cc) % 4]
                    k_, kn_ = ko[cc]
                    for (ri, a_, b_) in segs:
                        lo = a_ - glo; hi = b_ - glo
                        for t in range(NT):
                            MM(pt[:, lo:hi], ksd4[:, ri, t, cc * 128:(cc + 1) * 128], tb3[:, t, lo:hi], t == 0, t == NT - 1, [ksdn, tbn], [pn])
                        CP('act', k_[:, lo:hi], pt[:, lo:hi], [pn], [kn_])
                        DMA(ksp_d[L][ri, cc * 128:(cc + 1) * 128, a_ - ri * NFp:b_ - ri * NFp], k_[:, lo:hi], [kn_], ['ksp%d' % L])
            P.barrier()
            A.release(m0)

        def run_seq(l, b, is_ctx):
            last = (l == DEPTH - 1)
            L = LC if is_ctx else S
            NT = L // 128; CH = min(512, L); NQ = L // CH
            v = 2 if is_ctx else b
            if is_ctx:
                src = I['ctx'][b] if l == 0 else xcs_d[b]
                dst = xcs_d[b]
            else:
                src = I['x'][b] if l == 0 else xs_d[b]
                dst = out[b] if last else xs_d[b]
            ctx_full = is_ctx and not last
            yc_d = ycat_d[L]
            mseq = A.mark()
            if dbg and dbg[0] == ('ycat', l, b, is_ctx):
                mz_ = A.mark()
                zt, ztn_ = A.alloc(L, BF16)
                MS('pool', zt, 0.0, [ztn_])
                for ch_ in range(8):
                    DMA(yc_d[ch_], zt, [ztn_], ['ycat_d'])
                A.release(mz_)
            hT, hTn = A.alloc(8 * L, BF16)
            hT3 = hT.rearrange("p (k t) -> p k t", k=8)
            wst = [A.alloc(1024) for _ in range(2)]

            def s1():
                m0 = A.mark()
                xb = [A.alloc(1024) for _ in range(2)]
                for t in range(NT):
                    xt, xn = xb[t % 2]
                    DMA(xt, src[t * 128:(t + 1) * 128, :], [], [xn])
                    for half in range(2):
                        bi = (2 * t + half) % 4
                        for kk_ in range(4):
                            k = half * 4 + kk_
                            TR(ps[bi][:, kk_ * 128:(kk_ + 1) * 128], xt[:, k * 128:(k + 1) * 128], 128, [xn], [psn[bi]])
                        for kk_ in range(4):
                            k = half * 4 + kk_
                            ACT(hT3[:, k, t * 128:(t + 1) * 128], ps[bi][:, kk_ * 128:(kk_ + 1) * 128], AF.Identity, [psn[bi], 'modT', 'modT1'], [hTn],
                                scale=mT14[:, l, 8 + k, v:v + 1], bias=mT4[:, l, k, v:v + 1])
                P.barrier()
                A.release(m0)
            s1()

            def conformer():
                m0 = A.mark()
                wc, wcn = A.alloc(8 * 512, BF16)
                wc3 = wc.rearrange("p (k c) -> p k c", k=8)
                load_w(wc3, wcn, I['w_in'][l][:, 0:512], 512, wst)
                cw3 = cw[:].rearrange("p (c k) -> p c k", c=2); cv33 = cv3[:].rearrange("p (c k) -> p c k", c=2)
                sg = [A.alloc(CH) for _ in range(2)]
                tm = [A.alloc(CH) for _ in range(4)]
                for cc in range(2):
                    m1 = A.mark()
                    up, upn = A.alloc(L + 30); acc, an = A.alloc(L); yo, yon = A.alloc(L, BF16)
                    MS('pool', up[:, 0:15], 0.0, [upn]); MS('pool', up[:, L + 15:L + 30], 0.0, [upn])
                    for q in range(NQ):
                        cs = slice(q * CH, (q + 1) * CH)
                        pa = ps[q % 2]; pg = ps[2 + q % 2]
                        for k in range(8):
                            MM(pa[:, 0:CH], wc3[:, k, cc * 128:(cc + 1) * 128], hT3[:, k, cs], k == 0, k == 7, [wcn, hTn], [psn[q % 2]])
                        for k in range(8):
                            MM(pg[:, 0:CH], wc3[:, k, 256 + cc * 128:256 + (cc + 1) * 128], hT3[:, k, cs], k == 0, k == 7, [wcn, hTn], [psn[2 + q % 2]])
                        s_, sn_ = sg[q % 2]
                        ACT(s_, pg[:, 0:CH], AF.Sigmoid, [psn[2 + q % 2]], [sn_])
                        TT('dve', up[:, 15 + q * CH:15 + (q + 1) * CH], pa[:, 0:CH], s_, ALU.mult, [psn[q % 2], sn_], [upn])
                    TS('dve', acc, up[:, 0:L], cw3[:, cc, 0:1], cv33[:, cc, 0:1], ALU.mult, ALU.add, [upn, 'cw', 'cv3'], [an])
                    for k in range(1, 31):
                        STT(acc, up[:, k:k + L], cw3[:, cc, k:k + 1], acc, ALU.mult, ALU.add, [upn, 'cw', an], [an])
                    for q in range(NQ):
                        cs = slice(q * CH, (q + 1) * CH)
                        pm = ps[4 + q % 2]; pv = ps[6 + q % 2]
                        cen, cn_ = tm[0]; sq_, sqn = tm[1]; sd_, sdn = tm[2]; un_, unn = tm[3]
                        MM(pm[:, 0:CH], blk[:], acc[:, cs], True, True, ['blk', an], [psn[4 + q % 2]])
                        TT('dve', cen, acc[:, cs], pm[:, 0:CH], ALU.subtract, [an, psn[4 + q % 2]], [cn_])
                        ACT(sq_, cen, AF.Square, [cn_], [sqn])
                        MM(pv[:, 0:CH], blk[:], sq_, True, True, ['blk', sqn], [psn[6 + q % 2]])
                        ACT(sd_, pv[:, 0:CH], AF.Sqrt, [psn[6 + q % 2]], [sdn], bias=EPS)
                        RCP(sd_, sd_, [sdn], [sdn])
                        TT('dve', un_, cen, sd_, ALU.mult, [cn_, sdn], [unn])
                        ACT(yo[:, cs], un_, AF.Silu, [unn, 'cv3'], [yon], scale=cv33[:, cc, 1:2], bias=cv33[:, cc, 2:3])
                    DMA(yc_d[cc], yo, [yon], ['ycat_d'])
                    A.release(m1)
                P.barrier()
                A.release(m0)

            def attention():
                m0 = A.mark()
                wa, wan = A.alloc(8 * 768, BF16)
                wa3 = wa.rearrange("p (k c) -> p k c", k=8)
                load_w(wa3, wan, I['w_in'][l][:, 512:1280], 768, wst)
                vc14 = vc1[:].rearrange("p (t h e) -> p t h e", t=2, h=4)
                kcT3 = kcT[:].rearrange("p (h t) -> p h t", h=4)
                if is_ctx:
                    V14 = vc14; Vn = 'vc1'
                else:
                    V1, Vn = A.alloc(NT * 4 * 65, BF16)
                    V14 = V1.rearrange("p (t h e) -> p t h e", t=NT, h=4)
                MS('pool', V14, 1.0, [Vn])
                for t in range(NT):
                    pv = ps[t % 2]
                    for k in range(8):
                        MM(pv[:, 0:256], hT3[:, k, t * 128:(t + 1) * 128], wa3[:, k, 512:768], k == 0, k == 7, [hTn, wan], [psn[t % 2]])
                    CP('act', V14[:, t, :, 0:64], pv[:, 0:256].rearrange("p (h e) -> p h e", h=4), [psn[t % 2]], [Vn])
                CK('attn_V')
                need_y = (not is_ctx) or ctx_full
                if not is_ctx:
                    rps = [A.alloc(2 * CH, parts=64) for _ in range(2)]
                    mk, mkn = A.alloc(9 * 896, BF16)
                    mk3 = mk.rearrange("p (i c) -> p i c", i=9)
                    DMA(mk3, I['namask'], [], [mkn])
                    PT, PTn = A.alloc(NT * 896, BF16)
                    def keytiles(j):
                        lo = min(max(2 * j - 4, 0), 24); hi = min(max(2 * j + 1 - 4, 0), 24) + 7
                        return list(range(lo // 2, hi // 2 + 1))
                    mcls = lambda i: i if i < 4 else (4 if i <= 11 else i - 7)
                    PT3 = PT.rearrange("p (i c) -> p i c", i=NT)
                    qrt, qrn = A.alloc(L, BF16, parts=64); krt, krn = A.alloc(L, BF16, parts=64)
                    t1s = [A.alloc(CH, parts=64) for _ in range(2)]; t2s = [A.alloc(CH, parts=64) for _ in range(2)]
                    et = [A.alloc(384) for _ in range(2)]; et2 = [A.alloc(384) for _ in range(2)]
                if need_y:
                    yb, ybn = A.alloc(NT * 256)
                    yb3 = yb.rearrange("p (t c) -> p t c", c=256)
                    qpl, qpn = A.alloc(L, BF16, parts=64)
                    PcT, PcTn = A.alloc(2 * L, BF16)
                    PcT3 = PcT.rearrange("p (c t) -> p c t", c=2)
                    rz = [A.alloc(1) for _ in range(2)]
                bK3 = biasK[:].rearrange("p (h c) -> p h c", h=4)
                for h in range(4):
                    for q in range(NQ):
                        cs = slice(q * CH, (q + 1) * CH)
                        if need_y:
                            for k in range(8):
                                MM(ps[2][0:64, 0:CH], wa3[:, k, h * 64:(h + 1) * 64], hT3[:, k, cs], k == 0, k == 7, [wan, hTn], [psn[2]])
                            CP('act', qpl[:, cs], ps[2][0:64, 0:CH], [psn[2]], [qpn])
                        if not is_ctx:
                            rp, rpn = rps[q % 2]
                            rp3 = rp.rearrange("p (a t) -> p a t", a=2)
                            DMA(rp3[:, 0, :], I['rope'][0][:, cs], [], [rpn]); DMA(rp3[:, 1, :], I['rope'][1][:, cs], [], [rpn])
                            if plan.get('fine'): CK('f_q')
                            for k in range(8):
                                MM(ps[3][0:32, 0:CH], wa3[:, k, h * 64 + 32:h * 64 + 64], hT3[:, k, cs], k == 0, k == 7, [wan, hTn], [psn[3]])
                            if plan.get('fine'): CK('f_sw1')
                            for k in range(8):
                                MM(ps[3][32:64, 0:CH], wa3[:, k, h * 64:h * 64 + 32], hT3[:, k, cs], k == 0, k == 7, [wan, hTn], [psn[3]])
                            if plan.get('fine'): CK('f_sw2')
                            t1, t1n = t1s[0]; t2, t2n = t2s[0]
                            TT('dve', t1, ps[2][0:64, 0:CH], rp3[:, 0, :], ALU.mult, [psn[2], rpn], [t1n])
                            if plan.get('fine'): CK('f_t1')
                            TT('dve', t2, ps[3][0:64, 0:CH], rp3[:, 1, :], ALU.mult, [psn[3], rpn], [t2n])
                            if plan.get('fine'): CK('f_t2')
                            TT('pool', qrt[:, cs], t1, t2, ALU.add, [t1n, t2n], [qrn])
                            if plan.get('fine'): CK('f_add')
                        for k in range(8):
                            MM(ps[4][0:64, 0:CH], wa3[:, k, 256 + h * 64:256 + (h + 1) * 64], hT3[:, k, cs], k == 0, k == 7, [wan, hTn], [psn[4]])
                        if is_ctx:
                            CP('act', kcT3[:, h, cs], ps[4][0:64, 0:CH], [psn[4]], ['kcT'])
                        else:
                            for k in range(8):
                                MM(ps[5][0:32, 0:CH], wa3[:, k, 256 + h * 64 + 32:256 + h * 64 + 64], hT3[:, k, cs], k == 0, k == 7, [wan, hTn], [psn[5]])
                            for k in range(8):
                                MM(ps[5][32:64, 0:CH], wa3[:, k, 256 + h * 64:256 + h * 64 + 32], hT3[:, k, cs], k == 0, k == 7, [wan, hTn], [psn[5]])
                            t1, t1n = t1s[1]; t2, t2n = t2s[1]
                            TT('dve', t1, ps[4][0:64, 0:CH], rp3[:, 0, :], ALU.mult, [psn[4], rpn], [t1n])
                            TT('dve', t2, ps[5][0:64, 0:CH], rp3[:, 1, :], ALU.mult, [psn[5], rpn], [t2n])
                            TT('pool', krt[:, cs], t1, t2, ALU.add, [t1n, t2n], [krn])
                    CK('attn_qk%d' % h)
                    if not need_y:
                        continue
                    if not is_ctx:
                        it = 0
                        for i in range(NT):
                            js = [j for j in range(NT) if i in keytiles(j)]
                            runs = [js[a_:a_ + 3] for a_ in range(0, len(js), 3)]
                            for run in runs:
                                ja, jb = run[0], run[-1]
                                n = (jb - ja + 1) * 128; c0 = (ja - i + 3) * 128
                                bi = 6 + it % 2
                                e1, e1n = et[it % 2]; e2, e2n = et2[it % 2]; it += 1
                                MM(ps[bi][:, 0:n], krt[:, i * 128:(i + 1) * 128], qrt[:, ja * 128:(jb + 1) * 128], True, True, [krn, qrn], [psn[bi]])
                                STT(e1[:, 0:n], ps[bi][:, 0:n], 0.125, bK3[:, h, c0:c0 + n], ALU.mult, ALU.add, [psn[bi], 'biasK'], [e1n])
                                ACT(e2[:, 0:n], e1[:, 0:n], AF.Exp, [e1n], [e2n])
                                TT('pool', PT3[:, i, c0:c0 + n], e2[:, 0:n], mk3[:, mcls(i), c0:c0 + n], ALU.mult, [e2n, mkn], [PTn])
                    for q in range(NQ):
                        cs = slice(q * CH, (q + 1) * CH)
                        for ct in range(2):
                            bi = (q * 2 + ct) % 2
                            MM(ps[bi][:, 0:CH], kcT3[:, h, ct * 128:(ct + 1) * 128], qpl[:, cs], True, True, ['kcT', qpn], [psn[bi]])
                            ACT(PcT3[:, ct, cs], ps[bi][:, 0:CH], AF.Exp, [psn[bi]], [PcTn], scale=0.125)
                    CK('attn_sc%d' % h)
                    for j in range(NT):
                        bi = 2 + j % 2
                        mms = []
                        if not is_ctx:
                            for i in keytiles(j):
                                mms.append((PT3[:, i, (j - i + 3) * 128:(j - i + 4) * 128], V14[:, i, h, :], [PTn, Vn]))
                        for ct in range(2):
                            mms.append((PcT3[:, ct, j * 128:(j + 1) * 128], vc14[:, ct, h, :], [PcTn, 'vc1']))
                        for ii, (lt, rh, rr) in enumerate(mms):
                            MM(ps[bi][:, 0:65], lt, rh, ii == 0, ii == len(mms) - 1, rr, [psn[bi]])
                        r_, rn_ = rz[j % 2]
                        RCP(r_, ps[bi][:, 64:65], [psn[bi]], [rn_])
                        ACT(yb3[:, j, h * 64:(h + 1) * 64], ps[bi][:, 0:64], AF.Identity, [psn[bi], rn_], [ybn], scale=r_)
                CK('attn_pv')
                if need_y:
                    yo, yon = A.alloc(2 * L, BF16)
                    yo3 = yo.rearrange("p (c t) -> p c t", c=2)
                    for j in range(NT):
                        for cc in range(2):
                            bi = 4 + (2 * j + cc) % 4
                            TR(ps[bi][:, 0:128], yb3[:, j, cc * 128:(cc + 1) * 128], 128, [ybn], [psn[bi]])
                            CP('dve' if cc else 'act', yo3[:, cc, j * 128:(j + 1) * 128], ps[bi][:, 0:128], [psn[bi]], [yon])
                    CK('attn_tr')
                    for cc in range(2):
                        DMA(yc_d[2 + cc], yo3[:, cc, :], [yon], ['ycat_d'])
                    CK('attn_out')
                P.barrier()
                A.release(m0)

            def hyena():
                NFp = L + 128; NFT = NFp // 128
                m0 = A.mark()
                wh, whn = A.alloc(8 * 768, BF16)
                wh3 = wh.rearrange("p (k c) -> p k c", k=8)
                load_w(wh3, whn, I['w_in'][l][:, 1280:2048], 768, wst)
                hsp3 = hsp[:].rearrange("p (c k) -> p c k", c=6)
                invv = I['inv%d' % L].rearrange("a f t -> (a f) t")
                TH = min(L, 1024); nbk = TH // CH
                for cc in range(2):
                    m1 = A.mark()
                    xc0 = A.alloc(L); xc2 = A.alloc(L)
                    ms_ = A.mark()
                    xc1 = A.alloc(L)
                    xc = [xc0, xc1, xc2]
                    pad, padn = A.alloc(L + 2)
                    MS('pool', pad[:, 0:1], 0.0, [padn]); MS('pool', pad[:, L + 1:L + 2], 0.0, [padn])
                    for g in range(3):
                        ci = g * 2 + cc
                        for q in range(NQ):
                            pp = ps[q % 2]
                            for k in range(8):
                                MM(pp[:, 0:CH], wh3[:, k, g * 256 + cc * 128:g * 256 + (cc + 1) * 128], hT3[:, k, q * CH:(q + 1) * CH], k == 0, k == 7, [whn, hTn], [psn[q % 2]])
                            CP('act', pad[:, 1 + q * CH:1 + (q + 1) * CH], pp[:, 0:CH], [psn[q % 2]], [padn])
                        x_, xn_ = xc[g]
                        TS('dve', x_, pad[:, 0:L], hsp3[:, ci, 0:1], hsp3[:, ci, 3:4], ALU.mult, ALU.add, [padn, 'hsp'], [xn_])
                        STT(x_, pad[:, 1:L + 1], hsp3[:, ci, 1:2], x_, ALU.mult, ALU.add, [padn, 'hsp', xn_], [xn_])
                        STT(x_, pad[:, 2:L + 2], hsp3[:, ci, 2:3], x_, ALU.mult, ALU.add, [padn, 'hsp', xn_], [xn_])
                    u, un = xc[2]
                    TT('dve', u, xc[2][0], xc[1][0], ALU.mult, [xc[2][1], xc[1][1]], [un])
                    A.release(ms_)
                    Utm, Utn = A.alloc(NT * 128, BF16)
                    Ut3 = Utm.rearrange("p (t c) -> p t c", c=128)
                    for t in range(NT):
                        bi = 2 + t % 2
                        TR(ps[bi][:, 0:128], u[:, t * 128:(t + 1) * 128], 128, [un], [psn[bi]])
                        CP('act' if t % 2 else 'dve', Ut3[:, t, :], ps[bi][:, 0:128], [psn[bi]], [Utn])
                    Uf, Ufn = A.alloc(2 * NFp)
                    m2 = A.mark()
                    tbs = [A.alloc(NT * 512, BF16) for _ in range(2)]
                    it = 0
                    for c0 in range(0, 2 * NFp, 512):
                        n = min(512, 2 * NFp - c0)
                        tb, tbn = tbs[it % 2]
                        tb3 = tb.rearrange("p (t c) -> p t c", c=512)
                        DMA(tb, I['fw%d' % L][c0 // 512].rearrange("p t c -> p (t c)"), [], [tbn])
                        bi = 4 + it % 2; it += 1
                        for t in range(NT):
                            MM(ps[bi][:, 0:n], Ut3[:, t, :], tb3[:, t, 0:n], t == 0, t == NT - 1, [Utn, tbn], [psn[bi]])
                        CP('act', Uf[:, c0:c0 + n], ps[bi][:, 0:n], [psn[bi]], [Ufn])
                    A.release(m2)
                    Kf, Kfn = A.alloc(2 * NFp)
                    DMA(Kf[:, 0:NFp], ksp_d[L][0, cc * 128:(cc + 1) * 128, :], ['ksp%d' % L], [Kfn])
                    DMA(Kf[:, NFp:2 * NFp], ksp_d[L][1, cc * 128:(cc + 1) * 128, :], ['ksp%d' % L], [Kfn])
                    Yf, Yfn = A.alloc(2 * NFp)
                    ta, tan = A.alloc(NFp); tb_, tbn_ = A.alloc(NFp)
                    Uc = Uf[:, 0:NFp]; Us = Uf[:, NFp:2 * NFp]; Kr = Kf[:, 0:NFp]; Ki = Kf[:, NFp:2 * NFp]
                    TT('dve', ta, Uc, Kr, ALU.mult, [Ufn, Kfn], [tan]); TT('pool', tb_, Us, Ki, ALU.mult, [Ufn, Kfn], [tbn_])
                    TT('dve', Yf[:, 0:NFp], ta, tb_, ALU.add, [tan, tbn_], [Yfn])
                    TT('dve', ta, Uc, Ki, ALU.mult, [Ufn, Kfn], [tan]); TT('pool', tb_, Us, Kr, ALU.mult, [Ufn, Kfn], [tbn_])
                    TT('dve', Yf[:, NFp:2 * NFp], ta, tb_, ALU.subtract, [tan, tbn_], [Yfn])
                    YT, YTn = A.alloc(2 * NFT * 128, BF16)
                    YT3 = YT.rearrange("p (f c) -> p f c", c=128)
                    for ft in range(2 * NFT):
                        bi = 2 + ft % 2
                        TR(ps[bi][:, 0:128], Yf[:, ft * 128:(ft + 1) * 128], 128, [Yfn], [psn[bi]])
                        CP('act' if ft % 2 else 'dve', YT3[:, ft, :], ps[bi][:, 0:128], [psn[bi]], [YTn])
                    ibs = [A.alloc(TH, BF16) for _ in range(4)]
                    yo, yon = A.alloc(L, BF16)
                    tq = [A.alloc(CH) for _ in range(2)]
                    for th in range(L // TH):
                        for ft in range(2 * NFT):
                            ib, ibn = ibs[ft % 4]
                            DMA(ib, invv[ft * 128:(ft + 1) * 128, th * TH:(th + 1) * TH], [], [ibn])
                            for bq in range(nbk):
                                MM(ps[4 + bq][:, 0:CH], YT3[:, ft, :], ib[:, bq * CH:(bq + 1) * CH], ft == 0, ft == 2 * NFT - 1, [YTn, ibn], [psn[4 + bq]])
                        for bq in range(nbk):
                            cs = slice(th * TH + bq * CH, th * TH + (bq + 1) * CH)
                            t_, tn_ = tq[bq % 2]
                            STT(t_, u[:, cs], hskip[:, cc:cc + 1], ps[4 + bq][:, 0:CH], ALU.mult, ALU.add, [un, 'hskip', psn[4 + bq]], [tn_])
                            TT('dve', yo[:, cs], t_, xc[0][0][:, cs], ALU.mult, [tn_, xc[0][1]], [yon])
                    DMA(yc_d[4 + cc], yo, [yon], ['ycat_d'])
                    P.barrier()
                    A.release(m1)
                A.release(m0)

            def ssd():
                want_y = (not is_ctx) or ctx_full
                m0 = A.mark()
                wz, wzn = A.alloc(8 * 264, BF16); wx, wxn = A.alloc(8 * 512, BF16)
                wz3 = wz.rearrange("p (k c) -> p k c", k=8); wx3 = wx.rearrange("p (k c) -> p k c", k=8)
                load_w(wz3, wzn, I['w_in'][l][:, 2048:2304], 256, wst)
                for k in range(8):
                    st, sn = wst[k % 2]
                    DMA(st[:, 0:8], I['w_in'][l][k * 128:(k + 1) * 128, 2816:2824], [], [sn])
                    CP('pool', wz3[:, k, 256:264], st[:, 0:8], [sn], [wzn])
                load_w(wx3, wxn, I['w_in'][l][:, 2304:2816], 512, wst)
                ztm, ztn = A.alloc(NT * 256); dta, dtn = A.alloc(NT * 8); aal, aan = A.alloc(NT * 8)
                z3 = ztm.rearrange("p (t c) -> p t c", c=256); dt3 = dta.rearrange("p (t c) -> p t c", c=8); a3 = aal.rearrange("p (t c) -> p t c", c=8)
                for t in range(NT):
                    pz = ps[t % 2]
                    for k in range(8):
                        MM(pz[:, 0:264], hT3[:, k, t * 128:(t + 1) * 128], wz3[:, k, :], k == 0, k == 7, [hTn, wzn], [psn[t % 2]])
                    CP('act', z3[:, t, :], pz[:, 0:256], [psn[t % 2]], [ztn])
                    TT('dve', dt3[:, t, :], pz[:, 256:264], dtb_bc[:], ALU.add, [psn[t % 2], 'dtb_bc'], [dtn])
                ACT(dta, dta, AF.Exp, [dtn], [dtn])
                ACT(dta, dta, AF.Ln, [dtn], [dtn], bias=1.0)
                TT('dve', a3, dt3, A_bc[:].unsqueeze(1).to_broadcast([128, NT, 8]), ALU.mult, [dtn, 'A_bc'], [aan])
                xtm, xtn = A.alloc(NT * 256); x3 = xtm.rearrange("p (t c) -> p t c", c=256)
                Btm, Btn = A.alloc(NT * 128, BF16); B3 = Btm.rearrange("p (t c) -> p t c", c=128)
                BTb, BTn = A.alloc(2 * L, BF16, parts=64); CTb, CTn = A.alloc(2 * L, BF16, parts=64)
                BT3 = BTb.rearrange("p (g t) -> p g t", g=2); CT3 = CTb.rearrange("p (g t) -> p g t", g=2)
                sxp3 = sxp[:].rearrange("p (c k) -> p c k", c=2); sbp3 = sbp[:].rearrange("p (c k) -> p c k", c=4)
                m1 = A.mark()
                pad, padn = A.alloc(L + 2); cv, cvn = A.alloc(L); sx, sxn = A.alloc(L)
                MS('pool', pad[:, 0:1], 0.0, [padn]); MS('pool', pad[:, L + 1:L + 2], 0.0, [padn])
                chunks = [('x', 0, 128, 0), ('x', 1, 128, 128), ('B', 0, 64, 256), ('B', 1, 64, 320), ('C', 0, 64, 384), ('C', 1, 64, 448)]
                for (kind, idx, Pn, col0) in chunks:
                    for q in range(NQ):
                        pp = ps[2 + q % 2]
                        for k in range(8):
                            MM(pp[0:Pn, 0:CH], wx3[:, k, col0:col0 + Pn], hT3[:, k, q * CH:(q + 1) * CH], k == 0, k == 7, [wxn, hTn], [psn[2 + q % 2]])
                        CP('act', pad[0:Pn, 1 + q * CH:1 + (q + 1) * CH], pp[0:Pn, 0:CH], [psn[2 + q % 2]], [padn])
                    if kind == 'x':
                        wv = sxp3[:, idx, :]; wname = 'sxp'
                    else:
                        wv = sbp3[:, (0 if kind == 'B' else 2) + idx, :]; wname = 'sbp'
                    TS('dve', cv[0:Pn, :], pad[0:Pn, 0:L], wv[0:Pn, 0:1], wv[0:Pn, 3:4], ALU.mult, ALU.add, [padn, wname], [cvn])
                    STT(cv[0:Pn, :], pad[0:Pn, 1:L + 1], wv[0:Pn, 1:2], cv[0:Pn, :], ALU.mult, ALU.add, [padn, wname, cvn], [cvn])
                    STT(cv[0:Pn, :], pad[0:Pn, 2:L + 2], wv[0:Pn, 2:3], cv[0:Pn, :], ALU.mult, ALU.add, [padn, wname, cvn], [cvn])
                    if kind == 'C':
                        ACT(CT3[:, idx, :], cv[0:64, :], AF.Silu, [cvn], [CTn])
                        continue
                    ACT(sx[0:Pn, :], cv[0:Pn, :], AF.Silu, [cvn], [sxn])
                    if kind == 'B':
                        CP('pool', BT3[:, idx, :], sx[0:64, :], [sxn], [BTn])
                    for t in range(NT):
                        bi = 4 + t % 2
                        TR(ps[bi][:, 0:Pn], sx[0:Pn, t * 128:(t + 1) * 128], Pn, [sxn], [psn[bi]])
                        if kind == 'x':
                            CP('act' if t % 2 else 'dve', x3[:, t, idx * 128:(idx + 1) * 128], ps[bi][:, 0:128], [psn[bi]], [xtn])
                        else:
                            CP('act' if t % 2 else 'dve', B3[:, t, idx * 64:(idx + 1) * 64], ps[bi][:, 0:64], [psn[bi]], [Btn])
                P.barrier()
                A.release(m1)
                xd, xdn = A.alloc(NT * 512, BF16)
                xd5 = xd.rearrange("p (t d h e) -> p t d h e", d=2, h=4, e=64)
                x4 = xtm.rearrange("p (t h e) -> p t h e", h=4, e=64)
                for d in range(2):
                    TT('dve', xd5[:, :, d], x4, dt3[:, :, d * 4:(d + 1) * 4].unsqueeze(3).to_broadcast([128, NT, 4, 64]), ALU.mult, [xtn, dtn], [xdn])
                if want_y:
                    yal, yaln = A.alloc(NT * 256)
                    y4 = yal.rearrange("p (t h e) -> p t h e", h=4, e=64)
                    for t in range(NT):
                        TT('pool', y4[:, t], x4[:, t], dsk_bc[:].unsqueeze(2).to_broadcast([128, 4, 64]), ALU.mult, [xtn, 'dsk_bc'], [yaln])
                S3 = Sst[:].rearrange("p (i e) -> p i e", i=8); SB3 = SstB[:].rearrange("p (i e) -> p i e", i=8)
                if is_ctx:
                    MS('dve', Sst[:], 0.0, ['Sst%d' % i for i in range(8)])
                    MS('dve', SstB[:], 0.0, ['SstB%d' % i for i in range(8)])
                cum = [A.alloc(24) for _ in range(2)]
                GTs = [A.alloc(128) for _ in range(2)]
                abcs = [A.alloc(128) for _ in range(2)]; Lts = [A.alloc(128) for _ in range(2)]
                MTs = [A.alloc(128, BF16) for _ in range(2)]
                yds = [A.alloc(64) for _ in range(2)]; yd2s = [A.alloc(64) for _ in range(2)]
                xdds = [A.alloc(64, BF16) for _ in range(2)]
                ssdc3_ = ssdc[:].rearrange("p (a b) -> p a b", a=4)
                it = 0
                for d in range(2):
                    order = list(range(NT)) if d == 0 else list(range(NT - 1, -1, -1))
                    U = ssdc3_[:, d, :]; MSK = ssdc3_[:, 2 + d, :]
                    for c in order:
                        tok = slice(c * 128, (c + 1) * 128)
                        cm, cmn = cum[it % 2]
                        MM(ps[0][:, 0:4], U, a3[:, c, d * 4:(d + 1) * 4], True, True, ['ssdc', aan], [psn[0]])
                        MM(ps[0][:, 8:12], ones[:], a3[:, c, d * 4:(d + 1) * 4], True, True, ['ones', aan], [psn[0]])
                        CP('dve', cm[:, 0:4], ps[0][:, 0:4], [psn[0]], [cmn])
                        CP('dve', cm[:, 4:8], ps[0][:, 8:12], [psn[0]], [cmn])
                        TS('dve', cm[:, 8:12], cm[:, 0:4], -1.0, None, ALU.mult, None, [cmn], [cmn])
                        TT('dve', cm[:, 16:20], cm[:, 4:8], cm[:, 0:4], ALU.subtract, [cmn], [cmn])
                        ACT(cm[:, 12:16], cm[:, 0:4], AF.Exp, [cmn], [cmn])
                        ACT(cm[:, 16:20], cm[:, 16:20], AF.Exp, [cmn], [cmn])
                        ACT(cm[:, 20:24], cm[:, 4:8], AF.Exp, [cmn], [cmn])
                        for g in range(2):
                            GT, GTn = GTs[g]
                            MM(ps[1][:, 0:128], BT3[:, g, tok], CT3[:, g, tok], True, True, [BTn, CTn], [psn[1]])
                            CP('act', GT, ps[1][:, 0:128], [psn[1]], [GTn])
                            for hh in range(2):
                                h = g * 2 + hh; si = d * 4 + h
                                abc, abcn = abcs[hh]; Lt, Ltn = Lts[hh]; MT, MTn = MTs[hh]
                                yd, ydn = yds[hh]; yd2, yd2n = yd2s[hh]; xdd, xddn = xdds[hh]
                                bR = 2 + hh; bY = 4 + hh; bS = 6 + hh
                                TS('pool', abc, ones[:], a3[:, c, si:si + 1], None, ALU.mult, None, ['ones', aan], [abcn])
                                MM(ps[bR][:, 0:128], abc, U, True, False, [abcn, 'ssdc'], [psn[bR]])
                                MM(ps[bR][:, 0:128], ident[:], MSK, False, True, ['ident', 'ssdc'], [psn[bR]])
                                if d == 0:
                                    ACT(Lt, ps[bR][:, 0:128], AF.Exp, [psn[bR], cmn], [Ltn], bias=cm[:, 8 + h:9 + h], scale=1.0)
                                else:
                                    ACT(Lt, ps[bR][:, 0:128], AF.Exp, [psn[bR], cmn], [Ltn], bias=cm[:, h:h + 1], scale=-1.0)
                                if want_y:
                                    TT('dve', MT, GT, Lt, ALU.mult, [GTn, Ltn], [MTn])
                                    MM(ps[bY][:, 0:64], MT, xd5[:, c, d, h, :], True, True, [MTn, xdn], [psn[bY]])
                                    MM(ps[bY][:, 64:128], CT3[:, g, tok], SB3[:, si, :], True, True, [CTn, 'SstB%d' % si], [psn[bY]])
                                    CP('act', yd, ps[bY][:, 0:64], [psn[bY]], [ydn])
                                    ecol = cm[:, 12 + h:13 + h] if d == 0 else cm[:, 16 + h:17 + h]
                                    STT(yd2, ps[bY][:, 64:128], ecol, yd, ALU.mult, ALU.add, [psn[bY], cmn, ydn], [yd2n])
                                    TT('pool', y4[:, c, h, :], y4[:, c, h, :], yd2, ALU.add, [yaln, yd2n], [yaln])
                                dcol = cm[:, 16 + h:17 + h] if d == 0 else cm[:, 12 + h:13 + h]
                                TS('pool', xdd, xd5[:, c, d, h, :], dcol, None, ALU.mult, None, [xdn, cmn], [xddn])
                                MM(ps[bS][0:64, 0:64], B3[:, c, g * 64:(g + 1) * 64], xdd, True, True, [Btn, xddn], [psn[bS]])
                                STT(S3[:, si, :], S3[:, si, :], cm[0:64, 20 + h:21 + h], ps[bS][0:64, 0:64], ALU.mult, ALU.add,
                                    ['Sst%d' % si, cmn, psn[bS]], ['Sst%d' % si])
                                CP('act', SB3[:, si, :], S3[:, si, :], ['Sst%d' % si], ['SstB%d' % si])
                        it += 1
                if dbg and dbg[0] == 'sst' and is_ctx:
                    DMA(dbg_out, Sst[:], ['Sst%d' % i for i in range(8)], ['dbg'], q='sp')
                if want_y:
                    szt, szn = A.alloc(NT * 256)
                    ACT(szt, ztm, AF.Silu, [ztn], [szn])
                    TT('dve', yal, yal, szt, ALU.mult, [yaln, szn], [yaln])
                    ACT(szt, yal, AF.Square, [yaln], [szn])
                    ssq, ssqn = A.alloc(NT * 2)
                    P.op('dve', lambda e: e.reduce_sum(out=ssq, in_=szt.rearrange("p (a c) -> p a c", c=128), axis=AX.X), r=[szn], w=[ssqn])
                    ACT(ssq, ssq, AF.Sqrt, [ssqn], [ssqn], bias=EPS, scale=1.0 / 128)
                    RCP(ssq, ssq, [ssqn], [ssqn])
                    TT('dve', yal.rearrange("p (a c) -> p a c", c=128), yal.rearrange("p (a c) -> p a c", c=128),
                       ssq.unsqueeze(2).to_broadcast([128, NT * 2, 128]), ALU.mult, [yaln, ssqn], [yaln])
                    y3 = yal.rearrange("p (t c) -> p t c", c=256)
                    TT('dve', y3, y3, ng_bc[:].unsqueeze(1).to_broadcast([128, NT, 256]), ALU.mult, [yaln, 'ng_bc'], [yaln])
                    yo, yon = A.alloc(2 * L, BF16)
                    yo3 = yo.rearrange("p (c t) -> p c t", c=2)
                    for t in range(NT):
                        for cc in range(2):
                            bi = 4 + (2 * t + cc) % 4
                            TR(ps[bi][:, 0:128], y3[:, t, cc * 128:(cc + 1) * 128], 128, [yaln], [psn[bi]])
                            CP('dve' if cc else 'act', yo3[:, cc, t * 128:(t + 1) * 128], ps[bi][:, 0:128], [psn[bi]], [yon])
                    for cc in range(2):
                        DMA(yc_d[6 + cc], yo3[:, cc, :], [yon], ['ycat_d'])
                P.barrier()
                A.release(m0)

            if 'conf' in mixers and (not is_ctx or ctx_full):
                conformer()
            if 'attn' in mixers:
                attention()
            if 'hy' in mixers and (not is_ctx or ctx_full):
                hyena()
            if 'ssd' in mixers:
                ssd()
            P.barrier()
            A.release(mseq)
            if dbg and dbg[0] == ('ycat', l, b, is_ctx):
                DMA(dbg_out, yc_d, ['ycat_d'], ['dbg'], q='sp')
                P.barrier()
            if do_tail and (not is_ctx or ctx_full):
                tail(l, b, is_ctx, src, dst, L, v)

        def tail(l, b, is_ctx, src, dst, L, v):
            NT = L // 128
            yc_d = ycat_d[L]
            m0 = A.mark()
            wo, won = A.alloc(8 * 1024, BF16); wq, wqn = A.alloc(8 * 2048, BF16)
            wo3 = wo.rearrange("p (k c) -> p k c", k=8); wq3 = wq.rearrange("p (k c) -> p k c", k=8)
            mw = A.mark()
            wst = [A.alloc(1024) for _ in range(2)]
            load_w(wo3, won, I['w_out'][l], 1024, wst)
            load_w(wq3[:, :, 0:1024], wqn, I['peer_wq'][l][:, 0:1024], 1024, wst)
            load_w(wq3[:, :, 1024:2048], wqn, I['peer_wq'][l][:, 1024:2048], 1024, wst)
            A.release(mw)
            rows = {}
            for nm, srcrow in (('ln1g', I['ln1_g'][l:l + 1, :]), ('ln1b', I['ln1_b'][l:l + 1, :]), ('ln2g', I['ln2_g'][l:l + 1, :]), ('ln2b', I['ln2_b'][l:l + 1, :]),
                               ('g1', mod_d[l, v:v + 1, 2 * D:3 * D]), ('sh2', mod_d[l, v:v + 1, 3 * D:4 * D]),
                               ('sc2', mod_d[l, v:v + 1, 4 * D:5 * D]), ('g2', mod_d[l, v:v + 1, 5 * D:6 * D])):
                t_, n_ = A.alloc(1024)
                DMA(t_, srcrow.partition_broadcast(128), ['mod_d'], [n_])
                rows[nm] = (t_, n_)
            TS('dve', rows['sc2'][0], rows['sc2'][0], 1.0, None, ALU.add, None, [rows['sc2'][1]], [rows['sc2'][1]])
            CK('t_rows')
            keysT, kTn = A.alloc(16 * 128)
            kT3 = keysT.rearrange("p (c n) -> p c n", c=16)
            mk_ = A.mark()
            kst = [A.alloc(128) for _ in range(2)]
            for c in range(16):
                st, sn = kst[c % 2]
                DMA(st, I['peer_keys'][l, c // 2, c % 2], [], [sn])
                TR(ps[c % 2][:, 0:128], st, 128, [sn], [psn[c % 2]])
                CP('dve', kT3[:, c, :], ps[c % 2][:, 0:128], [psn[c % 2]], [kTn])
            A.release(mk_)
            ycb = [A.alloc(8 * 128, BF16) for _ in range(2)]
            xb = [A.alloc(1024) for _ in range(2)]
            tt_, ttn = A.alloc(1024); r1, r1n = A.alloc(1024); x1, x1n = A.alloc(1024); h2, h2n = A.alloc(1024)
            junk, jn = tt_, ttn
            st8, st8n = A.alloc(8)
            h2T, h2Tn = A.alloc(8 * 128, BF16); h2T3 = h2T.rearrange("p (k t) -> p k t", k=8)
            qT, qTn = A.alloc(16 * 128); qT3 = qT.rearrange("p (c t) -> p c t", c=16)
            sc, scn = A.alloc(16 * 128); sc_b, sc_bn = A.alloc(16 * 128)
            wk, wkn = A.alloc(256)
            m16, m16n = A.alloc(256); i16, i16n = A.alloc(256, U32); i16f, i16fn = A.alloc(256)
            m163 = m16.rearrange("p (c k) -> p c k", c=16); i163 = i16.rearrange("p (c k) -> p c k", c=16)
            cand, candn = A.alloc(2048); eid, eidn = A.alloc(2560)
            cv, cvn = A.alloc(128); ex, exn = A.alloc(128); gz, gzn = A.alloc(16)
            ci_, cin = A.alloc(128, U32); cif, cifn = A.alloc(128)
            iot, iotn = A.alloc(256)
            DMA(iot, I['iota256'].partition_broadcast(128), [], [iotn])
            esel, eseln = A.alloc(128); eseli, eselin = A.alloc(128, I32)
            if not DENSE:
                av, avn = A.alloc(128); wv, wvn = A.alloc(128)
                gb = [A.alloc(1024) for _ in range(3)]
                acc, accn = A.alloc(1024)
            else:
                sm, smn = A.alloc(3 * 128); sm3 = sm.rearrange("p (a t) -> p a t", a=3)
                i1i, i1in = A.alloc(128, I32); i1f, i1fn = A.alloc(128); i2f, i2fn = A.alloc(128); etmp, etn = A.alloc(128)
                af_, afn = A.alloc(128); bf_, bfn = A.alloc(128)

            def layernorm(xin, xinn, gname, bname, xout, xoutn):
                TS('dve', st8[:, 1:2], st8[:, 0:1], -1.0 / D, None, ALU.mult, None, [st8n], [st8n])
                ACT(junk, xin, AF.Square, [xinn, st8n], [jn, st8n], bias=st8[:, 1:2], accum=st8[:, 2:3])
                ACT(st8[:, 3:4], st8[:, 2:3], AF.Sqrt, [st8n], [st8n], bias=EPS, scale=1.0 / D)
                RCP(st8[:, 4:5], st8[:, 3:4], [st8n], [st8n])
                TS('dve', xout, xin, st8[:, 1:2], st8[:, 4:5], ALU.add, ALU.mult, [xinn, st8n], [xoutn])
                TT('dve', xout, xout, rows[gname][0], ALU.mult, [xoutn, rows[gname][1]], [xoutn])
                TT('dve', xout, xout, rows[bname][0], ALU.add, [xoutn, rows[bname][1]], [xoutn])

            scs = [(sc, scn), (sc_b, sc_bn)]
            def front(t):
                sc3 = scs[t % 2][0].rearrange("p (c n) -> p c n", c=16); scn = scs[t % 2][1]
                tok = slice(t * 128, (t + 1) * 128)
                yc, ycn = ycb[t % 2]; yc3 = yc.rearrange("p (k t) -> p k t", k=8)
                xt, xn = xb[t % 2]
                DMA(yc3, yc_d[:, :, tok].rearrange("k p t -> p k t"), ['ycat_d'], [ycn])
                DMA(xt, src[tok, :], [], [xn])
                for nh in range(2):
                    for k in range(8):
                        MM(ps[nh][:, :], yc3[:, k, :], wo3[:, k, nh * 512:(nh + 1) * 512], k == 0, k == 7, [ycn, won], [psn[nh]])
                    TT('dve', tt_[:, nh * 512:(nh + 1) * 512], ps[nh][:, :], rows['g1'][0][:, nh * 512:(nh + 1) * 512], ALU.mult, [psn[nh], rows['g1'][1]], [ttn])
                STT(r1, xt, ALPHA, tt_, ALU.mult, ALU.add, [xn, ttn], [r1n, st8n], accum=st8[:, 0:1])
                layernorm(r1, r1n, 'ln1g', 'ln1b', x1, x1n)
                CK('t_ln1')
                TT('dve', h2, x1, rows['sc2'][0], ALU.mult, [x1n, rows['sc2'][1]], [h2n])
                TT('dve', h2, h2, rows['sh2'][0], ALU.add, [h2n, rows['sh2'][1]], [h2n])
                for half in range(2):
                    bi = 2 + half
                    for kk_ in range(4):
                        k = half * 4 + kk_
                        TR(ps[bi][:, kk_ * 128:(kk_ + 1) * 128], h2[:, k * 128:(k + 1) * 128], 128, [h2n], [psn[bi]])
                    CP('act', h2T3[:, half * 4:(half + 1) * 4, :], ps[bi][:, :].rearrange("p (k t) -> p k t", k=4), [psn[bi]], [h2Tn])
                CK('t_h2T')
                if DENSE:
                    DMA(x1_d[tok, :], x1, [x1n], ['x1_d'], q='sp')
                    DMA(h2T_d[:, :, tok].rearrange("k p t -> p k t"), h2T3, [h2Tn], ['h2T_d'], q='sp')
                for cq in range(4):
                    bi = 4 + cq % 2
                    for ci in range(4):
                        c = cq * 4 + ci
                        for k in range(8):
                            MM(ps[bi][:, ci * 128:(ci + 1) * 128], wq3[:, k, c * 128:(c + 1) * 128], h2T3[:, k, :], k == 0, k == 7, [wqn, h2Tn], [psn[bi]])
                    CP('act', qT3[:, cq * 4:(cq + 1) * 4, :], ps[bi][:, :].rearrange("p (c t) -> p c t", c=4), [psn[bi]], [qTn])
                CK('t_qT')
                for cq in range(4):
                    bi = 6 + cq % 2
                    for ci in range(4):
                        c = cq * 4 + ci
                        MM(ps[bi][:, ci * 128:(ci + 1) * 128], qT3[:, c, :], kT3[:, c, :], True, True, [qTn, kTn], [psn[bi]])
                    CP('act', sc3[:, cq * 4:(cq + 1) * 4, :], ps[bi][:, :].rearrange("p (c n) -> p c n", c=4), [psn[bi]], [scn])
                CK('t_sc')
            def back(t):
                tok = slice(t * 128, (t + 1) * 128)
                sc3 = scs[t % 2][0].rearrange("p (c n) -> p c n", c=16); scn = scs[t % 2][1]
                m16c = ['%s_%d' % (m16n, c) for c in range(16)]; i16c = ['%s_%d' % (i16n, c) for c in range(16)]
                wks = [(wk[:, 0:128], wkn + 'a'), (wk[:, 128:256], wkn + 'b')]
                for c0 in range(0, 16, 2):
                    pr = (c0, c0 + 1)
                    for c in pr:
                        P.op('dve', lambda e, c=c: e.max(out=m163[:, c, 0:8], in_=sc3[:, c, :]), r=[scn], w=[m16c[c]])
                    for c in pr:
                        P.op('dve', lambda e, c=c: e.max_index(out=i163[:, c, 0:8], in_max=m163[:, c, 0:8], in_values=sc3[:, c, :]), r=[scn, m16c[c]], w=[i16c[c]])
                    for j_, c in enumerate(pr):
                        P.op('dve', lambda e, c=c, j_=j_: e.match_replace(out=wks[j_][0], in_to_replace=m163[:, c, 0:8], in_values=sc3[:, c, :], imm_value=-1e30), r=[scn, m16c[c]], w=[wks[j_][1]])
                    for j_, c in enumerate(pr):
                        P.op('dve', lambda e, c=c, j_=j_: e.max(out=m163[:, c, 8:16], in_=wks[j_][0]), r=[wks[j_][1]], w=[m16c[c]])
                    for j_, c in enumerate(pr):
                        P.op('dve', lambda e, c=c, j_=j_: e.max_index(out=i163[:, c, 8:16], in_max=m163[:, c, 8:16], in_values=wks[j_][0]), r=[wks[j_][1], m16c[c]], w=[i16c[c]])
                CP('dve', i16f, i16, i16c, [i16fn])
                CK('t_top16')
                m4 = m16.rearrange("p (h s k) -> p h s k", h=8, s=2); if4 = i16f.rearrange("p (h s k) -> p h s k", h=8, s=2)
                cand4 = cand.rearrange("p (h a b) -> p h a b", h=8, a=16)
                TT('dve', cand4, m4[:, :, 0, :].unsqueeze(3).to_broadcast([128, 8, 16, 16]), m4[:, :, 1, :].unsqueeze(2).to_broadcast([128, 8, 16, 16]), ALU.add, m16c, [candn])
                cand3 = cand.rearrange("p (h c) -> p h c", h=8)
                cv3_ = cv.rearrange("p (h k) -> p h k", h=8); ex3 = ex.rearrange("p (h k) -> p h k", h=8)
                CK('t_cand')
                ci3 = ci_.rearrange("p (h k) -> p h k", h=8)
                cvh = ['%s_%d' % (cvn, h) for h in range(8)]; cih = ['%s_%d' % (cin, h) for h in range(8)]
                wk2 = [(eid[:, 0:256], eidn + 'a'), (eid[:, 256:512], eidn + 'b')]
                for h0 in range(0, 8, 2):
                    pr = (h0, h0 + 1)
                    for h in pr:
                        P.op('dve', lambda e, h=h: e.max(out=cv3_[:, h, 0:8], in_=cand3[:, h, :]), r=[candn], w=[cvh[h]])
                    for h in pr:
                        P.op('dve', lambda e, h=h: e.max_index(out=ci3[:, h, 0:8], in_max=cv3_[:, h, 0:8], in_values=cand3[:, h, :]), r=[candn, cvh[h]], w=[cih[h]])
                    for j_, h in enumerate(pr):
                        P.op('dve', lambda e, h=h, j_=j_: e.match_replace(out=wk2[j_][0], in_to_replace=cv3_[:, h, 0:8], in_values=cand3[:, h, :], imm_value=-1e30), r=[candn, cvh[h]], w=[wk2[j_][1]])
                    for j_, h in enumerate(pr):
                        P.op('dve', lambda e, h=h, j_=j_: e.max(out=cv3_[:, h, 8:16], in_=wk2[j_][0]), r=[wk2[j_][1]], w=[cvh[h]])
                    for j_, h in enumerate(pr):
                        P.op('dve', lambda e, h=h, j_=j_: e.max_index(out=ci3[:, h, 8:16], in_max=cv3_[:, h, 8:16], in_values=wk2[j_][0]), r=[wk2[j_][1], cvh[h]], w=[cih[h]])
                CP('dve', cif, ci_, cih, [cifn])
                TS('dve', gz[:, 0:8], cv3_[:, :, 0], -1.0, None, ALU.mult, None, cvh, [gzn])
                CK('t_cv')
                for h in range(8):
                    ACT(ex3[:, h, :], cv3_[:, h, :], AF.Exp, [cvh[h], gzn], [exn, gzn], bias=gz[:, h:h + 1], accum=gz[:, 8 + h:9 + h])
                RCP(gz[:, 8:16], gz[:, 8:16], [gzn], [gzn])
                TT('dve', ex3, ex3, gz[:, 8:16].unsqueeze(2).to_broadcast([128, 8, 16]), ALU.mult, [exn, gzn], [exn])
                CK('t_sm')
                CK('t_esel')
                if DENSE:
                    TS('dve', etmp, cif, 1.0 / 16, -0.47, ALU.mult, ALU.add, [cifn], [etn])
                    CP('dve', i1i, etmp, [etn], [i1in])
                    CP('dve', af_, i1i, [i1in], [afn])
                    STT(bf_, af_, -16.0, cif, ALU.mult, ALU.add, [afn, cifn], [bfn])
                    oh3 = eid[:, 512:2560].rearrange("p (s a) -> p s a", a=16)
                    oh4 = eid[:, 512:2560].rearrange("p (h k a) -> p h k a", h=8, k=16)
                    io16 = iot[:, 0:16].unsqueeze(1).to_broadcast([128, 128, 16])
                    for (sel_, seln_, side, dst_, dstn_) in ((af_, afn, 0, i1f, i1fn), (bf_, bfn, 1, i2f, i2fn)):
                        TT('dve', oh3, io16, sel_.unsqueeze(2).to_broadcast([128, 128, 16]), ALU.is_equal, [iotn, seln_], [eidn])
                        TT('dve', oh4, oh4, if4[:, :, side, :].unsqueeze(2).to_broadcast([128, 8, 16, 16]), ALU.mult, [eidn, i16fn], [eidn])
                        P.op('dve', lambda e, dst_=dst_: e.reduce_sum(out=dst_, in_=oh3, axis=AX.X), r=[eidn], w=[dstn_])
                    for a_, (src_, srcn_) in enumerate(((ex, exn), (i1f, i1fn), (i2f, i2fn))):
                        TR(ps[a_][:, 0:128], src_, 128, [srcn_], [psn[a_]])
                        CP('act', sm3[:, a_, :], ps[a_][:, 0:128], [psn[a_]], [smn])
                    DMA(sm_d[t], sm, [smn], ['sm_d'], q='sp')
                    return
                for j in range(128):
                    g_, gn_ = gb[j % 3]
                    P.op('pool', lambda e, j=j, g_=g_: e.indirect_dma_start(out=g_, out_offset=None, in_=I['peer_u%d' % l],
                         in_offset=bass.IndirectOffsetOnAxis(ap=eseli[:, j:j + 1], axis=0)), r=[eselin], w=[gn_], dma=True)
                    STT(junk, g_, 1.0, h2, ALU.mult, ALU.mult, [gn_, h2n], [jn, avn], accum=av[:, j:j + 1])
                ACT(wv, av, AF.Gelu_apprx_tanh, [avn], [wvn])
                CK('t_ug')
                TT('dve', wv, wv, ex, ALU.mult, [wvn, exn], [wvn])
                for j in range(128):
                    g_, gn_ = gb[j % 3]
                    P.op('pool', lambda e, j=j, g_=g_: e.indirect_dma_start(out=g_, out_offset=None, in_=I['peer_v%d' % l],
                         in_offset=bass.IndirectOffsetOnAxis(ap=eseli[:, j:j + 1], axis=0)), r=[eselin], w=[gn_], dma=True)
                    if j == 0:
                        TS('dve', acc, g_, wv[:, 0:1], None, ALU.mult, None, [gn_, wvn], [accn])
                    else:
                        STT(acc, g_, wv[:, j:j + 1], acc, ALU.mult, ALU.add, [gn_, wvn, accn], [accn])
                if dbg and dbg[0] == ('peer', l, b, is_ctx) :
                    DMA(dbg_out[tok, :], acc, [accn], ['dbg'], q='sp')
                TT('dve', acc, acc, rows['g2'][0], ALU.mult, [accn, rows['g2'][1]], [accn])
                STT(r1, x1, ALPHA, acc, ALU.mult, ALU.add, [x1n, accn], [r1n, st8n], accum=st8[:, 0:1])
                layernorm(r1, r1n, 'ln2g', 'ln2b', h2, h2n)
                CK('t_ln2')
                DMA(dst[tok, :], h2, [h2n], ['dst'], q='sp')
                CK('t_tile%d_%s' % (t, 'c' if is_ctx else 'l'))
            front(0)
            for t in range(NT):
                if t + 1 < NT:
                    front(t + 1)
                back(t)
            P.barrier()
            A.release(m0)
            if DENSE:
                tailB(l, b, is_ctx, dst, L, v)
            CK('tail_done')

        def phase_tables(l):
            m0 = A.mark()
            ub = [A.alloc(1024) for _ in range(2)]; vf = [A.alloc(1024) for _ in range(2)]
            ut = [A.alloc(1024, BF16) for _ in range(2)]; vb = [A.alloc(1024, BF16) for _ in range(2)]
            for i1 in range(128):
                rs_ = slice(i1 * 128, (i1 + 1) * 128)
                u_, un_ = ub[i1 % 2]; o_, on_ = ut[i1 % 2]
                o3 = o_.rearrange("p (k e) -> p k e", k=8)
                DMA(u_, I['peer_u%d' % l][rs_, :], [], [un_])
                for half in range(2):
                    bi = (2 * i1 + half) % 4
                    for kk_ in range(4):
                        k = half * 4 + kk_
                        TR(ps[bi][:, kk_ * 128:(kk_ + 1) * 128], u_[:, k * 128:(k + 1) * 128], 128, [un_], [psn[bi]])
                    CP('act' if half else 'dve', o3[:, half * 4:(half + 1) * 4, :], ps[bi][:, :].rearrange("p (k e) -> p k e", k=4), [psn[bi]], [on_])
                DMA(UT_d[i1], o_, [on_], ['UT_d'])
                v_, vn_ = vf[i1 % 2]; w_, wn_ = vb[i1 % 2]
                DMA(v_, I['peer_v%d' % l][rs_, :], [], [vn_])
                CP('pool', w_, v_, [vn_], [wn_])
                DMA(V_d[i1], w_, [wn_], ['V_d'])
            P.barrier()
            A.release(m0)

        def tailB(l, b, is_ctx, dst, L, v):
            m0 = A.mark()
            rows = {}
            for nm, srcrow in (('ln2g', I['ln2_g'][l:l + 1, :]), ('ln2b', I['ln2_b'][l:l + 1, :]), ('g2', mod_d[l, v:v + 1, 5 * D:6 * D])):
                t_, n_ = A.alloc(1024)
                DMA(t_, srcrow.partition_broadcast(128), ['mod_d'], [n_])
                rows[nm] = (t_, n_)
            iot, iotn = A.alloc(128)
            DMA(iot, I['iota256'][:, 0:128].partition_broadcast(128), [], [iotn])
            W_, Wn = A.alloc(128 * 256, BF16); W3 = W_.rearrange("p (i t) -> p i t", i=128)
            Wall = ['%s_%d' % (Wn, i_) for i_ in range(128)]
            h2s, h2sn = A.alloc(8 * 256, BF16); h2s3 = h2s.rearrange("p (k t) -> p k t", k=8)
            sms, smsn = A.alloc(2 * 384); sms4 = sms.rearrange("p (j a t) -> p j a t", j=2, a=3)
            Qb = [A.alloc(32 * 128, BF16) for _ in range(2)]; Rb = [A.alloc(32 * 128, BF16) for _ in range(2)]
            R1, R1n = A.alloc(32 * 128, BF16)
            NSB = 6
            utb = [A.alloc(1024, BF16) for _ in range(NSB)]; vtb = [A.alloc(1024, BF16) for _ in range(NSB)]
            gab = [A.alloc(256, BF16) for _ in range(2)]
            x1t, x1tn = A.alloc(1024); acc, accn = A.alloc(1024); r1, r1n = A.alloc(1024); xo, xon = A.alloc(1024)
            st8, st8n = A.alloc(8)
            iot3 = iot.unsqueeze(1).to_broadcast([128, 32, 128])
            for sp_ in range(L // 256):
                t0 = sp_ * 256
                DMA(h2s3, h2T_d[:, :, t0:t0 + 256].rearrange("k p t -> p k t"), ['h2T_d'], [h2sn])
                for jt in range(2):
                    DMA(sms[:, jt * 384:(jt + 1) * 384], sm_d[2 * sp_ + jt], ['sm_d'], [smsn])
                for sub in range(8):
                    jt = sub // 4; tr = slice((sub % 4) * 32, (sub % 4 + 1) * 32)
                    q_, qn_ = Qb[sub % 2]; r_, rn_ = Rb[sub % 2]
                    q3 = q_.rearrange("p (t i) -> p t i", t=32); r3 = r_.rearrange("p (t i) -> p t i", t=32); R13 = R1.rearrange("p (t i) -> p t i", t=32)
                    TT('dve', q3, iot3, sms4[:, jt, 2, tr].unsqueeze(2).to_broadcast([128, 32, 128]), ALU.is_equal, [iotn, smsn], [qn_])
                    TT('dve', R13, iot3, sms4[:, jt, 1, tr].unsqueeze(2).to_broadcast([128, 32, 128]), ALU.is_equal, [iotn, smsn], [R1n])
                    TT('pool', r3, R13, sms4[:, jt, 0, tr].unsqueeze(2).to_broadcast([128, 32, 128]), ALU.mult, [R1n, smsn], [rn_])
                    for j in range(32):
                        tg = sub * 32 + j
                        bi = (tg // 4) % 2
                        MM(ps[bi][:, (tg % 4) * 128:(tg % 4 + 1) * 128], q3[:, j, :], r3[:, j, :], True, True, [qn_, rn_], [psn[bi]])
                        if tg % 4 == 3:
                            CP('act' if (tg // 4) % 2 else 'dve', W3[:, :, tg - 3:tg + 1].rearrange("p i t -> p t i"),
                               ps[bi][:, :].rearrange("p (t i) -> p t i", t=4), [psn[bi]], Wall)
                def vmm(i1):
                    v_, vn_ = vtb[i1 % NSB]
                    for jt in range(2):
                        for dh in range(2):
                            bi = 4 + jt * 2 + dh
                            MM(ps[bi][:, :], W3[:, i1, jt * 128:(jt + 1) * 128], v_[:, dh * 512:(dh + 1) * 512], i1 == 0, i1 == 127, [Wall[i1], vn_], [psn[bi]])
                for i1 in range(128):
                    u_, un_ = utb[i1 % NSB]; u3 = u_.rearrange("p (k e) -> p k e", k=8)
                    DMA(u_, UT_d[i1], ['UT_d'], [un_], q='sp')
                    v_, vn_ = vtb[i1 % NSB]
                    DMA(v_, V_d[i1], ['V_d'], [vn_], q='act')
                    bi = 2 + i1 % 2
                    for k in range(8):
                        MM(ps[bi][:, 0:256], u3[:, k, :], h2s3[:, k, :], k == 0, k == 7, [un_, h2sn], [psn[bi]])
                    g_, gn_ = gab[i1 % 2]
                    ACT(g_, ps[bi][:, 0:256], AF.Gelu_apprx_tanh, [psn[bi]], [gn_])
                    TT('pool' if i1 % 2 else 'dve', W3[:, i1, :], W3[:, i1, :], g_, ALU.mult, [Wall[i1], gn_], [Wall[i1]])
                    if i1 >= 1:
                        vmm(i1 - 1)
                vmm(127)
                for jt in range(2):
                    tok = slice(t0 + jt * 128, t0 + (jt + 1) * 128)
                    DMA(x1t, x1_d[tok, :], ['x1_d'], [x1tn])
                    for dh in range(2):
                        bi = 4 + jt * 2 + dh
                        TT('dve', acc[:, dh * 512:(dh + 1) * 512], ps[bi][:, :], rows['g2'][0][:, dh * 512:(dh + 1) * 512], ALU.mult, [psn[bi], rows['g2'][1]], [accn])
                    if dbg and dbg[0] == ('peer', l, b, is_ctx):
                        pass
                    STT(r1, x1t, ALPHA, acc, ALU.mult, ALU.add, [x1tn, accn], [r1n, st8n], accum=st8[:, 0:1])
                    TS('dve', st8[:, 1:2], st8[:, 0:1], -1.0 / D, None, ALU.mult, None, [st8n], [st8n])
                    ACT(acc, r1, AF.Square, [r1n, st8n], [accn, st8n], bias=st8[:, 1:2], accum=st8[:, 2:3])
                    ACT(st8[:, 3:4], st8[:, 2:3], AF.Sqrt, [st8n], [st8n], bias=EPS, scale=1.0 / D)
                    RCP(st8[:, 4:5], st8[:, 3:4], [st8n], [st8n])
                    TS('dve', xo, r1, st8[:, 1:2], st8[:, 4:5], ALU.add, ALU.mult, [r1n, st8n], [xon])
                    TT('dve', xo, xo, rows['ln2g'][0], ALU.mult, [xon, rows['ln2g'][1]], [xon])
                    TT('dve', xo, xo, rows['ln2b'][0], ALU.add, [xon, rows['ln2b'][1]], [xon])
                    DMA(dst[tok, :], xo, [xon], ['dst'], q='sp')
            P.barrier()
            A.release(m0)

        try:
            for l in layers:
                phase_params(l)
                CK('params')
                if DENSE and do_tail:
                    phase_tables(l)
                if 'hy' in mixers:
                    phase_filters(l, S)
                    CK('filtS')
                    if l < DEPTH - 1:
                        phase_filters(l, LC)
                        CK('filtC')
                for b in batches:
                    run_seq(l, b, True)
                    run_seq(l, b, False)
        except _Stop:
            pass

        if dbg and dbg[0] == 'xs':
            DMA(dbg_out, xs_d[dbg[3]], ['x'], ['dbg'], q='sp')
        if dbg and dbg[0] == 'xcs':
            DMA(dbg_out, xcs_d[dbg[3]], ['x'], ['dbg'], q='sp')
        if dbg and dbg[0] == 'modT':
            DMA(dbg_out, modT[:], ['modT'], ['dbg'], q='sp')
        if dbg and dbg[0] == 'biasK':
            DMA(dbg_out, biasK[:], ['biasK'], ['dbg'], q='sp')
        if dbg and dbg[0] in ('ksp2048', 'ksp256'):
            DMA(dbg_out, ksp_d[int(dbg[0][3:])], ['x'], ['dbg'], q='sp')
        P.barrier()
        with nc.Block() as block:
            P.emit(sems, block)
    nc._prog_nops = P.nops
    return nc


_CACHE = {}


def kernel(**inputs):
    if 'nc' not in _CACHE:
        _CACHE['nc'] = build()
        _CACHE['consts'] = consts()
    nc = _CACHE['nc']; C = _CACHE['consts']
    f32 = lambda a: np.ascontiguousarray(np.asarray(a, dtype=np.float32))
    shared = {k: f32(inputs[k]) for k in WSHAPES if not k.startswith('peer_u') and not k.startswith('peer_v')}
    for l in range(DEPTH):
        shared['peer_u%d' % l] = f32(inputs['peer_u'][l]); shared['peer_v%d' % l] = f32(inputs['peer_v'][l])
    shared.update(C)
    x = f32(inputs['x']); ctx = f32(inputs['ctx']); c = f32(inputs['c']); cc = f32(inputs['c_ctx'])
    in_maps = []
    for i in range(8):
        m = dict(shared)
        m['x'] = x[2 * i:2 * i + 2]; m['ctx'] = ctx[2 * i:2 * i + 2]
        m['cvec'] = np.ascontiguousarray(np.stack([c[2 * i], c[2 * i + 1], cc]))
        in_maps.append(m)
    res = run_bass_kernel_spmd(nc, in_maps, core_ids=list(range(8)))
    return np.concatenate([np.asarray(r['out']) for r in res.results], axis=0).astype(np.float32)
```

```python
import math
from contextlib import ExitStack
import numpy as np
import ml_dtypes
import concourse.bass as bass
import concourse.mybir as mybir
from concourse.bass_utils import run_bass_kernel_spmd

F32 = mybir.dt.float32; BF16 = mybir.dt.bfloat16; I32 = mybir.dt.int32; U32 = mybir.dt.uint32
AF = mybir.ActivationFunctionType; ALU = mybir.AluOpType; AX = mybir.AxisListType

D = 1024; NB = 2; S = 2048; LC = 256; DEPTH = 2; G = 256
ALPHA = (2.0 * DEPTH) ** 0.25
EPS = 1e-5
NDQ = 20
DENSE = True
RELAX_SAME_ENGINE = False
PI = math.pi
ARENA_COLS = 44 * 1024

WSHAPES = dict(
    w_ada=[2, 1024, 6144], b_ada=[2, 6144], w_in=[2, 1024, 2824], w_out=[2, 1024, 1024],
    ln1_g=[2, 1024], ln1_b=[2, 1024], ln2_g=[2, 1024], ln2_b=[2, 1024],
    conf_dw_w=[2, 31, 256], conf_dw_b=[2, 256], conf_norm_g=[2, 256], conf_norm_b=[2, 256],
    na_rpb=[2, 4, 15, 31], hy_short_w=[2, 3, 768], hy_short_b=[2, 768], hy_w1=[2, 33, 64], hy_b1=[2, 64],
    hy_w2=[2, 64, 64], hy_b2=[2, 64], hy_w3=[2, 64, 512], hy_decay=[2, 512], hy_bias=[2, 256],
    ssd_conv_w=[2, 3, 512], ssd_conv_b=[2, 512], ssd_a_log=[2, 2, 4], ssd_dt_bias=[2, 2, 4], ssd_d=[2, 4],
    ssd_norm_g=[2, 256], peer_wq=[2, 1024, 2048], peer_keys=[2, 8, 2, 128, 128],
    peer_u0=[16384, 1024], peer_u1=[16384, 1024], peer_v0=[16384, 1024], peer_v1=[16384, 1024])


class Prog:
    def __init__(self, nc):
        self.nc = nc
        self.E = dict(pe=nc.tensor, dve=nc.vector, act=nc.scalar, pool=nc.gpsimd, sp=nc.sync)
        self.stream = {e: [] for e in self.E}
        self.cnt = {}
        self.waited = {e: {} for e in self.E}
        self.bufs = {}
        self.dq = {e: 0 for e in self.E}
        self.nops = 0

    def _need(self, eng, deps):
        for k, v in deps.items():
            if eng == 'pe' and k == 'c_pe':
                continue
            if self.waited[eng].get(k, 0) < v:
                self.stream[eng].append(('wait', k, v))
                self.waited[eng][k] = v

    def op(self, eng, fn, r=(), w=(), dma=False):
        w = list(w) + [b for b in r if b.startswith('ps') and b not in w]
        deps = {}
        own = 'c_' + eng
        owncnt = self.cnt.get(own, 0)
        relax = RELAX_SAME_ENGINE and not dma
        def add(tok, raw):
            if tok is None:
                return
            if relax and tok[0] == own:
                if not raw or tok[1] < owncnt:
                    return
            if deps.get(tok[0], 0) < tok[1]:
                deps[tok[0]] = tok[1]
        psw = set(b for b in r if b.startswith('ps'))
        for b in r:
            st = self.bufs.get(b)
            if st:
                add(st['w'], True)
        for b in w:
            st = self.bufs.get(b)
            if st:
                add(st['w'], b in psw)
                for k, v in st['r'].items():
                    add((k, v), False)
        if dma:
            i = self.dq[eng] % NDQ
            self.dq[eng] += 1
            key = 'd_%s_%d' % (eng, i)
            cur = self.cnt.get(key, 0)
            if cur:
                add((key, cur), True)
            inc = 16
        else:
            key = 'c_' + eng
            inc = 1
        self._need(eng, deps)
        self.cnt[key] = self.cnt.get(key, 0) + inc
        tok = (key, self.cnt[key])
        self.stream[eng].append(('op', fn, key, inc))
        self.nops += 1
        for b in r:
            st = self.bufs.setdefault(b, {'w': None, 'r': {}})
            if st['r'].get(tok[0], 0) < tok[1]:
                st['r'][tok[0]] = tok[1]
        for b in w:
            self.bufs[b] = {'w': tok, 'r': {}}
        return tok

    def barrier(self):
        for e in self.E:
            self._need(e, dict(self.cnt))
        self.bufs = {}

    def emit(self, sems, block):
        def mk(e):
            def body(eng):
                for it in self.stream[e]:
                    if it[0] == 'wait':
                        eng.wait_ge(sems[it[1]], it[2])
                    else:
                        ins = it[1](eng)
                        ins.then_inc(sems[it[2]], it[3])
            return body
        block.tensor(mk('pe')); block.vector(mk('dve')); block.scalar(mk('act'))
        block.gpsimd(mk('pool')); block.sync(mk('sp'))

    def sem_keys(self):
        ks = ['c_' + e for e in self.E]
        for e in ('sp', 'act', 'pool'):
            ks += ['d_%s_%d' % (e, i) for i in range(NDQ)]
        return ks


class _Stop(Exception):
    pass


class Arena:
    def __init__(self, t, n):
        self.t = t; self.n = n; self.top = 0; self.k = 0; self.P = None

    def alloc(self, cols, dt=F32, parts=128):
        n32 = cols if dt != BF16 else (cols + 1) // 2
        assert self.top + n32 <= self.n, ('arena overflow', self.top, n32, self.n)
        a = self.t[0:parts, self.top:self.top + n32]
        if dt != F32:
            a = a.bitcast(dt)
            if dt == BF16 and cols % 2:
                a = a[:, 0:cols]
        self.top += n32
        self.hw = max(getattr(self, 'hw', 0), self.top)
        self.k += 1
        return a, 'A%d' % self.k

    def mark(self):
        return self.top

    def release(self, m):
        if self.P is not None:
            self.P.barrier()
        if m == 0 or getattr(self, 'verbose', False):
            pass
        self.top = m


def consts():
    C = {}
    n_f = 16
    inv = 10000.0 ** (-np.arange(n_f, dtype=np.float32) / n_f)
    t = np.arange(S)
    r = (t // 64).astype(np.float32); col = (t % 64).astype(np.float32)
    ang = np.concatenate([r[:, None] * inv, col[:, None] * inv], -1)
    cos = np.cos(ang).astype(np.float32).T; sin = np.sin(ang).astype(np.float32).T
    C['rope'] = np.stack([np.concatenate([cos, cos], 0), np.concatenate([-sin, sin], 0)]).astype(np.float32)
    m = np.zeros((9, 128, 896), np.float32)
    kl = np.arange(128); krl = kl // 64; kc = kl % 64
    for cls, i in enumerate([0, 1, 2, 3, 6, 12, 13, 14, 15]):
        for dj in range(-3, 4):
            j = i + dj
            if j < 0 or j > 15:
                continue
            ql = np.arange(128); rl = ql // 64; c = ql % 64
            rr = 2 * j + rl; kr = 2 * i + krl
            rs = np.clip(rr - 4, 0, 24); cs = np.clip(c - 8, 0, 48)
            ok = ((kr[:, None] >= rs[None, :]) & (kr[:, None] < rs[None, :] + 8)
                  & (kc[:, None] >= cs[None, :]) & (kc[:, None] < cs[None, :] + 16))
            m[cls, :, (dj + 3) * 128:(dj + 4) * 128] = ok
    C['namask'] = np.ascontiguousarray(m.transpose(1, 0, 2)).astype(ml_dtypes.bfloat16)
    sp = np.arange(128)[:, None]; lq = np.arange(128)[None, :]
    C['iota256'] = np.arange(256, dtype=np.float32)[None, :]
    C['ssdc'] = np.stack([(sp <= lq), (sp < lq), np.where(lq < sp, -30000.0, 0.0), np.where(lq > sp, 30000.0, 0.0)]).astype(np.float32)
    for L in (S, LC):
        tn = np.arange(L, dtype=np.float32)[:, None] / np.float32(L)
        bands = np.arange(1, 17, dtype=np.float32)[None, :]
        a2 = (2.0 * math.pi * bands * tn).astype(np.float32)
        z = np.concatenate([tn, np.sin(a2), np.cos(a2)], -1).astype(np.float32)
        C['zT%d' % L] = np.ascontiguousarray(z.T)
        N = 2 * L; NFp = L + 128
        s = np.arange(L, dtype=np.float64)[:, None]; f = np.arange(NFp, dtype=np.float64)[None, :]
        th = 2 * np.pi * ((s * f) % N) / N
        valid = (f <= L)
        fw = np.concatenate([np.cos(th) * valid, np.sin(th) * valid], 1)
        nch = (2 * NFp + 511) // 512
        fwp = np.zeros((L, nch * 512), np.float64); fwp[:, :2 * NFp] = fw
        fwc = fwp.reshape(L // 128, 128, nch, 512).transpose(2, 1, 0, 3)
        C['fw%d' % L] = np.ascontiguousarray(fwc).astype(np.float32).astype(ml_dtypes.bfloat16)
        wf = np.where((f == 0) | (f == L), 1.0, 2.0) * valid / N
        iv = np.stack([(np.cos(th) * wf).T, (-np.sin(th) * wf).T])
        C['inv%d' % L] = iv.astype(np.float32).astype(ml_dtypes.bfloat16)
    return C


CSHAPES = dict(rope=([2, 64, S], F32), iota256=([1, 256], F32), namask=([128, 9, 896], BF16), ssdc=([4, 128, 128], F32),
               zT2048=([33, S], F32), zT256=([33, LC], F32),
               fw2048=([9, 128, 16, 512], BF16), inv2048=([2, S + 128, S], BF16),
               fw256=([2, 128, 2, 512], BF16), inv256=([2, LC + 128, LC], BF16))


def build(plan=None, dbg=None):
    plan = plan or {}
    layers = plan.get('layers', list(range(DEPTH)))
    batches = plan.get('batches', list(range(NB)))
    mixers = plan.get('mixers', ['conf', 'attn', 'hy', 'ssd'])
    do_tail = plan.get('tail', True)
    nc = bass.Bass("TRN2", target_bir_lowering=False)
    I = {}
    def din(name, shape, dt=F32):
        I[name] = nc.dram_tensor(name, list(shape), dt, kind="ExternalInput").ap()
    def dscr(name, shape, dt=F32):
        return nc.dram_tensor(name, list(shape), dt, kind="Internal").ap()
    din('x', [NB, S, D]); din('ctx', [NB, LC, D]); din('cvec', [3, D])
    for k, shp in WSHAPES.items():
        din(k, shp)
    for k, (shp, dt) in CSHAPES.items():
        din(k, shp, dt)
    out = nc.dram_tensor('out', [NB, S, D], F32, kind="ExternalOutput").ap()
    mod_d = dscr('mod_d', [DEPTH, 3, 6 * D])
    xs_d = dscr('xs_d', [NB, S, D]); xcs_d = dscr('xcs_d', [NB, LC, D])
    ycat_d = {S: dscr('ycat_lat', [8, 128, S], BF16), LC: dscr('ycat_ctx', [8, 128, LC], BF16)}
    ksp_d = {S: dscr('ksp_lat', [2, 256, S + 128]), LC: dscr('ksp_ctx', [2, 256, LC + 128])}
    zq_d = dscr('zq_d', [60, 64, 127])
    UT_d = dscr('UT_d', [128, 128, 1024], BF16); V_d = dscr('V_d', [128, 128, 1024], BF16)
    x1_d = dscr('x1_d', [S, D]); h2T_d = dscr('h2T_d', [8, 128, S], BF16); sm_d = dscr('sm_d', [S // 128, 128, 3 * 128])
    dbg_out = None
    if dbg:
        dbg_out = nc.dram_tensor('dbg', list(dbg[1]), dbg[2] if len(dbg) > 2 and dbg[2] is not None else F32, kind="ExternalOutput").ap()

    P = Prog(nc)
    with ExitStack() as es:
        def sb(name, shape, dt=F32):
            return es.enter_context(nc.sbuf_tensor('s_' + name, list(shape), dt))
        A = Arena(sb('arena', [128, ARENA_COLS], F32), ARENA_COLS)
        A.P = P
        ident = sb('ident', [128, 128]); ones = sb('ones', [128, 128]); blk = sb('blk', [128, 128])
        modT = sb('modT', [128, DEPTH * 48 * 3]); modT1 = sb('modT1', [128, DEPTH * 48 * 3])
        ssdc = sb('ssdc', [128, 4 * 128])
        cw = sb('cw', [128, 2 * 31]); cv3 = sb('cv3', [128, 2 * 3])
        hsp = sb('hsp', [128, 6 * 4]); hskip = sb('hskip', [128, 2])
        sxp = sb('sxp', [128, 2 * 4]); sbp = sb('sbp', [64, 4 * 4])
        A_bc = sb('A_bc', [128, 8]); dtb_bc = sb('dtb_bc', [128, 8]); dsk_bc = sb('dsk_bc', [128, 4]); ng_bc = sb('ng_bc', [128, 256])
        biasK = sb('biasK', [128, 4 * 896])
        kcT = sb('kcT', [64, 4 * 256], BF16); vc1 = sb('vc1', [128, 2 * 4 * 65], BF16)
        Sst = sb('Sst', [64, 8 * 64]); SstB = sb('SstB', [64, 8 * 64], BF16)
        ps = [es.enter_context(nc.psum_tensor('ps%d' % i, [128, 512], F32)) for i in range(8)]
        psn = ['ps%d' % i for i in range(8)]
        sems = {k: es.enter_context(nc.semaphore(k)) for k in P.sem_keys()}
        qrot = [0]

        ck = [0]
        def CK(tag):
            ck[0] += 1
            if plan.get('stop') == ck[0] or plan.get('stoptag') == tag:
                print('STOP at', tag, flush=True)
                raise _Stop()
        def MM(o, lhsT, rhs, start, stop, r, w):
            P.op('pe', lambda e: e.matmul(o, lhsT=lhsT, rhs=rhs, start=start, stop=stop), r=r, w=w)
        def TR(o, in_, n, r, w):
            P.op('pe', lambda e: e.transpose(out=o, in_=in_, identity=ident[0:n, 0:n]), r=list(r) + ['ident'], w=w)
        def DMA(o, in_, r, w, q=None, slow=False):
            if q is None:
                q = ('sp', 'act')[qrot[0] % 2]; qrot[0] += 1
            P.op(q, lambda e: e.dma_start(out=o, in_=in_, allow_slow_non_contiguous=slow), r=r, w=w, dma=True)
        def ACT(o, in_, func, r, w, bias=None, scale=None, accum=None):
            kw = {}
            if bias is not None: kw['bias'] = bias
            if scale is not None: kw['scale'] = scale
            if accum is not None: kw['accum_out'] = accum
            P.op('act', lambda e: e.activation(out=o, in_=in_, func=func, **kw), r=r, w=w)
        def TT(eng, o, in0, in1, op, r, w):
            P.op(eng, lambda e: e.tensor_tensor(out=o, in0=in0, in1=in1, op=op), r=r, w=w)
        def TS(eng, o, in0, s1, s2, op0, op1, r, w, accum=None):
            kw = {}
            if op1 is not None: kw['op1'] = op1
            if accum is not None: kw['accum_out'] = accum
            P.op(eng, lambda e: e.tensor_scalar(out=o, in0=in0, scalar1=s1, scalar2=s2, op0=op0, **kw), r=r, w=w)
        def STT(o, in0, sc, in1, op0, op1, r, w, accum=None):
            kw = {}
            if accum is not None: kw['accum_out'] = accum
            P.op('dve', lambda e: e.scalar_tensor_tensor(out=o, in0=in0, scalar=sc, in1=in1, op0=op0, op1=op1, **kw), r=r, w=w)
        def CP(eng, o, in_, r, w):
            if eng == 'act':
                P.op('act', lambda e: e.copy(out=o, in_=in_), r=r, w=w)
            else:
                P.op(eng, lambda e: e.tensor_copy(out=o, in_=in_), r=r, w=w)
        def MS(eng, o, val, w):
            P.op(eng, lambda e: e.memset(o, val), w=w)
        def RCP(o, in_, r, w):
            P.op('dve', lambda e: e.reciprocal(out=o, in_=in_), r=r, w=w)

        MS('pool', ident[:], 0.0, ['ident'])
        P.op('pool', lambda e: e.affine_select(out=ident[:], in_=ident[:], pattern=[[-1, 128]], compare_op=ALU.not_equal,
                                                fill=1.0, base=0, channel_multiplier=1), r=['ident'], w=['ident'])
        MS('pool', ones[:], 1.0, ['ones'])
        MS('pool', modT[:], 0.0, ['modT'])
        MS('pool', blk[:], 0.0, ['blk'])
        MS('pool', blk[0:64, 0:64], 1.0 / 64, ['blk'])
        MS('pool', blk[64:128, 64:128], 1.0 / 64, ['blk'])
        ssdc3 = ssdc[:].rearrange("p (a b) -> p a b", a=4)
        for a_ in range(4):
            DMA(ssdc3[:, a_, :], I['ssdc'][a_], [], ['ssdc'])
        P.barrier()

        mT4 = modT[:].rearrange("p (l j v) -> p l j v", l=DEPTH, j=48)
        mT14 = modT1[:].rearrange("p (l j v) -> p l j v", l=DEPTH, j=48)

        def phase_mod(l):
            m0 = A.mark()
            cT, cTn = A.alloc(24); sT, sTn = A.alloc(24)
            mrow, mrn = A.alloc(6 * D, parts=3)
            wbuf = [A.alloc(8 * 512) for _ in range(2)]
            brow, brn = A.alloc(6 * D, parts=1)
            cT3 = cT.rearrange("p (k v) -> p k v", v=3)
            for v in range(3):
                DMA(cT3[:, :, v], I['cvec'][v].rearrange("(k p) -> p k", p=128), [], [cTn], slow=True)
            DMA(brow, I['b_ada'][l:l + 1, :], [], [brn])
            ACT(sT, cT, AF.Silu, [cTn], [sTn])
            sT3 = sT.rearrange("p (k v) -> p k v", v=3)
            for n in range(12):
                wb, wbn = wbuf[n % 2]
                wb3 = wb.rearrange("p (k c) -> p k c", c=512)
                DMA(wb3, I['w_ada'][l, :, n * 512:(n + 1) * 512].rearrange("(k p) c -> p k c", p=128), [], [wbn])
                pt = ps[n % 2]; pn = psn[n % 2]
                for k in range(8):
                    MM(pt[0:3, :], sT3[:, k, :], wb3[:, k, :], k == 0, False, [sTn, wbn], [pn])
                MM(pt[0:3, :], ones[0:1, 0:3], brow[0:1, n * 512:(n + 1) * 512], False, True, ['ones', brn], [pn])
                CP('dve', mrow[0:3, n * 512:(n + 1) * 512], pt[0:3, :], [pn], [mrn])
            DMA(mod_d[l], mrow, [mrn], ['mod_d'], q='sp')
            for j in range(48):
                pt = ps[2 + j % 2]; pn = psn[2 + j % 2]
                TR(pt[:, 0:3], mrow[0:3, j * 128:(j + 1) * 128], 3, [mrn], [pn])
                CP('dve', mT4[:, l, j, :], pt[:, 0:3], [pn], ['modT'])
            P.barrier()
            A.release(m0)

        for l in layers:
            phase_mod(l)
        TS('dve', modT1[:], modT[:], 1.0, None, ALU.add, None, ['modT'], ['modT1'])
        P.barrier()

        def paramT(dst3, dname, rows, n, nch, Pn, c0=0):
            m0 = A.mark()
            if not isinstance(rows, list):
                C = rows.shape[-1]
                stg, sn = A.alloc(C, parts=n)
                DMA(stg[0:n, :], rows, [], [sn])
            else:
                C = rows[0].shape[-1]
                stg, sn = A.alloc(C, parts=max(n, 1))
                j = 0
                for rw in rows:
                    nr = rw.shape[0]
                    DMA(stg[j:j + nr, :], rw, [], [sn])
                    j += nr
            for ch in range(nch):
                pt = ps[ch % 2]; pn = psn[ch % 2]
                TR(pt[0:Pn, 0:n], stg[0:n, c0 + ch * Pn:c0 + (ch + 1) * Pn], n, [sn], [pn])
                CP('dve', dst3[0:Pn, ch, :], pt[0:Pn, 0:n], [pn], [dname])
            P.barrier()
            A.release(m0)

        def load_w(dst3, dname, src2d, n, wst):
            for k in range(8):
                st, sn = wst[k % 2]
                DMA(st[:, 0:n], src2d[k * 128:(k + 1) * 128, :], [], [sn])
                CP('act' if k % 2 else 'pool', dst3[:, k, 0:n], st[:, 0:n], [sn], [dname])

        def phase_params(l):
            row = lambda nm: I[nm][l:l + 1, :]
            paramT(cw[:].rearrange("p (c k) -> p c k", c=2), 'cw', I['conf_dw_w'][l], 31, 2, 128)
            paramT(cv3[:].rearrange("p (c k) -> p c k", c=2), 'cv3', [row('conf_dw_b'), row('conf_norm_g'), row('conf_norm_b')], 3, 2, 128)
            paramT(hsp[:].rearrange("p (c k) -> p c k", c=6), 'hsp', [I['hy_short_w'][l], row('hy_short_b')], 4, 6, 128)
            paramT(hskip[:].rearrange("p (c k) -> p c k", c=2), 'hskip', [row('hy_bias')], 1, 2, 128)
            srows = [I['ssd_conv_w'][l], row('ssd_conv_b')]
            paramT(sxp[:].rearrange("p (c k) -> p c k", c=2), 'sxp', srows, 4, 2, 128)
            paramT(sbp[:].rearrange("p (c k) -> p c k", c=4), 'sbp', srows, 4, 4, 64, c0=256)
            alog = I['ssd_a_log'].rearrange("l a b -> l (a b)")[l:l + 1, :]
            dtb = I['ssd_dt_bias'].rearrange("l a b -> l (a b)")[l:l + 1, :]
            DMA(A_bc[:], alog.partition_broadcast(128), [], ['A_bc'])
            DMA(dtb_bc[:], dtb.partition_broadcast(128), [], ['dtb_bc'])
            DMA(dsk_bc[:], row('ssd_d').partition_broadcast(128), [], ['dsk_bc'])
            DMA(ng_bc[:], row('ssd_norm_g').partition_broadcast(128), [], ['ng_bc'])
            ACT(A_bc[:], A_bc[:], AF.Exp, ['A_bc'], ['A_bc'])
            TS('dve', A_bc[:], A_bc[:], -1.0, None, ALU.mult, None, ['A_bc'], ['A_bc'])
            m0 = A.mark()
            Pt, Pn_ = A.alloc(127, parts=60)
            MS('pool', Pt, 0.0, [Pn_])
            DMA(Pt[:, 48:79], I['na_rpb'][l].rearrange("h a b -> (h a) b"), [Pn_], [Pn_])
            DMA(zq_d, Pt.unsqueeze(1).to_broadcast([60, 64, 127]), [Pn_], ['zq_d'], q='sp')
            bq, bqn = A.alloc(4 * 896)
            bq3 = bq.rearrange("p (h c) -> p h c", h=4)
            bK3 = biasK[:].rearrange("p (h c) -> p h c", h=4)
            for h in range(4):
                for di in range(-3, 4):
                    for rl in range(2):
                        for krl in range(2):
                            X = 2 * di + krl - rl + 7
                            src = bass.AP(zq_d.tensor, (h * 15 + X) * 64 * 127 + 63, [[126, 64], [1, 64]])
                            DMA(bq3[rl * 64:(rl + 1) * 64, h, (di + 3) * 128 + krl * 64:(di + 3) * 128 + (krl + 1) * 64], src, ['zq_d'], [bqn])
            k_ = 0
            for h in range(4):
                for dj in range(-3, 4):
                    di = -dj
                    pt = ps[k_ % 2]; pn = psn[k_ % 2]; k_ += 1
                    TR(pt[:, 0:128], bq3[:, h, (di + 3) * 128:(di + 4) * 128], 128, [bqn], [pn])
                    CP('dve', bK3[:, h, (dj + 3) * 128:(dj + 4) * 128], pt[:, 0:128], [pn], ['biasK'])
            P.barrier()
            A.release(m0)

        def sinwrap(dst, src, bias, n, tmp, r, w):
            tA, tAn = tmp[0]; tB, tBn = tmp[1]; tC, tCn = tmp[2]
            TS('dve', tA[0:64, 0:n], src, bias, None, ALU.add, None, r, [tAn])
            TS('dve', tB[0:64, 0:n], tA[0:64, 0:n], PI, -2 * PI, ALU.is_gt, ALU.mult, [tAn], [tBn])
            TT('dve', tC[0:64, 0:n], tA[0:64, 0:n], tB[0:64, 0:n], ALU.add, [tAn, tBn], [tCn])
            TS('dve', tB[0:64, 0:n], tA[0:64, 0:n], -PI, 2 * PI, ALU.is_lt, ALU.mult, [tAn], [tBn])
            TT('dve', tA[0:64, 0:n], tC[0:64, 0:n], tB[0:64, 0:n], ALU.add, [tCn, tBn], [tAn])
            ACT(dst, tA[0:64, 0:n], AF.Sin, [tAn], w)

        def phase_filters(l, L):
            NT = L // 128; CH = min(512, L); NQ = L // CH; NFp = L + 128
            m0 = A.mark()
            zT, zn = A.alloc(L, parts=33)
            DMA(zT, I['zT%d' % L], [], [zn])
            w1, w1n = A.alloc(64, parts=33); w2, w2n = A.alloc(64, parts=64); w3, w3n = A.alloc(512, parts=64)
            dec, decn = A.alloc(512, parts=1)
            b12, b12n = A.alloc(2, parts=64)
            DMA(w1, I['hy_w1'][l], [], [w1n]); DMA(w2, I['hy_w2'][l], [], [w2n]); DMA(w3, I['hy_w3'][l], [], [w3n])
            DMA(dec, I['hy_decay'][l:l + 1, :], [], [decn])
            bst, bstn = A.alloc(64, parts=2)
            DMA(bst[0:1, :], I['hy_b1'][l:l + 1, :], [], [bstn]); DMA(bst[1:2, :], I['hy_b2'][l:l + 1, :], [], [bstn])
            TR(ps[0][0:64, 0:2], bst[0:2, 0:64], 2, [bstn], [psn[0]])
            CP('dve', b12, ps[0][0:64, 0:2], [psn[0]], [b12n])
            h1, h1n = A.alloc(L, parts=64); h2, h2n = A.alloc(L, parts=64)
            tmp = [A.alloc(CH, parts=64) for _ in range(3)]
            for q in range(NQ):
                cs = slice(q * CH, (q + 1) * CH)
                MM(ps[1][0:64, 0:CH], w1[0:33, :], zT[0:33, cs], True, True, [w1n, zn], [psn[1]])
                sinwrap(h1[:, cs], ps[1][0:64, 0:CH], b12[:, 0:1], CH, tmp, [psn[1], b12n], [h1n])
            for q in range(NQ):
                cs = slice(q * CH, (q + 1) * CH)
                MM(ps[2][0:64, 0:CH], w2[0:64, :], h1[:, cs], True, True, [w2n, h1n], [psn[2]])
                sinwrap(h2[:, cs], ps[2][0:64, 0:CH], b12[:, 1:2], CH, tmp, [psn[2], b12n], [h2n])
            kk, kkn = A.alloc(NT * 512)
            kk3 = kk.rearrange("p (t c) -> p t c", c=512)
            et = [A.alloc(512) for _ in range(2)]; abt = [A.alloc(512) for _ in range(2)]
            for t in range(NT):
                pa = ps[2 + t % 2]; pb = ps[4 + t % 2]
                MM(pa[:, 0:512], h2[0:64, t * 128:(t + 1) * 128], w3[0:64, :], True, True, [h2n, w3n], [psn[2 + t % 2]])
                MM(pb[:, 0:512], zT[0:1, t * 128:(t + 1) * 128], dec[0:1, :], True, True, [zn, decn], [psn[4 + t % 2]])
                e_, en = et[t % 2]; a_, an = abt[t % 2]
                ACT(e_, pb[:, 0:512], AF.Exp, [psn[4 + t % 2]], [en], scale=-1.0)
                TT('dve', kk3[:, t, :], pa[:, 0:512], e_, ALU.mult, [psn[2 + t % 2], en], [kkn])
                ACT(a_, kk3[:, t, :], AF.Abs, [kkn], [an])
                MM(ps[6][:, 0:512], ones[:], a_, t == 0, t == NT - 1, ['ones', an], [psn[6]])
            rect, rn = A.alloc(512)
            TS('dve', rect, ps[6][:, 0:512], 1e-6, None, ALU.add, None, [psn[6]], [rn])
            RCP(rect, rect, [rn], [rn])
            TT('dve', kk3, kk3, rect.unsqueeze(1).to_broadcast([128, NT, 512]), ALU.mult, [kkn, rn], [kkn])
            MS('dve', kk3[0:1, 0, 256:512], 0.0, [kkn])
            ksd, ksdn = A.alloc(2 * NT * 256, BF16)
            ksd4 = ksd.rearrange("p (a t c) -> p a t c", a=2, c=256)
            TT('dve', ksd4[:, 0], kk3[:, :, 0:256], kk3[:, :, 256:512], ALU.add, [kkn], [ksdn])
            TT('dve', ksd4[:, 1], kk3[:, :, 256:512], kk3[:, :, 0:256], ALU.subtract, [kkn], [ksdn])
            if dbg and dbg[0] == 'filt%d' % L:
                DMA(dbg_out, kk, [kkn], ['dbg'], q='sp')
            tbs = [A.alloc(NT * 512, BF16) for _ in range(2)]
            ko = [A.alloc(512) for _ in range(2)]
            nch = (2 * NFp + 511) // 512
            for g_ in range(nch):
                tb, tbn = tbs[g_ % 2]
                tb3 = tb.rearrange("p (t c) -> p t c", c=512)
                DMA(tb, I['fw%d' % L][g_].rearrange("p t c -> p (t c)"), [], [tbn])
                glo = g_ * 512; ghi = min(glo + 512, 2 * NFp)
                segs = []
                if glo < NFp:
                    segs.append((0, glo, min(ghi, NFp)))
                if ghi > NFp:
                    segs.append((1, max(glo, NFp), ghi))
                for cc in range(2):
                    pt = ps[(g_ * 2 + cc) % 4]; pn = psn[(g_ * 2 + cc) % 4]
                    k_, kn_ = ko[cc]
                    for (ri, a_, b_) in segs:
                        lo = a_ - glo; hi = b_ - glo
                        for t in range(NT):
                            MM(pt[:, lo:hi], ksd4[:, ri, t, cc * 128:(cc + 1) * 128], tb3[:, t, lo:hi], t == 0, t == NT - 1, [ksdn, tbn], [pn])
                        CP('act', k_[:, lo:hi], pt[:, lo:hi], [pn], [kn_])
                        DMA(ksp_d[L][ri, cc * 128:(cc + 1) * 128, a_ - ri * NFp:b_ - ri * NFp], k_[:, lo:hi], [kn_], ['ksp%d' % L])
            P.barrier()
            A.release(m0)

        def run_seq(l, b, is_ctx):
            last = (l == DEPTH - 1)
            L = LC if is_ctx else S
            NT = L // 128; CH = min(512, L); NQ = L // CH
            v = 2 if is_ctx else b
            if is_ctx:
                src = I['ctx'][b] if l == 0 else xcs_d[b]
                dst = xcs_d[b]
            else:
                src = I['x'][b] if l == 0 else xs_d[b]
                dst = out[b] if last else xs_d[b]
            ctx_full = is_ctx and not last
            yc_d = ycat_d[L]
            mseq = A.mark()
            if dbg and dbg[0] == ('ycat', l, b, is_ctx):
                mz_ = A.mark()
                zt, ztn_ = A.alloc(L, BF16)
                MS('pool', zt, 0.0, [ztn_])
                for ch_ in range(8):
                    DMA(yc_d[ch_], zt, [ztn_], ['ycat_d'])
                A.release(mz_)
            hT, hTn = A.alloc(8 * L, BF16)
            hT3 = hT.rearrange("p (k t) -> p k t", k=8)
            wst = [A.alloc(1024) for _ in range(2)]

            def s1():
                m0 = A.mark()
                xb = [A.alloc(1024) for _ in range(2)]
                for t in range(NT):
                    xt, xn = xb[t % 2]
                    DMA(xt, src[t * 128:(t + 1) * 128, :], [], [xn])
                    for half in range(2):
                        bi = (2 * t + half) % 4
                        for kk_ in range(4):
                            k = half * 4 + kk_
                            TR(ps[bi][:, kk_ * 128:(kk_ + 1) * 128], xt[:, k * 128:(k + 1) * 128], 128, [xn], [psn[bi]])
                        for kk_ in range(4):
                            k = half * 4 + kk_
                            ACT(hT3[:, k, t * 128:(t + 1) * 128], ps[bi][:, kk_ * 128:(kk_ + 1) * 128], AF.Identity, [psn[bi], 'modT', 'modT1'], [hTn],
                                scale=mT14[:, l, 8 + k, v:v + 1], bias=mT4[:, l, k, v:v + 1])
                P.barrier()
                A.release(m0)
            s1()

            def conformer():
                m0 = A.mark()
                wc, wcn = A.alloc(8 * 512, BF16)
                wc3 = wc.rearrange("p (k c) -> p k c", k=8)
                load_w(wc3, wcn, I['w_in'][l][:, 0:512], 512, wst)
                cw3 = cw[:].rearrange("p (c k) -> p c k", c=2); cv33 = cv3[:].rearrange("p (c k) -> p c k", c=2)
                sg = [A.alloc(CH) for _ in range(2)]
                tm = [A.alloc(CH) for _ in range(4)]
                for cc in range(2):
                    m1 = A.mark()
                    up, upn = A.alloc(L + 30); acc, an = A.alloc(L); yo, yon = A.alloc(L, BF16)
                    MS('pool', up[:, 0:15], 0.0, [upn]); MS('pool', up[:, L + 15:L + 30], 0.0, [upn])
                    for q in range(NQ):
                        cs = slice(q * CH, (q + 1) * CH)
                        pa = ps[q % 2]; pg = ps[2 + q % 2]
                        for k in range(8):
                            MM(pa[:, 0:CH], wc3[:, k, cc * 128:(cc + 1) * 128], hT3[:, k, cs], k == 0, k == 7, [wcn, hTn], [psn[q % 2]])
                        for k in range(8):
                            MM(pg[:, 0:CH], wc3[:, k, 256 + cc * 128:256 + (cc + 1) * 128], hT3[:, k, cs], k == 0, k == 7, [wcn, hTn], [psn[2 + q % 2]])
                        s_, sn_ = sg[q % 2]
                        ACT(s_, pg[:, 0:CH], AF.Sigmoid, [psn[2 + q % 2]], [sn_])
                        TT('dve', up[:, 15 + q * CH:15 + (q + 1) * CH], pa[:, 0:CH], s_, ALU.mult, [psn[q % 2], sn_], [upn])
                    TS('dve', acc, up[:, 0:L], cw3[:, cc, 0:1], cv33[:, cc, 0:1], ALU.mult, ALU.add, [upn, 'cw', 'cv3'], [an])
                    for k in range(1, 31):
                        STT(acc, up[:, k:k + L], cw3[:, cc, k:k + 1], acc, ALU.mult, ALU.add, [upn, 'cw', an], [an])
                    for q in range(NQ):
                        cs = slice(q * CH, (q + 1) * CH)
                        pm = ps[4 + q % 2]; pv = ps[6 + q % 2]
                        cen, cn_ = tm[0]; sq_, sqn = tm[1]; sd_, sdn = tm[2]; un_, unn = tm[3]
                        MM(pm[:, 0:CH], blk[:], acc[:, cs], True, True, ['blk', an], [psn[4 + q % 2]])
                        TT('dve', cen, acc[:, cs], pm[:, 0:CH], ALU.subtract, [an, psn[4 + q % 2]], [cn_])
                        ACT(sq_, cen, AF.Square, [cn_], [sqn])
                        MM(pv[:, 0:CH], blk[:], sq_, True, True, ['blk', sqn], [psn[6 + q % 2]])
                        ACT(sd_, pv[:, 0:CH], AF.Sqrt, [psn[6 + q % 2]], [sdn], bias=EPS)
                        RCP(sd_, sd_, [sdn], [sdn])
                        TT('dve', un_, cen, sd_, ALU.mult, [cn_, sdn], [unn])
                        ACT(yo[:, cs], un_, AF.Silu, [unn, 'cv3'], [yon], scale=cv33[:, cc, 1:2], bias=cv33[:, cc, 2:3])
                    DMA(yc_d[cc], yo, [yon], ['ycat_d'])
                    A.release(m1)
                P.barrier()
                A.release(m0)

            def attention():
                m0 = A.mark()
                wa, wan = A.alloc(8 * 768, BF16)
                wa3 = wa.rearrange("p (k c) -> p k c", k=8)
                load_w(wa3, wan, I['w_in'][l][:, 512:1280], 768, wst)
                vc14 = vc1[:].rearrange("p (t h e) -> p t h e", t=2, h=4)
                kcT3 = kcT[:].rearrange("p (h t) -> p h t", h=4)
                if is_ctx:
                    V14 = vc14; Vn = 'vc1'
                else:
                    V1, Vn = A.alloc(NT * 4 * 65, BF16)
                    V14 = V1.rearrange("p (t h e) -> p t h e", t=NT, h=4)
                MS('pool', V14, 1.0, [Vn])
                for t in range(NT):
                    pv = ps[t % 2]
                    for k in range(8):
                        MM(pv[:, 0:256], hT3[:, k, t * 128:(t + 1) * 128], wa3[:, k, 512:768], k == 0, k == 7, [hTn, wan], [psn[t % 2]])
                    CP('act', V14[:, t, :, 0:64], pv[:, 0:256].rearrange("p (h e) -> p h e", h=4), [psn[t % 2]], [Vn])
                CK('attn_V')
                need_y = (not is_ctx) or ctx_full
                if not is_ctx:
                    rps = [A.alloc(2 * CH, parts=64) for _ in range(2)]
                    mk, mkn = A.alloc(9 * 896, BF16)
                    mk3 = mk.rearrange("p (i c) -> p i c", i=9)
                    DMA(mk3, I['namask'], [], [mkn])
                    PT, PTn = A.alloc(NT * 896, BF16)
                    def keytiles(j):
                        lo = min(max(2 * j - 4, 0), 24); hi = min(max(2 * j + 1 - 4, 0), 24) + 7
                        return list(range(lo // 2, hi // 2 + 1))
                    mcls = lambda i: i if i < 4 else (4 if i <= 11 else i - 7)
                    PT3 = PT.rearrange("p (i c) -> p i c", i=NT)
                    qrt, qrn = A.alloc(L, BF16, parts=64); krt, krn = A.alloc(L, BF16, parts=64)
                    t1s = [A.alloc(CH, parts=64) for _ in range(2)]; t2s = [A.alloc(CH, parts=64) for _ in range(2)]
                    et = [A.alloc(384) for _ in range(2)]; et2 = [A.alloc(384) for _ in range(2)]
                if need_y:
                    yb, ybn = A.alloc(NT * 256)
                    yb3 = yb.rearrange("p (t c) -> p t c", c=256)
                    qpl, qpn = A.alloc(L, BF16, parts=64)
                    PcT, PcTn = A.alloc(2 * L, BF16)
                    PcT3 = PcT.rearrange("p (c t) -> p c t", c=2)
                    rz = [A.alloc(1) for _ in range(2)]
                bK3 = biasK[:].rearrange("p (h c) -> p h c", h=4)
                for h in range(4):
                    for q in range(NQ):
                        cs = slice(q * CH, (q + 1) * CH)
                        if need_y:
                            for k in range(8):
                                MM(ps[2][0:64, 0:CH], wa3[:, k, h * 64:(h + 1) * 64], hT3[:, k, cs], k == 0, k == 7, [wan, hTn], [psn[2]])
                            CP('act', qpl[:, cs], ps[2][0:64, 0:CH], [psn[2]], [qpn])
                        if not is_ctx:
                            rp, rpn = rps[q % 2]
                            rp3 = rp.rearrange("p (a t) -> p a t", a=2)
                            DMA(rp3[:, 0, :], I['rope'][0][:, cs], [], [rpn]); DMA(rp3[:, 1, :], I['rope'][1][:, cs], [], [rpn])
                            if plan.get('fine'): CK('f_q')
                            for k in range(8):
                                MM(ps[3][0:32, 0:CH], wa3[:, k, h * 64 + 32:h * 64 + 64], hT3[:, k, cs], k == 0, k == 7, [wan, hTn], [psn[3]])
                            if plan.get('fine'): CK('f_sw1')
                            for k in range(8):
                                MM(ps[3][32:64, 0:CH], wa3[:, k, h * 64:h * 64 + 32], hT3[:, k, cs], k == 0, k == 7, [wan, hTn], [psn[3]])
                            if plan.get('fine'): CK('f_sw2')
                            t1, t1n = t1s[0]; t2, t2n = t2s[0]
                            TT('dve', t1, ps[2][0:64, 0:CH], rp3[:, 0, :], ALU.mult, [psn[2], rpn], [t1n])
                            if plan.get('fine'): CK('f_t1')
                            TT('dve', t2, ps[3][0:64, 0:CH], rp3[:, 1, :], ALU.mult, [psn[3], rpn], [t2n])
                            if plan.get('fine'): CK('f_t2')
                            TT('pool', qrt[:, cs], t1, t2, ALU.add, [t1n, t2n], [qrn])
                            if plan.get('fine'): CK('f_add')
                        for k in range(8):
                            MM(ps[4][0:64, 0:CH], wa3[:, k, 256 + h * 64:256 + (h + 1) * 64], hT3[:, k, cs], k == 0, k == 7, [wan, hTn], [psn[4]])
                        if is_ctx:
                            CP('act', kcT3[:, h, cs], ps[4][0:64, 0:CH], [psn[4]], ['kcT'])
                        else:
                            for k in range(8):
                                MM(ps[5][0:32, 0:CH], wa3[:, k, 256 + h * 64 + 32:256 + h * 64 + 64], hT3[:, k, cs], k == 0, k == 7, [wan, hTn], [psn[5]])
                            for k in range(8):
                                MM(ps[5][32:64, 0:CH], wa3[:, k, 256 + h * 64:256 + h * 64 + 32], hT3[:, k, cs], k == 0, k == 7, [wan, hTn], [psn[5]])
                            t1, t1n = t1s[1]; t2, t2n = t2s[1]
                            TT('dve', t1, ps[4][0:64, 0:CH], rp3[:, 0, :], ALU.mult, [psn[4], rpn], [t1n])
                            TT('dve', t2, ps[5][0:64, 0:CH], rp3[:, 1, :], ALU.mult, [psn[5], rpn], [t2n])
                            TT('pool', krt[:, cs], t1, t2, ALU.add, [t1n, t2n], [krn])
                    CK('attn_qk%d' % h)
                    if not need_y:
                        continue
                    if not is_ctx:
                        it = 0
                        for i in range(NT):
                            js = [j for j in range(NT) if i in keytiles(j)]
                            runs = [js[a_:a_ + 3] for a_ in range(0, len(js), 3)]
                            for run in runs:
                                ja, jb = run[0], run[-1]
                                n = (jb - ja + 1) * 128; c0 = (ja - i + 3) * 128
                                bi = 6 + it % 2
                                e1, e1n = et[it % 2]; e2, e2n = et2[it % 2]; it += 1
                                MM(ps[bi][:, 0:n], krt[:, i * 128:(i + 1) * 128], qrt[:, ja * 128:(jb + 1) * 128], True, True, [krn, qrn], [psn[bi]])
                                STT(e1[:, 0:n], ps[bi][:, 0:n], 0.125, bK3[:, h, c0:c0 + n], ALU.mult, ALU.add, [psn[bi], 'biasK'], [e1n])
                                ACT(e2[:, 0:n], e1[:, 0:n], AF.Exp, [e1n], [e2n])
                                TT('pool', PT3[:, i, c0:c0 + n], e2[:, 0:n], mk3[:, mcls(i), c0:c0 + n], ALU.mult, [e2n, mkn], [PTn])
                    for q in range(NQ):
                        cs = slice(q * CH, (q + 1) * CH)
                        for ct in range(2):
                            bi = (q * 2 + ct) % 2
                            MM(ps[bi][:, 0:CH], kcT3[:, h, ct * 128:(ct + 1) * 128], qpl[:, cs], True, True, ['kcT', qpn], [psn[bi]])
                            ACT(PcT3[:, ct, cs], ps[bi][:, 0:CH], AF.Exp, [psn[bi]], [PcTn], scale=0.125)
                    CK('attn_sc%d' % h)
                    for j in range(NT):
                        bi = 2 + j % 2
                        mms = []
                        if not is_ctx:
                            for i in keytiles(j):
                                mms.append((PT3[:, i, (j - i + 3) * 128:(j - i + 4) * 128], V14[:, i, h, :], [PTn, Vn]))
                        for ct in range(2):
                            mms.append((PcT3[:, ct, j * 128:(j + 1) * 128], vc14[:, ct, h, :], [PcTn, 'vc1']))
                        for ii, (lt, rh, rr) in enumerate(mms):
                            MM(ps[bi][:, 0:65], lt, rh, ii == 0, ii == len(mms) - 1, rr, [psn[bi]])
                        r_, rn_ = rz[j % 2]
                        RCP(r_, ps[bi][:, 64:65], [psn[bi]], [rn_])
                        ACT(yb3[:, j, h * 64:(h + 1) * 64], ps[bi][:, 0:64], AF.Identity, [psn[bi], rn_], [ybn], scale=r_)
                CK('attn_pv')
                if need_y:
                    yo, yon = A.alloc(2 * L, BF16)
                    yo3 = yo.rearrange("p (c t) -> p c t", c=2)
                    for j in range(NT):
                        for cc in range(2):
                            bi = 4 + (2 * j + cc) % 4
                            TR(ps[bi][:, 0:128], yb3[:, j, cc * 128:(cc + 1) * 128], 128, [ybn], [psn[bi]])
                            CP('dve' if cc else 'act', yo3[:, cc, j * 128:(j + 1) * 128], ps[bi][:, 0:128], [psn[bi]], [yon])
                    CK('attn_tr')
                    for cc in range(2):
                        DMA(yc_d[2 + cc], yo3[:, cc, :], [yon], ['ycat_d'])
                    CK('attn_out')
                P.barrier()
                A.release(m0)

            def hyena():
                NFp = L + 128; NFT = NFp // 128
                m0 = A.mark()
                wh, whn = A.alloc(8 * 768, BF16)
                wh3 = wh.rearrange("p (k c) -> p k c", k=8)
                load_w(wh3, whn, I['w_in'][l][:, 1280:2048], 768, wst)
                hsp3 = hsp[:].rearrange("p (c k) -> p c k", c=6)
                invv = I['inv%d' % L].rearrange("a f t -> (a f) t")
                TH = min(L, 1024); nbk = TH // CH
                for cc in range(2):
                    m1 = A.mark()
                    xc0 = A.alloc(L); xc2 = A.alloc(L)
                    ms_ = A.mark()
                    xc1 = A.alloc(L)
                    xc = [xc0, xc1, xc2]
                    pad, padn = A.alloc(L + 2)
                    MS('pool', pad[:, 0:1], 0.0, [padn]); MS('pool', pad[:, L + 1:L + 2], 0.0, [padn])
                    for g in range(3):
                        ci = g * 2 + cc
                        for q in range(NQ):
                            pp = ps[q % 2]
                            for k in range(8):
                                MM(pp[:, 0:CH], wh3[:, k, g * 256 + cc * 128:g * 256 + (cc + 1) * 128], hT3[:, k, q * CH:(q + 1) * CH], k == 0, k == 7, [whn, hTn], [psn[q % 2]])
                            CP('act', pad[:, 1 + q * CH:1 + (q + 1) * CH], pp[:, 0:CH], [psn[q % 2]], [padn])
                        x_, xn_ = xc[g]
                        TS('dve', x_, pad[:, 0:L], hsp3[:, ci, 0:1], hsp3[:, ci, 3:4], ALU.mult, ALU.add, [padn, 'hsp'], [xn_])
                        STT(x_, pad[:, 1:L + 1], hsp3[:, ci, 1:2], x_, ALU.mult, ALU.add, [padn, 'hsp', xn_], [xn_])
                        STT(x_, pad[:, 2:L + 2], hsp3[:, ci, 2:3], x_, ALU.mult, ALU.add, [padn, 'hsp', xn_], [xn_])
                    u, un = xc[2]
                    TT('dve', u, xc[2][0], xc[1][0], ALU.mult, [xc[2][1], xc[1][1]], [un])
                    A.release(ms_)
                    Utm, Utn = A.alloc(NT * 128, BF16)
                    Ut3 = Utm.rearrange("p (t c) -> p t c", c=128)
                    for t in range(NT):
                        bi = 2 + t % 2
                        TR(ps[bi][:, 0:128], u[:, t * 128:(t + 1) * 128], 128, [un], [psn[bi]])
                        CP('act' if t % 2 else 'dve', Ut3[:, t, :], ps[bi][:, 0:128], [psn[bi]], [Utn])
                    Uf, Ufn = A.alloc(2 * NFp)
                    m2 = A.mark()
                    tbs = [A.alloc(NT * 512, BF16) for _ in range(2)]
                    it = 0
                    for c0 in range(0, 2 * NFp, 512):
                        n = min(512, 2 * NFp - c0)
                        tb, tbn = tbs[it % 2]
                        tb3 = tb.rearrange("p (t c) -> p t c", c=512)
                        DMA(tb, I['fw%d' % L][c0 // 512].rearrange("p t c -> p (t c)"), [], [tbn])
                        bi = 4 + it % 2; it += 1
                        for t in range(NT):
                            MM(ps[bi][:, 0:n], Ut3[:, t, :], tb3[:, t, 0:n], t == 0, t == NT - 1, [Utn, tbn], [psn[bi]])
                        CP('act', Uf[:, c0:c0 + n], ps[bi][:, 0:n], [psn[bi]], [Ufn])
                    A.release(m2)
                    Kf, Kfn = A.alloc(2 * NFp)
                    DMA(Kf[:, 0:NFp], ksp_d[L][0, cc * 128:(cc + 1) * 128, :], ['ksp%d' % L], [Kfn])
                    DMA(Kf[:, NFp:2 * NFp], ksp_d[L][1, cc * 128:(cc + 1) * 128, :], ['ksp%d' % L], [Kfn])
                    Yf, Yfn = A.alloc(2 * NFp)
                    ta, tan = A.alloc(NFp); tb_, tbn_ = A.alloc(NFp)
                    Uc = Uf[:, 0:NFp]; Us = Uf[:, NFp:2 * NFp]; Kr = Kf[:, 0:NFp]; Ki = Kf[:, NFp:2 * NFp]
                    TT('dve', ta, Uc, Kr, ALU.mult, [Ufn, Kfn], [tan]); TT('pool', tb_, Us, Ki, ALU.mult, [Ufn, Kfn], [tbn_])
                    TT('dve', Yf[:, 0:NFp], ta, tb_, ALU.add, [tan, tbn_], [Yfn])
                    TT('dve', ta, Uc, Ki, ALU.mult, [Ufn, Kfn], [tan]); TT('pool', tb_, Us, Kr, ALU.mult, [Ufn, Kfn], [tbn_])
                    TT('dve', Yf[:, NFp:2 * NFp], ta, tb_, ALU.subtract, [tan, tbn_], [Yfn])
                    YT, YTn = A.alloc(2 * NFT * 128, BF16)
                    YT3 = YT.rearrange("p (f c) -> p f c", c=128)
                    for ft in range(2 * NFT):
                        bi = 2 + ft % 2
                        TR(ps[bi][:, 0:128], Yf[:, ft * 128:(ft + 1) * 128], 128, [Yfn], [psn[bi]])
                        CP('act' if ft % 2 else 'dve', YT3[:, ft, :], ps[bi][:, 0:128], [psn[bi]], [YTn])
                    ibs = [A.alloc(TH, BF16) for _ in range(4)]
                    yo, yon = A.alloc(L, BF16)
                    tq = [A.alloc(CH) for _ in range(2)]
                    for th in range(L // TH):
                        for ft in range(2 * NFT):
                            ib, ibn = ibs[ft % 4]
                            DMA(ib, invv[ft * 128:(ft + 1) * 128, th * TH:(th + 1) * TH], [], [ibn])
                            for bq in range(nbk):
                                MM(ps[4 + bq][:, 0:CH], YT3[:, ft, :], ib[:, bq * CH:(bq + 1) * CH], ft == 0, ft == 2 * NFT - 1, [YTn, ibn], [psn[4 + bq]])
                        for bq in range(nbk):
                            cs = slice(th * TH + bq * CH, th * TH + (bq + 1) * CH)
                            t_, tn_ = tq[bq % 2]
                            STT(t_, u[:, cs], hskip[:, cc:cc + 1], ps[4 + bq][:, 0:CH], ALU.mult, ALU.add, [un, 'hskip', psn[4 + bq]], [tn_])
                            TT('dve', yo[:, cs], t_, xc[0][0][:, cs], ALU.mult, [tn_, xc[0][1]], [yon])
                    DMA(yc_d[4 + cc], yo, [yon], ['ycat_d'])
                    P.barrier()
                    A.release(m1)
                A.release(m0)

            def ssd():
                want_y = (not is_ctx) or ctx_full
                m0 = A.mark()
                wz, wzn = A.alloc(8 * 264, BF16); wx, wxn = A.alloc(8 * 512, BF16)
                wz3 = wz.rearrange("p (k c) -> p k c", k=8); wx3 = wx.rearrange("p (k c) -> p k c", k=8)
                load_w(wz3, wzn, I['w_in'][l][:, 2048:2304], 256, wst)
                for k in range(8):
                    st, sn = wst[k % 2]
                    DMA(st[:, 0:8], I['w_in'][l][k * 128:(k + 1) * 128, 2816:2824], [], [sn])
                    CP('pool', wz3[:, k, 256:264], st[:, 0:8], [sn], [wzn])
                load_w(wx3, wxn, I['w_in'][l][:, 2304:2816], 512, wst)
                ztm, ztn = A.alloc(NT * 256); dta, dtn = A.alloc(NT * 8); aal, aan = A.alloc(NT * 8)
                z3 = ztm.rearrange("p (t c) -> p t c", c=256); dt3 = dta.rearrange("p (t c) -> p t c", c=8); a3 = aal.rearrange("p (t c) -> p t c", c=8)
                for t in range(NT):
                    pz = ps[t % 2]
                    for k in range(8):
                        MM(pz[:, 0:264], hT3[:, k, t * 128:(t + 1) * 128], wz3[:, k, :], k == 0, k == 7, [hTn, wzn], [psn[t % 2]])
                    CP('act', z3[:, t, :], pz[:, 0:256], [psn[t % 2]], [ztn])
                    TT('dve', dt3[:, t, :], pz[:, 256:264], dtb_bc[:], ALU.add, [psn[t % 2], 'dtb_bc'], [dtn])
                ACT(dta, dta, AF.Exp, [dtn], [dtn])
                ACT(dta, dta, AF.Ln, [dtn], [dtn], bias=1.0)
                TT('dve', a3, dt3, A_bc[:].unsqueeze(1).to_broadcast([128, NT, 8]), ALU.mult, [dtn, 'A_bc'], [aan])
                xtm, xtn = A.alloc(NT * 256); x3 = xtm.rearrange("p (t c) -> p t c", c=256)
                Btm, Btn = A.alloc(NT * 128, BF16); B3 = Btm.rearrange("p (t c) -> p t c", c=128)
                BTb, BTn = A.alloc(2 * L, BF16, parts=64); CTb, CTn = A.alloc(2 * L, BF16, parts=64)
                BT3 = BTb.rearrange("p (g t) -> p g t", g=2); CT3 = CTb.rearrange("p (g t) -> p g t", g=2)
                sxp3 = sxp[:].rearrange("p (c k) -> p c k", c=2); sbp3 = sbp[:].rearrange("p (c k) -> p c k", c=4)
                m1 = A.mark()
                pad, padn = A.alloc(L + 2); cv, cvn = A.alloc(L); sx, sxn = A.alloc(L)
                MS('pool', pad[:, 0:1], 0.0, [padn]); MS('pool', pad[:, L + 1:L + 2], 0.0, [padn])
                chunks = [('x', 0, 128, 0), ('x', 1, 128, 128), ('B', 0, 64, 256), ('B', 1, 64, 320), ('C', 0, 64, 384), ('C', 1, 64, 448)]
                for (kind, idx, Pn, col0) in chunks:
                    for q in range(NQ):
                        pp = ps[2 + q % 2]
                        for k in range(8):
                            MM(pp[0:Pn, 0:CH], wx3[:, k, col0:col0 + Pn], hT3[:, k, q * CH:(q + 1) * CH], k == 0, k == 7, [wxn, hTn], [psn[2 + q % 2]])
                        CP('act', pad[0:Pn, 1 + q * CH:1 + (q + 1) * CH], pp[0:Pn, 0:CH], [psn[2 + q % 2]], [padn])
                    if kind == 'x':
                        wv = sxp3[:, idx, :]; wname = 'sxp'
                    else:
                        wv = sbp3[:, (0 if kind == 'B' else 2) + idx, :]; wname = 'sbp'
                    TS('dve', cv[0:Pn, :], pad[0:Pn, 0:L], wv[0:Pn, 0:1], wv[0:Pn, 3:4], ALU.mult, ALU.add, [padn, wname], [cvn])
                    STT(cv[0:Pn, :], pad[0:Pn, 1:L + 1], wv[0:Pn, 1:2], cv[0:Pn, :], ALU.mult, ALU.add, [padn, wname, cvn], [cvn])
                    STT(cv[0:Pn, :], pad[0:Pn, 2:L + 2], wv[0:Pn, 2:3], cv[0:Pn, :], ALU.mult, ALU.add, [padn, wname, cvn], [cvn])
                    if kind == 'C':
                        ACT(CT3[:, idx, :], cv[0:64, :], AF.Silu, [cvn], [CTn])
                        continue
                    ACT(sx[0:Pn, :], cv[0:Pn, :], AF.Silu, [cvn], [sxn])
                    if kind == 'B':
                        CP('pool', BT3[:, idx, :], sx[0:64, :], [sxn], [BTn])
                    for t in range(NT):
                        bi = 4 + t % 2
                        TR(ps[bi][:, 0:Pn], sx[0:Pn, t * 128:(t + 1) * 128], Pn, [sxn], [psn[bi]])
                        if kind == 'x':
                            CP('act' if t % 2 else 'dve', x3[:, t, idx * 128:(idx + 1) * 128], ps[bi][:, 0:128], [psn[bi]], [xtn])
                        else:
                            CP('act' if t % 2 else 'dve', B3[:, t, idx * 64:(idx + 1) * 64], ps[bi][:, 0:64], [psn[bi]], [Btn])
                P.barrier()
                A.release(m1)
                xd, xdn = A.alloc(NT * 512, BF16)
                xd5 = xd.rearrange("p (t d h e) -> p t d h e", d=2, h=4, e=64)
                x4 = xtm.rearrange("p (t h e) -> p t h e", h=4, e=64)
                for d in range(2):
                    TT('dve', xd5[:, :, d], x4, dt3[:, :, d * 4:(d + 1) * 4].unsqueeze(3).to_broadcast([128, NT, 4, 64]), ALU.mult, [xtn, dtn], [xdn])
                if want_y:
                    yal, yaln = A.alloc(NT * 256)
                    y4 = yal.rearrange("p (t h e) -> p t h e", h=4, e=64)
                    for t in range(NT):
                        TT('pool', y4[:, t], x4[:, t], dsk_bc[:].unsqueeze(2).to_broadcast([128, 4, 64]), ALU.mult, [xtn, 'dsk_bc'], [yaln])
                S3 = Sst[:].rearrange("p (i e) -> p i e", i=8); SB3 = SstB[:].rearrange("p (i e) -> p i e", i=8)
                if is_ctx:
                    MS('dve', Sst[:], 0.0, ['Sst%d' % i for i in range(8)])
                    MS('dve', SstB[:], 0.0, ['SstB%d' % i for i in range(8)])
                cum = [A.alloc(24) for _ in range(2)]
                GTs = [A.alloc(128) for _ in range(4)]
                abcs = [A.alloc(128) for _ in range(4)]; Lts = [A.alloc(128) for _ in range(4)]
                MTs = [A.alloc(128, BF16) for _ in range(4)]
                yds = [A.alloc(64) for _ in range(4)]; yd2s = [A.alloc(64) for _ in range(4)]
                xdds = [A.alloc(64, BF16) for _ in range(4)]
                ssdc3_ = ssdc[:].rearrange("p (a b) -> p a b", a=4)
                it = 0
                for step_ in range(NT):
                    for d in range(2):
                        c = step_ if d == 0 else NT - 1 - step_
                        U = ssdc3_[:, d, :]; MSK = ssdc3_[:, 2 + d, :]
                        tok = slice(c * 128, (c + 1) * 128)
                        cm, cmn = cum[d]
                        MM(ps[0][:, 0:4], U, a3[:, c, d * 4:(d + 1) * 4], True, True, ['ssdc', aan], [psn[0]])
                        MM(ps[0][:, 8:12], ones[:], a3[:, c, d * 4:(d + 1) * 4], True, True, ['ones', aan], [psn[0]])
                        CP('dve', cm[:, 0:4], ps[0][:, 0:4], [psn[0]], [cmn])
                        CP('dve', cm[:, 4:8], ps[0][:, 8:12], [psn[0]], [cmn])
                        TS('dve', cm[:, 8:12], cm[:, 0:4], -1.0, None, ALU.mult, None, [cmn], [cmn])
                        TT('dve', cm[:, 16:20], cm[:, 4:8], cm[:, 0:4], ALU.subtract, [cmn], [cmn])
                        ACT(cm[:, 12:16], cm[:, 0:4], AF.Exp, [cmn], [cmn])
                        ACT(cm[:, 16:20], cm[:, 16:20], AF.Exp, [cmn], [cmn])
                        ACT(cm[:, 20:24], cm[:, 4:8], AF.Exp, [cmn], [cmn])
                        for g in range(2):
                            GT, GTn = GTs[d * 2 + g]
                            MM(ps[1][:, 0:128], BT3[:, g, tok], CT3[:, g, tok], True, True, [BTn, CTn], [psn[1]])
                            CP('act', GT, ps[1][:, 0:128], [psn[1]], [GTn])
                            for hh in range(2):
                                h = g * 2 + hh; si = d * 4 + h
                                abc, abcn = abcs[d * 2 + hh]; Lt, Ltn = Lts[d * 2 + hh]; MT, MTn = MTs[d * 2 + hh]
                                yd, ydn = yds[d * 2 + hh]; yd2, yd2n = yd2s[d * 2 + hh]; xdd, xddn = xdds[d * 2 + hh]
                                bR = 2 + hh; bY = 4 + hh; bS = 6 + hh
                                TS('pool', abc, ones[:], a3[:, c, si:si + 1], None, ALU.mult, None, ['ones', aan], [abcn])
                                MM(ps[bR][:, 0:128], abc, U, True, False, [abcn, 'ssdc'], [psn[bR]])
                                MM(ps[bR][:, 0:128], ident[:], MSK, False, True, ['ident', 'ssdc'], [psn[bR]])
                                if d == 0:
                                    ACT(Lt, ps[bR][:, 0:128], AF.Exp, [psn[bR], cmn], [Ltn], bias=cm[:, 8 + h:9 + h], scale=1.0)
                                else:
                                    ACT(Lt, ps[bR][:, 0:128], AF.Exp, [psn[bR], cmn], [Ltn], bias=cm[:, h:h + 1], scale=-1.0)
                                if want_y:
                                    TT('dve', MT, GT, Lt, ALU.mult, [GTn, Ltn], [MTn])
                                    MM(ps[bY][:, 0:64], MT, xd5[:, c, d, h, :], True, True, [MTn, xdn], [psn[bY]])
                                    MM(ps[bY][:, 64:128], CT3[:, g, tok], SB3[:, si, :], True, True, [CTn, 'SstB%d' % si], [psn[bY]])
                                    CP('act', yd, ps[bY][:, 0:64], [psn[bY]], [ydn])
                                    ecol = cm[:, 12 + h:13 + h] if d == 0 else cm[:, 16 + h:17 + h]
                                    STT(yd2, ps[bY][:, 64:128], ecol, yd, ALU.mult, ALU.add, [psn[bY], cmn, ydn], [yd2n])
                                    TT('pool', y4[:, c, h, :], y4[:, c, h, :], yd2, ALU.add, [yaln, yd2n], [yaln])
                                dcol = cm[:, 16 + h:17 + h] if d == 0 else cm[:, 12 + h:13 + h]
                                TS('pool', xdd, xd5[:, c, d, h, :], dcol, None, ALU.mult, None, [xdn, cmn], [xddn])
                                MM(ps[bS][0:64, 0:64], B3[:, c, g * 64:(g + 1) * 64], xdd, True, True, [Btn, xddn], [psn[bS]])
                                STT(S3[:, si, :], S3[:, si, :], cm[0:64, 20 + h:21 + h], ps[bS][0:64, 0:64], ALU.mult, ALU.add,
                                    ['Sst%d' % si, cmn, psn[bS]], ['Sst%d' % si])
                                CP('act', SB3[:, si, :], S3[:, si, :], ['Sst%d' % si], ['SstB%d' % si])
                        it += 1
                if dbg and dbg[0] == 'sst' and is_ctx:
                    DMA(dbg_out, Sst[:], ['Sst%d' % i for i in range(8)], ['dbg'], q='sp')
                if want_y:
                    szt, szn = A.alloc(NT * 256)
                    ACT(szt, ztm, AF.Silu, [ztn], [szn])
                    TT('dve', yal, yal, szt, ALU.mult, [yaln, szn], [yaln])
                    ACT(szt, yal, AF.Square, [yaln], [szn])
                    ssq, ssqn = A.alloc(NT * 2)
                    P.op('dve', lambda e: e.reduce_sum(out=ssq, in_=szt.rearrange("p (a c) -> p a c", c=128), axis=AX.X), r=[szn], w=[ssqn])
                    ACT(ssq, ssq, AF.Sqrt, [ssqn], [ssqn], bias=EPS, scale=1.0 / 128)
                    RCP(ssq, ssq, [ssqn], [ssqn])
                    TT('dve', yal.rearrange("p (a c) -> p a c", c=128), yal.rearrange("p (a c) -> p a c", c=128),
                       ssq.unsqueeze(2).to_broadcast([128, NT * 2, 128]), ALU.mult, [yaln, ssqn], [yaln])
                    y3 = yal.rearrange("p (t c) -> p t c", c=256)
                    TT('dve', y3, y3, ng_bc[:].unsqueeze(1).to_broadcast([128, NT, 256]), ALU.mult, [yaln, 'ng_bc'], [yaln])
                    yo, yon = A.alloc(2 * L, BF16)
                    yo3 = yo.rearrange("p (c t) -> p c t", c=2)
                    for t in range(NT):
                        for cc in range(2):
                            bi = 4 + (2 * t + cc) % 4
                            TR(ps[bi][:, 0:128], y3[:, t, cc * 128:(cc + 1) * 128], 128, [yaln], [psn[bi]])
                            CP('dve' if cc else 'act', yo3[:, cc, t * 128:(t + 1) * 128], ps[bi][:, 0:128], [psn[bi]], [yon])
                    for cc in range(2):
                        DMA(yc_d[6 + cc], yo3[:, cc, :], [yon], ['ycat_d'])
                P.barrier()
                A.release(m0)

            if 'conf' in mixers and (not is_ctx or ctx_full):
                conformer()
            if 'attn' in mixers:
                attention()
            if 'hy' in mixers and (not is_ctx or ctx_full):
                hyena()
            if 'ssd' in mixers:
                ssd()
            P.barrier()
            A.release(mseq)
            if dbg and dbg[0] == ('ycat', l, b, is_ctx):
                DMA(dbg_out, yc_d, ['ycat_d'], ['dbg'], q='sp')
                P.barrier()
            if do_tail and (not is_ctx or ctx_full):
                tail(l, b, is_ctx, src, dst, L, v)

        def tail(l, b, is_ctx, src, dst, L, v):
            NT = L // 128
            yc_d = ycat_d[L]
            m0 = A.mark()
            wo, won = A.alloc(8 * 1024, BF16); wq, wqn = A.alloc(8 * 2048, BF16)
            wo3 = wo.rearrange("p (k c) -> p k c", k=8); wq3 = wq.rearrange("p (k c) -> p k c", k=8)
            mw = A.mark()
            wst = [A.alloc(1024) for _ in range(2)]
            load_w(wo3, won, I['w_out'][l], 1024, wst)
            load_w(wq3[:, :, 0:1024], wqn, I['peer_wq'][l][:, 0:1024], 1024, wst)
            load_w(wq3[:, :, 1024:2048], wqn, I['peer_wq'][l][:, 1024:2048], 1024, wst)
            A.release(mw)
            rows = {}
            for nm, srcrow in (('ln1g', I['ln1_g'][l:l + 1, :]), ('ln1b', I['ln1_b'][l:l + 1, :]), ('ln2g', I['ln2_g'][l:l + 1, :]), ('ln2b', I['ln2_b'][l:l + 1, :]),
                               ('g1', mod_d[l, v:v + 1, 2 * D:3 * D]), ('sh2', mod_d[l, v:v + 1, 3 * D:4 * D]),
                               ('sc2', mod_d[l, v:v + 1, 4 * D:5 * D]), ('g2', mod_d[l, v:v + 1, 5 * D:6 * D])):
                t_, n_ = A.alloc(1024)
                DMA(t_, srcrow.partition_broadcast(128), ['mod_d'], [n_])
                rows[nm] = (t_, n_)
            TS('dve', rows['sc2'][0], rows['sc2'][0], 1.0, None, ALU.add, None, [rows['sc2'][1]], [rows['sc2'][1]])
            CK('t_rows')
            keysT, kTn = A.alloc(16 * 128)
            kT3 = keysT.rearrange("p (c n) -> p c n", c=16)
            mk_ = A.mark()
            kst = [A.alloc(128) for _ in range(2)]
            for c in range(16):
                st, sn = kst[c % 2]
                DMA(st, I['peer_keys'][l, c // 2, c % 2], [], [sn])
                TR(ps[c % 2][:, 0:128], st, 128, [sn], [psn[c % 2]])
                CP('dve', kT3[:, c, :], ps[c % 2][:, 0:128], [psn[c % 2]], [kTn])
            A.release(mk_)
            ycb = [A.alloc(8 * 128, BF16) for _ in range(2)]
            xb = [A.alloc(1024) for _ in range(2)]
            tt_, ttn = A.alloc(1024); r1, r1n = A.alloc(1024); x1, x1n = A.alloc(1024); h2, h2n = A.alloc(1024)
            junk, jn = tt_, ttn
            st8, st8n = A.alloc(8)
            h2T, h2Tn = A.alloc(8 * 128, BF16); h2T3 = h2T.rearrange("p (k t) -> p k t", k=8)
            qT, qTn = A.alloc(16 * 128); qT3 = qT.rearrange("p (c t) -> p c t", c=16)
            sc, scn = A.alloc(16 * 128); sc_b, sc_bn = A.alloc(16 * 128)
            wk, wkn = A.alloc(256)
            m16, m16n = A.alloc(256); i16, i16n = A.alloc(256, U32); i16f, i16fn = A.alloc(256)
            m163 = m16.rearrange("p (c k) -> p c k", c=16); i163 = i16.rearrange("p (c k) -> p c k", c=16)
            cand, candn = A.alloc(2048); eid, eidn = A.alloc(2560)
            cv, cvn = A.alloc(128); ex, exn = A.alloc(128); gz, gzn = A.alloc(16)
            ci_, cin = A.alloc(128, U32); cif, cifn = A.alloc(128)
            iot, iotn = A.alloc(256)
            DMA(iot, I['iota256'].partition_broadcast(128), [], [iotn])
            esel, eseln = A.alloc(128); eseli, eselin = A.alloc(128, I32)
            if not DENSE:
                av, avn = A.alloc(128); wv, wvn = A.alloc(128)
                gb = [A.alloc(1024) for _ in range(3)]
                acc, accn = A.alloc(1024)
            else:
                sm, smn = A.alloc(3 * 128); sm3 = sm.rearrange("p (a t) -> p a t", a=3)
                i1i, i1in = A.alloc(128, I32); i1f, i1fn = A.alloc(128); i2f, i2fn = A.alloc(128); etmp, etn = A.alloc(128)
                af_, afn = A.alloc(128); bf_, bfn = A.alloc(128)

            def layernorm(xin, xinn, gname, bname, xout, xoutn):
                TS('dve', st8[:, 1:2], st8[:, 0:1], -1.0 / D, None, ALU.mult, None, [st8n], [st8n])
                ACT(junk, xin, AF.Square, [xinn, st8n], [jn, st8n], bias=st8[:, 1:2], accum=st8[:, 2:3])
                ACT(st8[:, 3:4], st8[:, 2:3], AF.Sqrt, [st8n], [st8n], bias=EPS, scale=1.0 / D)
                RCP(st8[:, 4:5], st8[:, 3:4], [st8n], [st8n])
                TS('dve', xout, xin, st8[:, 1:2], st8[:, 4:5], ALU.add, ALU.mult, [xinn, st8n], [xoutn])
                TT('dve', xout, xout, rows[gname][0], ALU.mult, [xoutn, rows[gname][1]], [xoutn])
                TT('dve', xout, xout, rows[bname][0], ALU.add, [xoutn, rows[bname][1]], [xoutn])

            scs = [(sc, scn), (sc_b, sc_bn)]
            def front(t):
                sc3 = scs[t % 2][0].rearrange("p (c n) -> p c n", c=16); scn = scs[t % 2][1]
                tok = slice(t * 128, (t + 1) * 128)
                yc, ycn = ycb[t % 2]; yc3 = yc.rearrange("p (k t) -> p k t", k=8)
                xt, xn = xb[t % 2]
                DMA(yc3, yc_d[:, :, tok].rearrange("k p t -> p k t"), ['ycat_d'], [ycn])
                DMA(xt, src[tok, :], [], [xn])
                for nh in range(2):
                    for k in range(8):
                        MM(ps[nh][:, :], yc3[:, k, :], wo3[:, k, nh * 512:(nh + 1) * 512], k == 0, k == 7, [ycn, won], [psn[nh]])
                    TT('dve', tt_[:, nh * 512:(nh + 1) * 512], ps[nh][:, :], rows['g1'][0][:, nh * 512:(nh + 1) * 512], ALU.mult, [psn[nh], rows['g1'][1]], [ttn])
                STT(r1, xt, ALPHA, tt_, ALU.mult, ALU.add, [xn, ttn], [r1n, st8n], accum=st8[:, 0:1])
                layernorm(r1, r1n, 'ln1g', 'ln1b', x1, x1n)
                CK('t_ln1')
                TT('dve', h2, x1, rows['sc2'][0], ALU.mult, [x1n, rows['sc2'][1]], [h2n])
                TT('dve', h2, h2, rows['sh2'][0], ALU.add, [h2n, rows['sh2'][1]], [h2n])
                for half in range(2):
                    bi = 2 + half
                    for kk_ in range(4):
                        k = half * 4 + kk_
                        TR(ps[bi][:, kk_ * 128:(kk_ + 1) * 128], h2[:, k * 128:(k + 1) * 128], 128, [h2n], [psn[bi]])
                    CP('act', h2T3[:, half * 4:(half + 1) * 4, :], ps[bi][:, :].rearrange("p (k t) -> p k t", k=4), [psn[bi]], [h2Tn])
                CK('t_h2T')
                if DENSE:
                    DMA(x1_d[tok, :], x1, [x1n], ['x1_d'], q='sp')
                    DMA(h2T_d[:, :, tok].rearrange("k p t -> p k t"), h2T3, [h2Tn], ['h2T_d'], q='sp')
                for cq in range(4):
                    bi = 4 + cq % 2
                    for ci in range(4):
                        c = cq * 4 + ci
                        for k in range(8):
                            MM(ps[bi][:, ci * 128:(ci + 1) * 128], wq3[:, k, c * 128:(c + 1) * 128], h2T3[:, k, :], k == 0, k == 7, [wqn, h2Tn], [psn[bi]])
                    CP('act', qT3[:, cq * 4:(cq + 1) * 4, :], ps[bi][:, :].rearrange("p (c t) -> p c t", c=4), [psn[bi]], [qTn])
                CK('t_qT')
                for cq in range(4):
                    bi = 6 + cq % 2
                    for ci in range(4):
                        c = cq * 4 + ci
                        MM(ps[bi][:, ci * 128:(ci + 1) * 128], qT3[:, c, :], kT3[:, c, :], True, True, [qTn, kTn], [psn[bi]])
                    CP('act', sc3[:, cq * 4:(cq + 1) * 4, :], ps[bi][:, :].rearrange("p (c n) -> p c n", c=4), [psn[bi]], [scn])
                CK('t_sc')
            def back(t):
                tok = slice(t * 128, (t + 1) * 128)
                sc3 = scs[t % 2][0].rearrange("p (c n) -> p c n", c=16); scn = scs[t % 2][1]
                m16c = ['%s_%d' % (m16n, c) for c in range(16)]; i16c = ['%s_%d' % (i16n, c) for c in range(16)]
                wks = [(wk[:, 0:128], wkn + 'a'), (wk[:, 128:256], wkn + 'b')]
                for c0 in range(0, 16, 2):
                    pr = (c0, c0 + 1)
                    for c in pr:
                        P.op('dve', lambda e, c=c: e.max(out=m163[:, c, 0:8], in_=sc3[:, c, :]), r=[scn], w=[m16c[c]])
                    for c in pr:
                        P.op('dve', lambda e, c=c: e.max_index(out=i163[:, c, 0:8], in_max=m163[:, c, 0:8], in_values=sc3[:, c, :]), r=[scn, m16c[c]], w=[i16c[c]])
                    for j_, c in enumerate(pr):
                        P.op('dve', lambda e, c=c, j_=j_: e.match_replace(out=wks[j_][0], in_to_replace=m163[:, c, 0:8], in_values=sc3[:, c, :], imm_value=-1e30), r=[scn, m16c[c]], w=[wks[j_][1]])
                    for j_, c in enumerate(pr):
                        P.op('dve', lambda e, c=c, j_=j_: e.max(out=m163[:, c, 8:16], in_=wks[j_][0]), r=[wks[j_][1]], w=[m16c[c]])
                    for j_, c in enumerate(pr):
                        P.op('dve', lambda e, c=c, j_=j_: e.max_index(out=i163[:, c, 8:16], in_max=m163[:, c, 8:16], in_values=wks[j_][0]), r=[wks[j_][1], m16c[c]], w=[i16c[c]])
                CP('dve', i16f, i16, i16c, [i16fn])
                CK('t_top16')
                m4 = m16.rearrange("p (h s k) -> p h s k", h=8, s=2); if4 = i16f.rearrange("p (h s k) -> p h s k", h=8, s=2)
                cand4 = cand.rearrange("p (h a b) -> p h a b", h=8, a=16)
                TT('dve', cand4, m4[:, :, 0, :].unsqueeze(3).to_broadcast([128, 8, 16, 16]), m4[:, :, 1, :].unsqueeze(2).to_broadcast([128, 8, 16, 16]), ALU.add, m16c, [candn])
                cand3 = cand.rearrange("p (h c) -> p h c", h=8)
                cv3_ = cv.rearrange("p (h k) -> p h k", h=8); ex3 = ex.rearrange("p (h k) -> p h k", h=8)
                CK('t_cand')
                ci3 = ci_.rearrange("p (h k) -> p h k", h=8)
                cvh = ['%s_%d' % (cvn, h) for h in range(8)]; cih = ['%s_%d' % (cin, h) for h in range(8)]
                wk2 = [(eid[:, 0:256], eidn + 'a'), (eid[:, 256:512], eidn + 'b')]
                for h0 in range(0, 8, 2):
                    pr = (h0, h0 + 1)
                    for h in pr:
                        P.op('dve', lambda e, h=h: e.max(out=cv3_[:, h, 0:8], in_=cand3[:, h, :]), r=[candn], w=[cvh[h]])
                    for h in pr:
                        P.op('dve', lambda e, h=h: e.max_index(out=ci3[:, h, 0:8], in_max=cv3_[:, h, 0:8], in_values=cand3[:, h, :]), r=[candn, cvh[h]], w=[cih[h]])
                    for j_, h in enumerate(pr):
                        P.op('dve', lambda e, h=h, j_=j_: e.match_replace(out=wk2[j_][0], in_to_replace=cv3_[:, h, 0:8], in_values=cand3[:, h, :], imm_value=-1e30), r=[candn, cvh[h]], w=[wk2[j_][1]])
                    for j_, h in enumerate(pr):
                        P.op('dve', lambda e, h=h, j_=j_: e.max(out=cv3_[:, h, 8:16], in_=wk2[j_][0]), r=[wk2[j_][1]], w=[cvh[h]])
                    for j_, h in enumerate(pr):
                        P.op('dve', lambda e, h=h, j_=j_: e.max_index(out=ci3[:, h, 8:16], in_max=cv3_[:, h, 8:16], in_values=wk2[j_][0]), r=[wk2[j_][1], cvh[h]], w=[cih[h]])
                CP('dve', cif, ci_, cih, [cifn])
                TS('dve', gz[:, 0:8], cv3_[:, :, 0], -1.0, None, ALU.mult, None, cvh, [gzn])
                CK('t_cv')
                for h in range(8):
                    ACT(ex3[:, h, :], cv3_[:, h, :], AF.Exp, [cvh[h], gzn], [exn, gzn], bias=gz[:, h:h + 1], accum=gz[:, 8 + h:9 + h])
                RCP(gz[:, 8:16], gz[:, 8:16], [gzn], [gzn])
                TT('dve', ex3, ex3, gz[:, 8:16].unsqueeze(2).to_broadcast([128, 8, 16]), ALU.mult, [exn, gzn], [exn])
                CK('t_sm')
                CK('t_esel')
                if DENSE:
                    TS('dve', etmp, cif, 1.0 / 16, -0.47, ALU.mult, ALU.add, [cifn], [etn])
                    CP('dve', i1i, etmp, [etn], [i1in])
                    CP('dve', af_, i1i, [i1in], [afn])
                    STT(bf_, af_, -16.0, cif, ALU.mult, ALU.add, [afn, cifn], [bfn])
                    oh3 = eid[:, 512:2560].rearrange("p (s a) -> p s a", a=16)
                    oh4 = eid[:, 512:2560].rearrange("p (h k a) -> p h k a", h=8, k=16)
                    io16 = iot[:, 0:16].unsqueeze(1).to_broadcast([128, 128, 16])
                    for (sel_, seln_, side, dst_, dstn_) in ((af_, afn, 0, i1f, i1fn), (bf_, bfn, 1, i2f, i2fn)):
                        TT('dve', oh3, io16, sel_.unsqueeze(2).to_broadcast([128, 128, 16]), ALU.is_equal, [iotn, seln_], [eidn])
                        TT('dve', oh4, oh4, if4[:, :, side, :].unsqueeze(2).to_broadcast([128, 8, 16, 16]), ALU.mult, [eidn, i16fn], [eidn])
                        P.op('dve', lambda e, dst_=dst_: e.reduce_sum(out=dst_, in_=oh3, axis=AX.X), r=[eidn], w=[dstn_])
                    for a_, (src_, srcn_) in enumerate(((ex, exn), (i1f, i1fn), (i2f, i2fn))):
                        TR(ps[a_][:, 0:128], src_, 128, [srcn_], [psn[a_]])
                        CP('act', sm3[:, a_, :], ps[a_][:, 0:128], [psn[a_]], [smn])
                    DMA(sm_d[t], sm, [smn], ['sm_d'], q='sp')
                    return
                for j in range(128):
                    g_, gn_ = gb[j % 3]
                    P.op('pool', lambda e, j=j, g_=g_: e.indirect_dma_start(out=g_, out_offset=None, in_=I['peer_u%d' % l],
                         in_offset=bass.IndirectOffsetOnAxis(ap=eseli[:, j:j + 1], axis=0)), r=[eselin], w=[gn_], dma=True)
                    STT(junk, g_, 1.0, h2, ALU.mult, ALU.mult, [gn_, h2n], [jn, avn], accum=av[:, j:j + 1])
                ACT(wv, av, AF.Gelu_apprx_tanh, [avn], [wvn])
                CK('t_ug')
                TT('dve', wv, wv, ex, ALU.mult, [wvn, exn], [wvn])
                for j in range(128):
                    g_, gn_ = gb[j % 3]
                    P.op('pool', lambda e, j=j, g_=g_: e.indirect_dma_start(out=g_, out_offset=None, in_=I['peer_v%d' % l],
                         in_offset=bass.IndirectOffsetOnAxis(ap=eseli[:, j:j + 1], axis=0)), r=[eselin], w=[gn_], dma=True)
                    if j == 0:
                        TS('dve', acc, g_, wv[:, 0:1], None, ALU.mult, None, [gn_, wvn], [accn])
                    else:
                        STT(acc, g_, wv[:, j:j + 1], acc, ALU.mult, ALU.add, [gn_, wvn, accn], [accn])
                if dbg and dbg[0] == ('peer', l, b, is_ctx) :
                    DMA(dbg_out[tok, :], acc, [accn], ['dbg'], q='sp')
                TT('dve', acc, acc, rows['g2'][0], ALU.mult, [accn, rows['g2'][1]], [accn])
                STT(r1, x1, ALPHA, acc, ALU.mult, ALU.add, [x1n, accn], [r1n, st8n], accum=st8[:, 0:1])
                layernorm(r1, r1n, 'ln2g', 'ln2b', h2, h2n)
                CK('t_ln2')
                DMA(dst[tok, :], h2, [h2n], ['dst'], q='sp')
                CK('t_tile%d_%s' % (t, 'c' if is_ctx else 'l'))
            front(0)
            for t in range(NT):
                if t + 1 < NT:
                    front(t + 1)
                back(t)
            P.barrier()
            A.release(m0)
            if DENSE:
                tailB(l, b, is_ctx, dst, L, v)
            CK('tail_done')

        def phase_tables(l):
            m0 = A.mark()
            ub = [A.alloc(1024) for _ in range(2)]; vf = [A.alloc(1024) for _ in range(2)]
            ut = [A.alloc(1024, BF16) for _ in range(2)]; vb = [A.alloc(1024, BF16) for _ in range(2)]
            for i1 in range(128):
                rs_ = slice(i1 * 128, (i1 + 1) * 128)
                u_, un_ = ub[i1 % 2]; o_, on_ = ut[i1 % 2]
                o3 = o_.rearrange("p (k e) -> p k e", k=8)
                DMA(u_, I['peer_u%d' % l][rs_, :], [], [un_])
                for half in range(2):
                    bi = (2 * i1 + half) % 4
                    for kk_ in range(4):
                        k = half * 4 + kk_
                        TR(ps[bi][:, kk_ * 128:(kk_ + 1) * 128], u_[:, k * 128:(k + 1) * 128], 128, [un_], [psn[bi]])
                    CP('act' if half else 'dve', o3[:, half * 4:(half + 1) * 4, :], ps[bi][:, :].rearrange("p (k e) -> p k e", k=4), [psn[bi]], [on_])
                DMA(UT_d[i1], o_, [on_], ['UT_d'])
                v_, vn_ = vf[i1 % 2]; w_, wn_ = vb[i1 % 2]
                DMA(v_, I['peer_v%d' % l][rs_, :], [], [vn_])
                CP('pool', w_, v_, [vn_], [wn_])
                DMA(V_d[i1], w_, [wn_], ['V_d'])
            P.barrier()
            A.release(m0)

        def tailB(l, b, is_ctx, dst, L, v):
            m0 = A.mark()
            rows = {}
            for nm, srcrow in (('ln2g', I['ln2_g'][l:l + 1, :]), ('ln2b', I['ln2_b'][l:l + 1, :]), ('g2', mod_d[l, v:v + 1, 5 * D:6 * D])):
                t_, n_ = A.alloc(1024)
                DMA(t_, srcrow.partition_broadcast(128), ['mod_d'], [n_])
                rows[nm] = (t_, n_)
            iot, iotn = A.alloc(128)
            DMA(iot, I['iota256'][:, 0:128].partition_broadcast(128), [], [iotn])
            W_, Wn = A.alloc(128 * 256, BF16); W3 = W_.rearrange("p (i t) -> p i t", i=128)
            Wall = ['%s_%d' % (Wn, i_) for i_ in range(128)]
            h2s, h2sn = A.alloc(8 * 256, BF16); h2s3 = h2s.rearrange("p (k t) -> p k t", k=8)
            sms, smsn = A.alloc(2 * 384); sms4 = sms.rearrange("p (j a t) -> p j a t", j=2, a=3)
            Qb = [A.alloc(32 * 128, BF16) for _ in range(2)]; Rb = [A.alloc(32 * 128, BF16) for _ in range(2)]
            R1, R1n = A.alloc(32 * 128, BF16)
            NSB = 6
            utb = [A.alloc(1024, BF16) for _ in range(NSB)]; vtb = [A.alloc(1024, BF16) for _ in range(NSB)]
            gab = [A.alloc(256, BF16) for _ in range(2)]
            x1t, x1tn = A.alloc(1024); acc, accn = A.alloc(1024); r1, r1n = A.alloc(1024); xo, xon = A.alloc(1024)
            st8, st8n = A.alloc(8)
            iot3 = iot.unsqueeze(1).to_broadcast([128, 32, 128])
            for sp_ in range(L // 256):
                t0 = sp_ * 256
                DMA(h2s3, h2T_d[:, :, t0:t0 + 256].rearrange("k p t -> p k t"), ['h2T_d'], [h2sn])
                for jt in range(2):
                    DMA(sms[:, jt * 384:(jt + 1) * 384], sm_d[2 * sp_ + jt], ['sm_d'], [smsn])
                for sub in range(8):
                    jt = sub // 4; tr = slice((sub % 4) * 32, (sub % 4 + 1) * 32)
                    q_, qn_ = Qb[sub % 2]; r_, rn_ = Rb[sub % 2]
                    q3 = q_.rearrange("p (t i) -> p t i", t=32); r3 = r_.rearrange("p (t i) -> p t i", t=32); R13 = R1.rearrange("p (t i) -> p t i", t=32)
                    TT('dve', q3, iot3, sms4[:, jt, 2, tr].unsqueeze(2).to_broadcast([128, 32, 128]), ALU.is_equal, [iotn, smsn], [qn_])
                    TT('dve', R13, iot3, sms4[:, jt, 1, tr].unsqueeze(2).to_broadcast([128, 32, 128]), ALU.is_equal, [iotn, smsn], [R1n])
                    TT('pool', r3, R13, sms4[:, jt, 0, tr].unsqueeze(2).to_broadcast([128, 32, 128]), ALU.mult, [R1n, smsn], [rn_])
                    for j in range(32):
                        tg = sub * 32 + j
                        bi = (tg // 4) % 2
                        MM(ps[bi][:, (tg % 4) * 128:(tg % 4 + 1) * 128], q3[:, j, :], r3[:, j, :], True, True, [qn_, rn_], [psn[bi]])
                        if tg % 4 == 3:
                            CP('act' if (tg // 4) % 2 else 'dve', W3[:, :, tg - 3:tg + 1].rearrange("p i t -> p t i"),
                               ps[bi][:, :].rearrange("p (t i) -> p t i", t=4), [psn[bi]], Wall)
                def vmm(i1):
                    v_, vn_ = vtb[i1 % NSB]
                    for jt in range(2):
                        for dh in range(2):
                            bi = 4 + jt * 2 + dh
                            MM(ps[bi][:, :], W3[:, i1, jt * 128:(jt + 1) * 128], v_[:, dh * 512:(dh + 1) * 512], i1 == 0, i1 == 127, [Wall[i1], vn_], [psn[bi]])
                for i1 in range(128):
                    u_, un_ = utb[i1 % NSB]; u3 = u_.rearrange("p (k e) -> p k e", k=8)
                    DMA(u_, UT_d[i1], ['UT_d'], [un_], q='sp')
                    v_, vn_ = vtb[i1 % NSB]
                    DMA(v_, V_d[i1], ['V_d'], [vn_], q='act')
                    bi = 2 + i1 % 2
                    for k in range(8):
                        MM(ps[bi][:, 0:256], u3[:, k, :], h2s3[:, k, :], k == 0, k == 7, [un_, h2sn], [psn[bi]])
                    g_, gn_ = gab[i1 % 2]
                    ACT(g_, ps[bi][:, 0:256], AF.Gelu_apprx_tanh, [psn[bi]], [gn_])
                    TT('pool' if i1 % 2 else 'dve', W3[:, i1, :], W3[:, i1, :], g_, ALU.mult, [Wall[i1], gn_], [Wall[i1]])
                    if i1 >= 1:
                        vmm(i1 - 1)
                vmm(127)
                for jt in range(2):
                    tok = slice(t0 + jt * 128, t0 + (jt + 1) * 128)
                    DMA(x1t, x1_d[tok, :], ['x1_d'], [x1tn])
                    for dh in range(2):
                        bi = 4 + jt * 2 + dh
                        TT('dve', acc[:, dh * 512:(dh + 1) * 512], ps[bi][:, :], rows['g2'][0][:, dh * 512:(dh + 1) * 512], ALU.mult, [psn[bi], rows['g2'][1]], [accn])
                    if dbg and dbg[0] == ('peer', l, b, is_ctx):
                        pass
                    STT(r1, x1t, ALPHA, acc, ALU.mult, ALU.add, [x1tn, accn], [r1n, st8n], accum=st8[:, 0:1])
                    TS('dve', st8[:, 1:2], st8[:, 0:1], -1.0 / D, None, ALU.mult, None, [st8n], [st8n])
                    ACT(acc, r1, AF.Square, [r1n, st8n], [accn, st8n], bias=st8[:, 1:2], accum=st8[:, 2:3])
                    ACT(st8[:, 3:4], st8[:, 2:3], AF.Sqrt, [st8n], [st8n], bias=EPS, scale=1.0 / D)
                    RCP(st8[:, 4:5], st8[:, 3:4], [st8n], [st8n])
                    TS('dve', xo, r1, st8[:, 1:2], st8[:, 4:5], ALU.add, ALU.mult, [r1n, st8n], [xon])
                    TT('dve', xo, xo, rows['ln2g'][0], ALU.mult, [xon, rows['ln2g'][1]], [xon])
                    TT('dve', xo, xo, rows['ln2b'][0], ALU.add, [xon, rows['ln2b'][1]], [xon])
                    DMA(dst[tok, :], xo, [xon], ['dst'], q='sp')
            P.barrier()
            A.release(m0)

        try:
            for l in layers:
                phase_params(l)
                CK('params')
                if DENSE and do_tail:
                    phase_tables(l)
                if 'hy' in mixers:
                    phase_filters(l, S)
                    CK('filtS')
                    if l < DEPTH - 1:
                        phase_filters(l, LC)
                        CK('filtC')
                for b in batches:
                    run_seq(l, b, True)
                    run_seq(l, b, False)
        except _Stop:
            pass

        if dbg and dbg[0] == 'xs':
            DMA(dbg_out, xs_d[dbg[3]], ['x'], ['dbg'], q='sp')
        if dbg and dbg[0] == 'xcs':
            DMA(dbg_out, xcs_d[dbg[3]], ['x'], ['dbg'], q='sp')
        if dbg and dbg[0] == 'modT':
            DMA(dbg_out, modT[:], ['modT'], ['dbg'], q='sp')
        if dbg and dbg[0] == 'biasK':
            DMA(dbg_out, biasK[:], ['biasK'], ['dbg'], q='sp')
        if dbg and dbg[0] in ('ksp2048', 'ksp256'):
            DMA(dbg_out, ksp_d[int(dbg[0][3:])], ['x'], ['dbg'], q='sp')
        P.barrier()
        with nc.Block() as block:
            P.emit(sems, block)
    nc._prog_nops = P.nops
    return nc


_CACHE = {}


def kernel(**inputs):
    if 'nc' not in _CACHE:
        _CACHE['nc'] = build()
        _CACHE['consts'] = consts()
    nc = _CACHE['nc']; C = _CACHE['consts']
    f32 = lambda a: np.ascontiguousarray(np.asarray(a, dtype=np.float32))
    shared = {k: f32(inputs[k]) for k in WSHAPES if not k.startswith('peer_u') and not k.startswith('peer_v')}
    for l in range(DEPTH):
        shared['peer_u%d' % l] = f32(inputs['peer_u'][l]); shared['peer_v%d' % l] = f32(inputs['peer_v'][l])
    shared.update(C)
    x = f32(inputs['x']); ctx = f32(inputs['ctx']); c = f32(inputs['c']); cc = f32(inputs['c_ctx'])
    in_maps = []
    for i in range(8):
        m = dict(shared)
        m['x'] = x[2 * i:2 * i + 2]; m['ctx'] = ctx[2 * i:2 * i + 2]
        m['cvec'] = np.ascontiguousarray(np.stack([c[2 * i], c[2 * i + 1], cc]))
        in_maps.append(m)
    res = run_bass_kernel_spmd(nc, in_maps, core_ids=list(range(8)))
    return np.concatenate([np.asarray(r['out']) for r in res.results], axis=0).astype(np.float32)
```
